# Optimizing a Trainium2 kernel written in Bass

```python
import jax, jax.numpy as jnp
from jax import lax
import numpy as np

D_MODEL = 4096
BATCH = 1
SEQ = 8192
DEPTH = 1

N_MEM = 256
MEM_HEADS = 4
MEM_WIDTH = D_MODEL // 4
MEM_HEAD_DIM = MEM_WIDTH // MEM_HEADS
SB_HEAD_DIM = 128
SB_WIDTH = 3 * D_MODEL // 8
SB_HEADS = SB_WIDTH // SB_HEAD_DIM
SB_BLOCK = 128
RW_HEAD_DIM = 64
RW_WIDTH = 3 * D_MODEL // 8
RW_HEADS = RW_WIDTH // RW_HEAD_DIM
RW_DECAY_LORA = 128
RW_ICLR_LORA = 128
RW_GATE_LORA = 480
RW_SEG = 3 * RW_WIDTH + RW_DECAY_LORA + RW_ICLR_LORA + RW_GATE_LORA
RW_SPLITS = [RW_WIDTH, 2 * RW_WIDTH, 3 * RW_WIDTH,
             3 * RW_WIDTH + RW_DECAY_LORA,
             3 * RW_WIDTH + RW_DECAY_LORA + RW_ICLR_LORA]
RW_GN_EPS = 64e-5
N_BRANCHES = 3
IN_SPLITS = [SB_WIDTH, 2 * SB_WIDTH, 3 * SB_WIDTH,
             3 * SB_WIDTH + RW_SEG,
             3 * SB_WIDTH + RW_SEG + MEM_WIDTH]
IN_COLS = 3 * SB_WIDTH + RW_SEG + MEM_WIDTH + N_BRANCHES * D_MODEL
D_FF = -(-8 * D_MODEL // (3 * 256)) * 256
RMS_EPS = 1e-6

kernel_name = "hybrid_stickbreak_rwkv7_memxattn_gated"


def rms_norm(x, g, eps=RMS_EPS):
    xf = x.astype(jnp.float32)
    y = xf * lax.rsqrt(jnp.mean(xf * xf, axis=-1, keepdims=True) + eps)
    return (y * g.astype(jnp.float32)).astype(x.dtype)


def token_shift(c, mix):
    prev = jnp.pad(c, ((0, 0), (1, 0), (0, 0)))[:, :-1]
    return c + (prev - c) * mix


def stick_breaking_attention(q, k, v):
    B, T, H, hd = q.shape
    nb = T // SB_BLOCK
    scale = hd ** -0.5
    qb = jnp.moveaxis(q.reshape(B, nb, SB_BLOCK, H, hd), 1, 0)
    kf = k.astype(jnp.float32)
    vf = v.astype(jnp.float32)
    key_pos = jnp.arange(T)

    def block(args):
        q_blk, i = args
        z = jnp.einsum('bqhd,bkhd->bhqk', q_blk.astype(jnp.float32), kf) * scale
        q_pos = i * SB_BLOCK + jnp.arange(SB_BLOCK)
        causal = key_pos[None, :] < q_pos[:, None]
        sp = jnp.where(causal, jax.nn.softplus(z), 0.0)
        tail = lax.cumsum(sp, axis=3, reverse=True)
        weights = jnp.exp(jnp.where(causal, z - tail, -jnp.inf))
        return jnp.einsum('bhqk,bkhd->bqhd', weights, vf)

    ob = lax.map(block, (qb, jnp.arange(nb)))
    return jnp.moveaxis(ob, 0, 1).reshape(B, T, H, hd).astype(q.dtype)


def rwkv7_time_mix(seg, mix, w0, w_up, a0, a_up, g_up, k_k, k_a, r_k, ln_g, ln_b):
    dt = seg.dtype
    seg = token_shift(seg, mix).astype(jnp.float32)
    r, k, v, wd, ad, gd = jnp.split(seg, RW_SPLITS, axis=-1)
    B, T, _ = r.shape
    f32 = lambda t: t.astype(jnp.float32)
    w_log = -jax.nn.softplus(-(f32(w0) + jnp.tanh(wd) @ f32(w_up))) - 0.5
    decay = jnp.exp(-jnp.exp(w_log))
    a = jax.nn.sigmoid(f32(a0) + ad @ f32(a_up))
    g = jax.nn.sigmoid(gd) @ f32(g_up)
    heads = lambda t: t.reshape(B, T, RW_HEADS, RW_HEAD_DIM)
    kk = heads(k * f32(k_k))
    kk = kk / jnp.maximum(jnp.sqrt(jnp.sum(kk * kk, axis=-1, keepdims=True)), 1e-12)
    k = k * (1.0 + (a - 1.0) * f32(k_a))
    r, k, v, a, decay = heads(r), heads(k), heads(v), heads(a), heads(decay)

    def step(S, inp):
        r_t, w_t, k_t, v_t, kk_t, a_t = inp
        sa = jnp.einsum('bhvk,bhk->bhv', S, -kk_t)
        S = (S * w_t[:, :, None, :]
             + jnp.einsum('bhv,bhk->bhvk', sa, kk_t * a_t)
             + jnp.einsum('bhv,bhk->bhvk', v_t, k_t))
        y_t = jnp.einsum('bhvk,bhk->bhv', S, r_t)
        return S, y_t

    tm = lambda t: jnp.moveaxis(t, 1, 0)
    S0 = jnp.zeros((B, RW_HEADS, RW_HEAD_DIM, RW_HEAD_DIM), jnp.float32)
    _, y = lax.scan(step, S0, (tm(r), tm(decay), tm(k), tm(v), tm(kk), tm(a)))
    y = jnp.moveaxis(y, 0, 1)
    mu = jnp.mean(y, axis=-1, keepdims=True)
    var = jnp.mean(jnp.square(y - mu), axis=-1, keepdims=True)
    y = ((y - mu) * lax.rsqrt(var + RW_GN_EPS)).reshape(B, T, RW_WIDTH) * f32(ln_g) + f32(ln_b)
    bonus = jnp.sum(r * k * f32(r_k), axis=-1, keepdims=True) * v
    y = y + bonus.reshape(B, T, RW_WIDTH)
    return (y * g).astype(dt)


def memory_cross_attention(q, mem_n, w_kv, q_g, k_g):
    B, T, _ = q.shape
    M = mem_n.shape[1]
    mk, mv = jnp.split(mem_n @ w_kv, 2, axis=-1)
    q = rms_norm(q.reshape(B, T, MEM_HEADS, MEM_HEAD_DIM), q_g)
    mk = rms_norm(mk.reshape(B, M, MEM_HEADS, MEM_HEAD_DIM), k_g)
    mv = mv.reshape(B, M, MEM_HEADS, MEM_HEAD_DIM)
    s = jnp.einsum('bthd,bmhd->bhtm', q.astype(jnp.float32), mk.astype(jnp.float32)) * MEM_HEAD_DIM ** -0.5
    p = jax.nn.softmax(s, axis=-1)
    o = jnp.einsum('bhtm,bmhd->bthd', p, mv.astype(jnp.float32))
    return o.reshape(B, T, MEM_WIDTH).astype(q.dtype)


def setup_inputs(seed: int = 0) -> dict:
    key = jax.random.key(seed)
    ks = iter(jax.random.split(key, 32))
    L = DEPTH
    f = jnp.float32

    def nrm(shape, fan_in):
        return jax.random.normal(next(ks), shape, f) * fan_in ** -0.5

    def gain(shape):
        return 1.0 + 0.02 * jax.random.normal(next(ks), shape, f)

    def small(shape, s):
        return s * jax.random.normal(next(ks), shape, f)

    return {
        "x": jax.random.normal(next(ks), (BATCH, SEQ, D_MODEL), f),
        "mem": jax.random.normal(next(ks), (BATCH, N_MEM, D_MODEL), f),
        "attn_norm_g": gain((L, D_MODEL)),
        "mem_norm_g": gain((L, D_MODEL)),
        "w_in": nrm((L, D_MODEL, IN_COLS), D_MODEL),
        "sb_q_norm_g": gain((L, SB_HEAD_DIM)),
        "sb_k_norm_g": gain((L, SB_HEAD_DIM)),
        "rw_mix": jax.random.uniform(next(ks), (L, RW_SEG), f),
        "rw_w0": jax.random.uniform(next(ks), (L, RW_WIDTH), f, minval=-6.0, maxval=-1.0),
        "rw_w_up": nrm((L, RW_DECAY_LORA, RW_WIDTH), RW_DECAY_LORA),
        "rw_a0": small((L, RW_WIDTH), 0.1),
        "rw_a_up": nrm((L, RW_ICLR_LORA, RW_WIDTH), RW_ICLR_LORA),
        "rw_g_up": nrm((L, RW_GATE_LORA, RW_WIDTH), RW_GATE_LORA),
        "rw_k_k": 0.85 + small((L, RW_WIDTH), 0.02),
        "rw_k_a": gain((L, RW_WIDTH)),
        "rw_r_k": small((L, RW_HEADS, RW_HEAD_DIM), 0.1),
        "rw_ln_g": gain((L, RW_WIDTH)),
        "rw_ln_b": small((L, RW_WIDTH), 0.02),
        "mem_w_kv": nrm((L, D_MODEL, 2 * MEM_WIDTH), D_MODEL),
        "mem_q_norm_g": gain((L, MEM_HEAD_DIM)),
        "mem_k_norm_g": gain((L, MEM_HEAD_DIM)),
        "w_sb_o": nrm((L, SB_WIDTH, D_MODEL), SB_WIDTH),
        "w_rw_o": nrm((L, RW_WIDTH, D_MODEL), RW_WIDTH),
        "w_mem_o": nrm((L, MEM_WIDTH, D_MODEL), MEM_WIDTH),
        "w_out": nrm((L, D_MODEL, D_MODEL), D_MODEL),
        "ffn_norm_g": gain((L, D_MODEL)),
        "w_gate": nrm((L, D_MODEL, D_FF), D_MODEL),
        "w_up": nrm((L, D_MODEL, D_FF), D_MODEL),
        "w_down": nrm((L, D_FF, D_MODEL), D_FF),
    }


def reference(x, mem, attn_norm_g, mem_norm_g, w_in, sb_q_norm_g, sb_k_norm_g,
              rw_mix, rw_w0, rw_w_up, rw_a0, rw_a_up, rw_g_up, rw_k_k, rw_k_a,
              rw_r_k, rw_ln_g, rw_ln_b, mem_w_kv, mem_q_norm_g, mem_k_norm_g,
              w_sb_o, w_rw_o, w_mem_o, w_out, ffn_norm_g, w_gate, w_up, w_down):
    B, T, _ = x.shape
    h = x
    for l in range(DEPTH):
        xn = rms_norm(h, attn_norm_g[l])
        p = xn @ w_in[l]
        sb_q, sb_k, sb_v, rw_seg, mem_q, gates = jnp.split(p, IN_SPLITS, axis=-1)

        sb_heads = lambda t: t.reshape(B, T, SB_HEADS, SB_HEAD_DIM)
        q = rms_norm(sb_heads(sb_q), sb_q_norm_g[l])
        k = rms_norm(sb_heads(sb_k), sb_k_norm_g[l])
        o_sb = stick_breaking_attention(q, k, sb_heads(sb_v)).reshape(B, T, SB_WIDTH)
        u_sb = o_sb @ w_sb_o[l]

        o_rw = rwkv7_time_mix(rw_seg, rw_mix[l], rw_w0[l], rw_w_up[l], rw_a0[l], rw_a_up[l],
                              rw_g_up[l], rw_k_k[l], rw_k_a[l], rw_r_k[l], rw_ln_g[l], rw_ln_b[l])
        u_rw = o_rw @ w_rw_o[l]

        mem_n = rms_norm(mem, mem_norm_g[l])
        o_mem = memory_cross_attention(mem_q, mem_n, mem_w_kv[l], mem_q_norm_g[l], mem_k_norm_g[l])
        u_mem = o_mem @ w_mem_o[l]

        g_sb, g_rw, g_mem = jnp.split(jax.nn.sigmoid(gates), N_BRANCHES, axis=-1)
        merged = g_sb * u_sb + g_rw * u_rw + g_mem * u_mem
        h = h + merged @ w_out[l]

        hn = rms_norm(h, ffn_norm_g[l])
        h = h + (jax.nn.silu(hn @ w_gate[l]) * (hn @ w_up[l])) @ w_down[l]
    return h
```

```python
import os
import numpy as np
import ml_dtypes
import concourse.bass as bass
import concourse.mybir as mybir
from concourse.bass_utils import run_bass_kernel_spmd

F32 = mybir.dt.float32
BF16 = mybir.dt.bfloat16
AF = mybir.ActivationFunctionType
ALU = mybir.AluOpType
AX = mybir.AxisListType
NPBF = ml_dtypes.bfloat16

PE, DVE, ACT, POOL, SP = "tensor", "vector", "scalar", "gpsimd", "sync"
ENGS = [PE, DVE, ACT, POOL, SP]

NCORES = 8
D = 4096
T = 8192
TL = T // NCORES
KC = D // 128
SBW, RWW, MEMW = 1536, 1536, 1024
RW_SEG = 5344
IN_COLS = 23264
DFF = 11008
NMEM = 256
EPS = 1e-6
GN_EPS = 64e-5


class Res:
    __slots__ = ("name", "last_w", "reads", "dsem", "dcnt", "ws")

    def __init__(self, name):
        self.name = name
        self.ws = []
        self.last_w = None
        self.reads = []
        self.dsem = None
        self.dcnt = 0


class Prog:
    def __init__(self, nc):
        self.nc = nc
        self.q = {e: [] for e in ENGS}
        self.seq = {e: 0 for e in ENGS}
        self.waited = {e: {} for e in ENGS}
        self.esem = {}
        self.nsem = 0
        self.dma_owners = []

    def _newsem(self, name):
        self.nsem += 1
        return self.nc.alloc_semaphore(f"s{self.nsem}_{name}")

    def _esem(self, eng):
        if eng not in self.esem:
            self.esem[eng] = self._newsem("e_" + eng)
        return self.esem[eng]

    def _deps(self, eng, r, w, dma_sem=None):
        deps = []
        for b in r:
            if b.last_w is not None:
                deps.append(b.last_w)
            deps.extend(b.ws)
        for b in w:
            if b.last_w is not None:
                if not (dma_sem is not None and b.last_w[0] is dma_sem):
                    deps.append(b.last_w)
            deps.extend(b.reads)
        wd = self.waited[eng]
        best = {}
        for (sem, val, src) in deps:
            if src == eng and eng == PE:
                continue
            k = id(sem)
            if wd.get(k, 0) >= val:
                continue
            if k not in best or best[k][1] < val:
                best[k] = (sem, val)
        for k, (sem, val) in best.items():
            wd[k] = val
        return list(best.values())

    def op(self, eng, fn, r=(), w=()):
        waits = self._deps(eng, r, w)
        sem = self._esem(eng)
        self.seq[eng] += 1
        ev = (sem, self.seq[eng], eng)
        for b in r:
            b.reads.append(ev)
        for b in w:
            b.last_w = ev
            b.reads = []
        self.q[eng].append((waits, fn, (sem, 1)))
        return ev

    def dma(self, eng, out, in_, r=(), w=(), own="w", **kw):
        owner = w[0] if own == "w" else r[0]
        if owner.dsem is None:
            owner.dsem = self._newsem("d_" + owner.name)
            self.dma_owners.append(owner)
        sem = owner.dsem
        waits = self._deps(eng, r, w, dma_sem=sem)
        owner.dcnt += 16
        ev = (sem, owner.dcnt, "dma")
        for b in r:
            b.reads.append(ev)
        for b in w:
            if own == "r":
                b.ws = [x for x in b.ws if x[0] is not sem] + [ev]
            else:
                b.last_w = ev
                b.reads = []
        self.q[eng].append((waits, lambda e: e.dma_start(out=out, in_=in_, **kw), (sem, 16)))
        return ev

    def barrier(self):
        targets = [(sem, self.seq[e]) for e, sem in self.esem.items() if self.seq[e] > 0]
        targets += [(o.dsem, o.dcnt) for o in self.dma_owners if o.dcnt > 0]
        for eng in ENGS:
            wd = self.waited[eng]
            waits = []
            for sem, val in targets:
                if wd.get(id(sem), 0) >= val:
                    continue
                wd[id(sem)] = val
                waits.append((sem, val))
            if waits:
                self.q[eng].append((waits, None, None))

    def allgather(self, in_ap, out_ap, r, w):
        sem = self._newsem("cc")
        waits = self._deps(POOL, r, w)
        ev = (sem, 1, "cc")
        for b in r:
            b.reads.append(ev)
        for b in w:
            b.last_w = ev
            b.reads = []
        self.q[POOL].append((waits, lambda e: e.collective_compute(
            "AllGather", ALU.bypass, replica_groups=[list(range(NCORES))], ins=[in_ap.opt()], outs=[out_ap.opt()]), (sem, 1)))
        self.q[POOL].append(([(sem, 1)], None, None))
        self.waited[POOL][id(sem)] = 1
        return ev

    def wait_all(self, eng, ress):
        deps = []
        for b in ress:
            if b.last_w is not None:
                deps.append((b.last_w[0], b.last_w[1]))
            deps.extend((x[0], x[1]) for x in b.ws)
        self.q[eng].append((deps, None, None))

    def replay(self, block):
        for eng in ENGS:
            items = self.q[eng]
            if not items:
                continue

            def body(e, items=items):
                for waits, fn, inc in items:
                    for sem, val in waits:
                        e.wait_ge(sem, val)
                    if fn is not None:
                        fn(e).then_inc(inc[0], inc[1])

            getattr(block, eng)(body)

    def mm(self, out, lhsT, rhs, start, stop, r, w):
        return self.op(PE, lambda e: e.matmul(out, lhsT=lhsT, rhs=rhs, start=start, stop=stop), r=r, w=w)

    def act(self, out, in_, func, r, w, bias=None, scale=None, eng=ACT):
        kw = {}
        if bias is not None:
            kw["bias"] = bias
        if scale is not None:
            kw["scale"] = scale
        return self.op(eng, lambda e: e.activation(out, in_, func, **kw), r=r, w=w)

    def tt(self, eng, out, a, b, op, r, w):
        return self.op(eng, lambda e: e.tensor_tensor(out, a, b, op), r=r, w=w)

    def ts(self, eng, out, a, s1, s2, op0, op1, r, w):
        if op1 is None:
            return self.op(eng, lambda e: e.tensor_scalar(out, a, s1, None, op0), r=r, w=w)
        return self.op(eng, lambda e: e.tensor_scalar(out, a, s1, s2, op0, op1), r=r, w=w)

    def stt(self, eng, out, a, s, b, op0, op1, r, w):
        return self.op(eng, lambda e: e.scalar_tensor_tensor(out, a, s, b, op0, op1), r=r, w=w)

    def cp(self, eng, out, in_, r, w):
        if eng == ACT:
            return self.op(eng, lambda e: e.copy(out, in_), r=r, w=w)
        return self.op(eng, lambda e: e.tensor_copy(out, in_), r=r, w=w)


class Ctx:
    def __init__(self, name):
        self.nc = bass.Bass("TRN2", target_bir_lowering=False)
        self.P = Prog(self.nc)
        self.nres = 0
        self.stacks = []
        nc = self.nc
        self.ps = [nc.alloc_psum_tensor(f"psb{i}", [128, 512], F32) for i in range(8)]
        self.rps = [Res(f"ps{i}") for i in range(8)]

    def res(self, name):
        self.nres += 1
        return Res(f"{name}{self.nres}")

    def sb(self, name, shape, dt):
        self.nres += 1
        nm = f"s_{name}_{self.nres}"
        if self.stacks:
            return self.stacks[-1].enter_context(self.nc.sbuf_tensor(nm, list(shape), dt))
        return self.nc.alloc_sbuf_tensor(nm, list(shape), dt)

    def phase(self):
        return _Phase(self)

    def din(self, name, shape, dt):
        return self.nc.dram_tensor(name, list(shape), dt, kind="ExternalInput").ap()

    def dout(self, name, shape, dt):
        return self.nc.dram_tensor(name, list(shape), dt, kind="ExternalOutput").ap()

    def dint(self, name, shape, dt):
        return self.nc.dram_tensor(name, list(shape), dt).ap()

    def finish(self, out_ress):
        self.P.wait_all(SP, out_ress)
        self.P.q[SP].append(([(o.dsem, o.dcnt) for o in self.P.dma_owners if o.dcnt > 0], None, None))
        with self.nc.Block() as block:
            self.P.replay(block)
        return self.nc


class _Phase:
    def __init__(self, C):
        self.C = C

    def __enter__(self):
        import contextlib
        self.st = contextlib.ExitStack()
        self.C.stacks.append(self.st)
        return self

    def __exit__(self, *a):
        self.C.P.barrier()
        self.C.stacks.pop()
        self.st.close()
        return False


def load_consts(C, cst_ap):
    P = C.P
    c32 = C.sb("c32", [128, 384], F32)
    cbf = C.sb("cbf", [128, 384], BF16)
    r32, rbf = C.res("c32"), C.res("cbf")
    P.dma(SP, c32[:], cst_ap, w=[r32])
    P.cp(DVE, cbf[:], c32[:], r=[r32], w=[rbf])
    C.c32, C.cbf, C.r_c32, C.r_cbf = c32, cbf, r32, rbf
    C.ones = cbf[:, 0:128]
    C.bones = cbf[:, 128:256]
    C.ident = cbf[:, 256:384]
    C.ident32 = c32[:, 256:384]


def rmsnorm_T(C, xT_ap, ntok, g_ap, r_g, out_bf, r_out, name):
    P = C.P
    nseg = [(s, min(512, ntok - s)) for s in range(0, ntok, 512)]
    assert len(nseg) <= 3
    xs = [C.sb(f"{name}_x{i}", [128, ntok], F32) for i in range(2)]
    rx = [C.res(f"{name}_x") for _ in range(2)]
    sq = [C.sb(f"{name}_sq{i}", [128, ntok], BF16) for i in range(2)]
    rsq = [C.res(f"{name}_sq") for _ in range(2)]
    rstd = C.sb(f"{name}_rstd", [128, ntok], F32)
    r_rstd = C.res(f"{name}_rstd")
    banks = [5, 6, 7][:len(nseg)]
    for c in range(KC):
        i = c % 2
        P.dma(SP, xs[i][:], xT_ap[:, c, :], w=[rx[i]])
        P.act(sq[i][:], xs[i][:], AF.Square, r=[rx[i]], w=[rsq[i]])
        for b, (s, n) in zip(banks, nseg):
            P.mm(C.ps[b][:, 0:n], C.ones, sq[i][:, s:s + n], c == 0, c == KC - 1, r=[rsq[i], C.r_cbf], w=[C.rps[b]])
    for b, (s, n) in zip(banks, nseg):
        P.act(rstd[:, s:s + n], C.ps[b][:, 0:n], AF.Sqrt, r=[C.rps[b], C.r_eps], w=[r_rstd], bias=C.eps_ap, scale=1.0 / D)
    P.op(DVE, lambda e: e.reciprocal(rstd[:], rstd[:]), r=[r_rstd], w=[r_rstd])
    for c in range(KC):
        i = c % 2
        P.dma(SP, xs[i][:], xT_ap[:, c, :], w=[rx[i]])
        P.stt(DVE, out_bf[:, c, :], xs[i][:], g_ap[:, c:c + 1], rstd[:], ALU.mult, ALU.mult, r=[rx[i], r_rstd, r_g], w=[r_out])


NPA = 32 + 2 + 42 + 5 * 12 + 1


def build_stage_a():
    C = Ctx("A")
    P, nc = C.P, C.nc
    NT = TL + 1
    xT = C.din("xT", [128, KC, NT], F32)
    prm_d = C.din("prm", [128, NPA], F32)
    cst_d = C.din("cst", [128, 384], F32)
    wqk_d = C.din("wqk", [24, 128, KC * 128], F32)
    wv_d = C.din("wv", [3, 128, KC * 512], F32)
    wrw_d = C.din("wrw", [42, 128, KC * 128], F32)
    lora_d = C.din("lora", [128, 6 * 1536], F32)
    qT_o = C.dout("qT", [12, 128, TL], BF16)
    kT_o = C.dout("kT", [12, 128, TL], BF16)
    v_o = C.dout("v", [TL, SBW], BF16)
    rwf_o = C.dout("rwf", [4, RWW, TL], BF16)
    rwv_o = C.dout("rwv", [TL, RWW], BF16)
    ld_o = C.dout("ld", [RWW, TL], F32)
    g_o = C.dout("g", [RWW, TL], F32)
    bon_o = C.dout("bon", [RWW, TL], F32)
    xn_o = C.dout("xn", [128, KC, TL], BF16)
    r_out = C.res("outs")

    load_consts(C, cst_d)
    prm = C.sb("prm", [128, NPA], F32)
    r_prm = C.res("prm")
    P.dma(SP, prm[:], prm_d, w=[r_prm])
    C.eps_ap = prm[:, NPA - 1:NPA]
    C.r_eps = r_prm
    g_attn = prm[:, 0:32]
    o_mix, o_w0, o_a0, o_kk, o_ka, o_rk = 34, 76, 88, 100, 112, 124

    xn = C.sb("xn", [128, KC, NT], BF16)
    r_xn = C.res("xn")
    NW = 3
    wb = [C.sb(f"wb{i}", [128, KC, 128], BF16) for i in range(NW)]
    rwb = [C.res("wb") for _ in range(NW)]
    with C.phase():
        rmsnorm_T(C, xT, NT, g_attn, r_prm, xn, r_xn, "na")
    P.dma(SP, xn_o, xn[:, :, 1:NT], r=[r_xn], w=[r_out], own="r")
    wcount = [0]

    def load_w(src):
        i = wcount[0] % NW
        wcount[0] += 1
        P.dma(POOL, wb[i][:], src.rearrange("p (c m) -> p c m", m=128), w=[rwb[i]])
        return wb[i], rwb[i]

    def gemm_fm(wt, rw, banks, segs):
        for b, (s, n) in zip(banks, segs):
            for c in range(KC):
                P.mm(C.ps[b][:, 0:n], wt[:, c, :], xn[:, c, s:s + n], c == 0, c == KC - 1, r=[rw, r_xn], w=[C.rps[b]])

    seg_main = [(1, 512), (513, 512)]
    seg_halo = [(0, 1)]

    with C.phase():
        qk32 = [C.sb(f"qk32_{i}", [128, TL], F32) for i in range(2)]
        rqk32 = [C.res("qk32") for _ in range(2)]
        qksq = [C.sb(f"qksq_{i}", [128, TL], BF16) for i in range(2)]
        rqksq = [C.res("qksq") for _ in range(2)]
        qkr = [C.sb(f"qkr_{i}", [128, TL], F32) for i in range(2)]
        rqkr = [C.res("qkr") for _ in range(2)]
        qkb = [C.sb(f"qkb_{i}", [128, TL], BF16) for i in range(2)]
        rqkb = [C.res("qkb") for _ in range(2)]
        def qk_gemm(j):
            wt, rw = load_w(wqk_d[j])
            gemm_fm(wt, rw, [0, 1] if j % 2 == 0 else [2, 3], seg_main)

        def qk_post(j):
            i = j % 2
            mb = [0, 1] if i == 0 else [2, 3]
            for h, b in enumerate(mb):
                P.cp(ACT, qk32[i][:, h * 512:(h + 1) * 512], C.ps[b][:, :], r=[C.rps[b]], w=[rqk32[i]])
            P.act(qksq[i][:], qk32[i][:], AF.Square, r=[rqk32[i]], w=[rqksq[i]])
            for h in range(2):
                P.mm(C.ps[4][:, :], C.ones, qksq[i][:, h * 512:(h + 1) * 512], True, True, r=[rqksq[i], C.r_cbf], w=[C.rps[4]])
                P.act(qkr[i][:, h * 512:(h + 1) * 512], C.ps[4][:, :], AF.Sqrt, r=[C.rps[4], r_prm], w=[rqkr[i]], bias=C.eps_ap, scale=1.0 / 128)
            P.op(DVE, lambda e, i=i: e.reciprocal(qkr[i][:], qkr[i][:]), r=[rqkr[i]], w=[rqkr[i]])
            gcol = prm[:, 32:33] if j < 12 else prm[:, 33:34]
            P.stt(DVE, qkb[i][:], qk32[i][:], gcol, qkr[i][:], ALU.mult, ALU.mult, r=[rqk32[i], rqkr[i], r_prm], w=[rqkb[i]])
            dst = qT_o[j] if j < 12 else kT_o[j - 12]
            P.dma(SP, dst, qkb[i][:], r=[rqkb[i]], w=[r_out], own="r")

        qk_gemm(0)
        for j in range(24):
            if j + 1 < 24:
                qk_gemm(j + 1)
            qk_post(j)

    with C.phase():
        wvb = C.sb("wvb", [128, KC, 512], BF16)
        rwvb = C.res("wvb")
        vst = [C.sb(f"vst{i}", [128, 512], BF16) for i in range(2)]
        rvst = [C.res("vst") for _ in range(2)]
        cnt = 0
        for gcol in range(3):
            P.dma(POOL, wvb[:], wv_d[gcol].rearrange("p (c m) -> p c m", m=512), w=[rwvb])
            for tt in range(8):
                b = cnt % 4
                i = cnt % 2
                cnt += 1
                for c in range(KC):
                    P.mm(C.ps[b][:, :], xn[:, c, 1 + tt * 128:1 + (tt + 1) * 128], wvb[:, c, :], c == 0, c == KC - 1, r=[rwvb, r_xn], w=[C.rps[b]])
                P.cp(ACT if cnt % 2 else DVE, vst[i][:], C.ps[b][:, :], r=[C.rps[b]], w=[rvst[i]])
                P.dma(SP, v_o[tt * 128:(tt + 1) * 128, gcol * 512:(gcol + 1) * 512], vst[i][:], r=[rvst[i]], w=[r_out], own="r")

    rwph = C.phase()
    rwph.__enter__()
    lorab = C.sb("lorab", [128, 6, 1536], BF16)
    r_lorab = C.res("lorab")
    with C.phase():
        lora32 = C.sb("lora32", [128, 1536], F32)
        r_l32 = C.res("l32")
        for q in range(6):
            P.dma(SP, lora32[:], lora_d[:, q * 1536:(q + 1) * 1536], w=[r_l32])
            P.cp(DVE, lorab[:, q, :], lora32[:], r=[r_l32], w=[r_lorab])

    seg32 = [C.sb(f"seg32_{i}", [128, NT], F32) for i in range(2)]
    rseg = [C.res("seg") for _ in range(2)]
    tmp = [C.sb(f"tsd_{i}", [128, TL], F32) for i in range(2)]
    rtmp = [C.res("tsd") for _ in range(2)]
    scount = [0]

    def rw_tile(jt, out_ap, r_o, post=None):
        wt, rw = load_w(wrw_d[jt])
        i = scount[0] % 2
        scount[0] += 1
        mb = [0, 1] if i == 0 else [2, 3]
        gemm_fm(wt, rw, mb, seg_main)
        gemm_fm(wt, rw, [4], seg_halo)
        P.cp(ACT, seg32[i][:, 1:513], C.ps[mb[0]][:, :], r=[C.rps[mb[0]]], w=[rseg[i]])
        P.cp(ACT, seg32[i][:, 513:1025], C.ps[mb[1]][:, :], r=[C.rps[mb[1]]], w=[rseg[i]])
        P.cp(DVE, seg32[i][:, 0:1], C.ps[4][:, 0:1], r=[C.rps[4]], w=[rseg[i]])
        P.tt(DVE, tmp[i][:], seg32[i][:, 0:TL], seg32[i][:, 1:NT], ALU.subtract, r=[rseg[i]], w=[rtmp[i]])
        P.stt(DVE, out_ap, tmp[i][:], prm[:, o_mix + jt:o_mix + jt + 1], seg32[i][:, 1:NT], ALU.mult, ALU.add,
              r=[rtmp[i], rseg[i], r_prm], w=[r_o])

    linb = C.sb("linb", [128, 6, TL], BF16)
    r_linb = C.res("linb")
    with C.phase():
        lin = C.sb("lin", [128, 6, TL], F32)
        r_lin = C.res("lin")
        for q in range(6):
            rw_tile(36 + q, lin[:, q, :], r_lin)
        P.act(linb[:, 0, :], lin[:, 0, :], AF.Tanh, r=[r_lin], w=[r_linb])
        P.cp(DVE, linb[:, 1, :], lin[:, 1, :], r=[r_lin], w=[r_linb])
        for q in range(2, 6):
            P.act(linb[:, q, :], lin[:, q, :], AF.Sigmoid, r=[r_lin], w=[r_linb])

    def f32buf(n):
        return C.sb(n, [128, TL], F32), C.res(n)

    def bfbuf(n):
        return C.sb(n, [128, TL], BF16), C.res(n)

    def dbl(fn, n):
        return [fn(f"{n}{i}") for i in range(2)]
    rr2, kk2, vv2, aa2 = dbl(f32buf, "rr"), dbl(f32buf, "kk"), dbl(f32buf, "vv"), dbl(f32buf, "aa")
    ld1, gt1 = f32buf("ldt"), f32buf("gt")
    ld2, gt2 = [ld1, ld1], [gt1, gt1]
    t1, r_t1 = f32buf("t1")
    t2, r_t2 = f32buf("t2")
    bon, r_bon = f32buf("bon")
    ob = [bfbuf(f"ob{i}") for i in range(5)]
    sqb, r_sqb = bfbuf("sqb")
    vT = [C.sb(f"vT{i}", [128, 128], BF16) for i in range(2)]
    rvT = [C.res("vT") for _ in range(2)]

    def bufs_for(ct):
        i = ct % 2
        return rr2[i], kk2[i], vv2[i], aa2[i], ld2[i], gt2[i]

    def rw_part1(ct):
        (rr, r_rr), (kk_, r_k), (vv, r_v), (aa, r_a), (ldt, r_ld), (gt, r_g) = bufs_for(ct)
        cs = slice(ct * 128, (ct + 1) * 128)
        cs = slice(ct * 128, (ct + 1) * 128)
        for h in range(2):
            hs = slice(h * 512, (h + 1) * 512)
            P.mm(C.ps[5][:, :], lorab[:, 0, cs], linb[:, 0, hs], True, True, r=[r_lorab, r_linb], w=[C.rps[5]])
            P.act(ldt[:, hs], C.ps[5][:, :], AF.Sigmoid, r=[C.rps[5], r_prm], w=[r_ld], bias=prm[:, o_w0 + ct:o_w0 + ct + 1], scale=1.0)
            P.mm(C.ps[6][:, :], lorab[:, 1, cs], linb[:, 1, hs], True, True, r=[r_lorab, r_linb], w=[C.rps[6]])
            P.act(aa[:, hs], C.ps[6][:, :], AF.Sigmoid, r=[C.rps[6], r_prm], w=[r_a], bias=prm[:, o_a0 + ct:o_a0 + ct + 1], scale=1.0)
            for q in range(4):
                P.mm(C.ps[7][:, :], lorab[:, 2 + q, cs], linb[:, 2 + q, hs], q == 0, q == 3, r=[r_lorab, r_linb], w=[C.rps[7]])
            P.cp(DVE, gt[:, hs], C.ps[7][:, :], r=[C.rps[7]], w=[r_g])
        P.ts(DVE, ldt[:], ldt[:], -float(np.exp(-0.5)), None, ALU.mult, None, r=[r_ld], w=[r_ld])
        P.dma(SP, ld_o[cs, :], ldt[:], r=[r_ld], w=[r_out], own="r")
        P.dma(SP, g_o[cs, :], gt[:], r=[r_g], w=[r_out], own="r")
        rw_tile(ct, rr[:], r_rr)
        rw_tile(12 + ct, kk_[:], r_k)
        rw_tile(24 + ct, vv[:], r_v)

    def rw_part2(ct):
        (rr, r_rr), (kk_, r_k), (vv, r_v), (aa, r_a), (ldt, r_ld), (gt, r_g) = bufs_for(ct)
        cs = slice(ct * 128, (ct + 1) * 128)
        P.cp(ACT, ob[0][0][:], rr[:], r=[r_rr], w=[ob[0][1]])
        P.dma(SP, rwf_o[0, cs, :], ob[0][0][:], r=[ob[0][1]], w=[r_out], own="r")
        P.ts(DVE, t1[:], kk_[:], prm[:, o_kk + ct:o_kk + ct + 1], None, ALU.mult, None, r=[r_k, r_prm], w=[r_t1])
        P.act(sqb[:], t1[:], AF.Square, r=[r_t1], w=[r_sqb])
        for h in range(2):
            hs = slice(h * 512, (h + 1) * 512)
            P.mm(C.ps[5][:, :], C.bones, sqb[:, hs], True, True, r=[r_sqb, C.r_cbf], w=[C.rps[5]])
            P.act(t2[:, hs], C.ps[5][:, :], AF.Sqrt, r=[C.rps[5]], w=[r_t2])
        P.ts(DVE, t2[:], t2[:], 1e-12, None, ALU.max, None, r=[r_t2], w=[r_t2])
        P.op(DVE, lambda e: e.reciprocal(t2[:], t2[:]), r=[r_t2], w=[r_t2])
        P.tt(DVE, t1[:], t1[:], t2[:], ALU.mult, r=[r_t1, r_t2], w=[r_t1])
        P.ts(DVE, ob[2][0][:], t1[:], -1.0, None, ALU.mult, None, r=[r_t1], w=[ob[2][1]])
        P.tt(DVE, ob[3][0][:], t1[:], aa[:], ALU.mult, r=[r_t1, r_a], w=[ob[3][1]])
        P.dma(SP, rwf_o[2, cs, :], ob[2][0][:], r=[ob[2][1]], w=[r_out], own="r")
        P.dma(SP, rwf_o[3, cs, :], ob[3][0][:], r=[ob[3][1]], w=[r_out], own="r")
        P.ts(DVE, t2[:], aa[:], -1.0, prm[:, o_ka + ct:o_ka + ct + 1], ALU.add, ALU.mult, r=[r_a, r_prm, r_t2], w=[r_t2])
        P.stt(DVE, kk_[:], t2[:], 1.0, kk_[:], ALU.add, ALU.mult, r=[r_t2, r_k], w=[r_k])
        P.cp(ACT, ob[1][0][:], kk_[:], r=[r_k], w=[ob[1][1]])
        P.dma(SP, rwf_o[1, cs, :], ob[1][0][:], r=[ob[1][1]], w=[r_out], own="r")
        P.stt(DVE, t1[:], rr[:], prm[:, o_rk + ct:o_rk + ct + 1], kk_[:], ALU.mult, ALU.mult, r=[r_rr, r_k, r_prm, r_t1], w=[r_t1])
        P.cp(ACT, sqb[:], t1[:], r=[r_t1], w=[r_sqb])
        for h in range(2):
            hs = slice(h * 512, (h + 1) * 512)
            P.mm(C.ps[6][:, :], C.bones, sqb[:, hs], True, True, r=[r_sqb, C.r_cbf], w=[C.rps[6]])
            P.tt(DVE, bon[:, hs], C.ps[6][:, :], vv[:, hs], ALU.mult, r=[C.rps[6], r_v], w=[r_bon])
        P.dma(SP, bon_o[cs, :], bon[:], r=[r_bon], w=[r_out], own="r")
        P.cp(ACT, ob[4][0][:], vv[:], r=[r_v], w=[ob[4][1]])
        for tt in range(8):
            i = tt % 2
            P.op(PE, lambda e, tt=tt: e.transpose(C.ps[7][:, 0:64].bitcast(BF16), ob[4][0][:, tt * 128:(tt + 1) * 128], C.ident),
                 r=[ob[4][1], C.r_cbf], w=[C.rps[7]])
            P.cp(DVE, vT[i][:], C.ps[7][:, 0:64].bitcast(BF16), r=[C.rps[7]], w=[rvT[i]])
            P.dma(SP, rwv_o[tt * 128:(tt + 1) * 128, cs], vT[i][:], r=[rvT[i]], w=[r_out], own="r")


    rw_part1(0)
    for ct in range(12):
        if ct + 1 < 12:
            rw_part1(ct + 1)
        rw_part2(ct)
    rwph.__exit__(None, None, None)
    return C.finish([r_out])


def _tile_lhsT(w, ncols=128):
    K, M = w.shape
    assert K % 128 == 0 and M % ncols == 0
    kc = K // 128
    a = w.reshape(kc, 128, M // ncols, ncols).transpose(2, 1, 0, 3)
    return np.ascontiguousarray(a).reshape(M // ncols, 128, kc * ncols)


def _pad_cols(w, m):
    if w.shape[1] == m:
        return w
    out = np.zeros((w.shape[0], m), w.dtype)
    out[:, :w.shape[1]] = w
    return out


def _cols(v, n):
    return np.ascontiguousarray(v.reshape(n, 128).T)


def _consts():
    c = np.zeros((128, 384), np.float32)
    c[:, 0:128] = 1.0
    c[0:64, 128:192] = 1.0
    c[64:128, 192:256] = 1.0
    c[:, 256:384] = np.eye(128, dtype=np.float32)
    return c


def prep_stage_a(inp):
    x = inp["x"][0]
    w_in = inp["w_in"][0]
    shared = {}
    shared["cst"] = _consts()
    shared["wqk"] = _tile_lhsT(w_in[:, 0:3072])
    shared["wv"] = _tile_lhsT(w_in[:, 3072:4608], 512)
    shared["wrw"] = _tile_lhsT(_pad_cols(w_in[:, 4608:4608 + RW_SEG], 42 * 128))
    lora = np.zeros((128, 6 * 1536), np.float32)
    lora[:, 0:1536] = inp["rw_w_up"][0]
    lora[:, 1536:3072] = inp["rw_a_up"][0]
    gup = np.zeros((512, 1536), np.float32)
    gup[:480] = inp["rw_g_up"][0]
    for q in range(4):
        lora[:, (2 + q) * 1536:(3 + q) * 1536] = gup[q * 128:(q + 1) * 128]
    shared["lora"] = lora
    prm = np.zeros((128, NPA), np.float32)
    prm[:, 0:32] = _cols(inp["attn_norm_g"][0], 32)
    prm[:, 32] = inp["sb_q_norm_g"][0]
    prm[:, 33] = inp["sb_k_norm_g"][0]
    mix = np.zeros(42 * 128, np.float32)
    mix[:RW_SEG] = inp["rw_mix"][0]
    prm[:, 34:76] = _cols(mix, 42)
    prm[:, 76:88] = _cols(inp["rw_w0"][0], 12)
    prm[:, 88:100] = _cols(inp["rw_a0"][0], 12)
    prm[:, 100:112] = _cols(inp["rw_k_k"][0], 12)
    prm[:, 112:124] = _cols(inp["rw_k_a"][0], 12)
    prm[:, 124:136] = _cols(inp["rw_r_k"][0].reshape(-1), 12)
    prm[:, NPA - 1] = EPS
    shared["prm"] = prm
    maps = []
    for c in range(NCORES):
        xc = np.zeros((TL + 1, D), np.float32)
        lo = c * TL - 1
        if c == 0:
            xc[1:] = x[0:TL]
        else:
            xc[:] = x[lo:lo + TL + 1]
        xT = np.ascontiguousarray(xc.T.reshape(KC, 128, TL + 1).transpose(1, 0, 2))
        m = dict(shared)
        m["xT"] = xT
        maps.append(m)
    return maps


CH = 64
SCL = 512
NCH = SCL // CH
NHB = 3


def build_stage_b(do_sb=True, do_rw=True, nsc=T // SCL, ngl=None, do_gates=True):
    C = Ctx("B")
    P, nc = C.P, C.nc
    cst_d = C.din("cst", [128, 384], F32)
    cst2_d = C.din("cst2", [128, 512], F32)
    qF_d = C.din("qF", [128, T], BF16)
    kF_d = C.din("kF", [128, T], BF16)
    vF_d = C.din("vF", [T, 128], BF16)
    qH_d = C.din("qH", [128, T // 2], BF16)
    kH_d = C.din("kH", [128, T], BF16)
    vH_d = C.din("vH", [T, 128], BF16)
    mF_d = C.din("mF", [128, 4, 512], BF16)
    mH_d = C.din("mH", [128, 8, 512], BF16)
    rwf_d = C.din("rwf", [NHB, 64, 4, T], BF16)
    ld_d = C.din("ld", [NHB, 64, T], F32)
    rwv_d = C.din("rwv", [NHB, T, 64], BF16)
    xn_d = C.din("xn", [128, KC, TL], BF16)
    wg_d = C.din("wg", [96, 128, KC * 128], F32)
    sg_o = C.dout("sg", [96, 128, TL], BF16)
    oF_o = C.dout("oF", [128, T], BF16)
    oH_o = C.dout("oH", [128, T // 2], BF16)
    y_o = C.dout("y", [NHB, 64, T], F32)
    r_out = C.res("outs")

    load_consts(C, cst_d)
    c2 = C.sb("c2", [128, 512], F32)
    c2b = C.sb("c2b", [128, 512], BF16)
    r_c2, r_c2b = C.res("c2"), C.res("c2b")
    P.dma(SP, c2[:], cst2_d, w=[r_c2])
    P.cp(DVE, c2b[:], c2[:], r=[r_c2], w=[r_c2b])
    tri, trip = c2b[:, 0:128], c2b[:, 128:256]
    maskM = c2[:, 256:384]
    maskN = c2[0:64, 384:448]
    one_col = C.c32[:, 0:1]

    slots = [(C.ps[b][:, 0:128], C.rps[b]) for b in (5, 6, 7)]
    sl_i = [0]

    def slot():
        s = slots[sl_i[0] % len(slots)]
        sl_i[0] += 1
        return s

    sbshare = {}

    def gates_chain(per_yield=int(os.environ.get("GPY", "14"))):
        qk, v = sbshare["qk"], sbshare["v"]
        r_q, r_k, r_v = sbshare["r"]
        ost, r_ost = sbshare["ost"], sbshare["r_ost"]
        xnh = qk[:, :].rearrange("p (c t) -> p c t", c=KC)
        r_xnh = C.res("xnh")
        wbs = [v[:, 0:KC, :], v[:, KC:2 * KC, :]]
        rwbs = [C.res("gwb") for _ in range(2)]
        cnt = 0
        wi = 0
        for hh in range(2):
            P.dma(SP, xnh, xn_d[:, :, hh * 512:(hh + 1) * 512], w=[r_xnh, r_q, r_k])
            for j in range(96):
                i = wi % 2
                P.dma(POOL, wbs[i], wg_d[j].rearrange("p (c m) -> p c m", m=128), w=[rwbs[i], r_v] if wi < 2 else [rwbs[i]])
                wi += 1
                b = j % 4
                for c in range(KC):
                    P.mm(C.ps[b][:, :], wbs[i][:, c, :], xnh[:, c, :], c == 0, c == KC - 1, r=[rwbs[i], r_xnh], w=[C.rps[b]])
                    cnt += 1
                    if cnt % per_yield == 0:
                        yield
                io = j % 2
                P.act(ost[io][:], C.ps[b][:, :], AF.Sigmoid, r=[C.rps[b]], w=[r_ost[io]])
                P.dma(SP, sg_o[j][:, hh * 512:(hh + 1) * 512], ost[io][:], r=[r_ost[io]], w=[r_out], own="r")

    def sb_pipeline(units):
        ZB, RB, OB = [0, 1], [2, 3], 4
        qk = C.sb("qk", [128, 2 * T], BF16)
        q, k = qk[:, 0:T], qk[:, T:2 * T]
        v = C.sb("v", [128, T // 128, 128], BF16)
        r_q, r_k, r_v = C.res("q"), C.res("k"), C.res("v")
        sbshare.update(qk=qk, v=v, r=(r_q, r_k, r_v))
        NE, NS, NX, NWT = 6, 4, 3, 3
        e32 = [C.sb(f"e32_{i}", [128, 512], F32) for i in range(NE)]
        re32 = [C.res("e32") for _ in range(NE)]
        sp = [C.sb(f"sp_{i}", [128, 512], BF16) for i in range(NS)]
        rsp = [C.res("sp") for _ in range(NS)]
        ex = [C.sb(f"ex_{i}", [128, 512], F32) for i in range(NX)]
        rex = [C.res("ex") for _ in range(NX)]
        wt = [C.sb(f"w_{i}", [128, 512], BF16) for i in range(NWT)]
        rwt = [C.res("w") for _ in range(NWT)]
        ost = [C.sb(f"ost{i}", [128, 512], BF16) for i in range(2)]
        r_ost = [C.res("ost") for _ in range(2)]
        sbshare.update(ost=ost, r_ost=r_ost)
        sacc = C.sb("sacc", [128, 512], F32)
        r_sacc = C.res("sacc")
        steps = []
        for ui, (q_d, k_d, v_d, m_sb, nq_blocks, kpg, o_d) in enumerate(units):
            ngroups = nq_blocks // 4 if ngl is None else ngl
            for g in range(ngroups):
                nkb = kpg * (g + 1)
                for idx, kb in enumerate(reversed(range(nkb))):
                    steps.append(dict(u=ui, g=g, kb=kb, first=idx == 0, last=idx == nkb - 1, mj=kb - (nkb - kpg)))
        ns = len(steps)
        gcount = [0]

        def load_unit(ui):
            q_d, k_d, v_d, m_sb, nq_blocks, kpg, o_d = units[ui]
            P.dma(SP, q[:, 0:nq_blocks * 128], q_d, w=[r_q])
            P.dma(SP, k[:], k_d, w=[r_k])
            P.dma(SP, v[:], v_d.rearrange("(b p) d -> p b d", p=128), w=[r_v])

        def st1(i):
            st = steps[i]
            if i == 0 or steps[i - 1]["u"] != st["u"]:
                load_unit(st["u"])
            zb = ZB[i % 2]
            P.mm(C.ps[zb][:, :], k[:, st["kb"] * 128:(st["kb"] + 1) * 128], q[:, st["g"] * 512:(st["g"] + 1) * 512], True, True,
                 r=[r_k, r_q], w=[C.rps[zb]])

        def st2(i):
            st = steps[i]
            zb = ZB[i % 2]
            ie, is_ = i % NE, i % NS
            P.act(e32[ie][:], C.ps[zb][:, :], AF.Exp, r=[C.rps[zb]], w=[re32[ie]], scale=float(128 ** -0.5))
            P.act(sp[is_][:], e32[ie][:], AF.Ln, r=[re32[ie], C.r_c32], w=[rsp[is_]], bias=one_col, scale=1.0)
            if st["mj"] >= 0:
                m_sb = units[st["u"]][3]
                P.tt(POOL, sp[is_][:], sp[is_][:], m_sb[:, st["mj"], :], ALU.mult, r=[rsp[is_], r_m], w=[rsp[is_]])

        def st3(i):
            st = steps[i]
            rb = RB[i % 2]
            is_ = i % NS
            P.mm(C.ps[rb][:, :], tri, sp[is_][:], True, st["first"], r=[rsp[is_], r_c2b], w=[C.rps[rb]])
            if not st["first"]:
                P.mm(C.ps[rb][:, :], C.c32[:, 0:128], sacc[:], False, True, r=[r_sacc, C.r_c32], w=[C.rps[rb]])

        def st4(i):
            st = steps[i]
            rb = RB[i % 2]
            is_, ix = i % NS, i % NX
            P.act(ex[ix][:], C.ps[rb][:, :], AF.Exp, r=[C.rps[rb]], w=[rex[ix]], scale=-1.0)
            if not st["last"]:
                if st["first"]:
                    P.cp(POOL, sacc[:], sp[is_][:], r=[rsp[is_]], w=[r_sacc])
                else:
                    P.tt(POOL, sacc[:], sacc[:], sp[is_][:], ALU.add, r=[rsp[is_], r_sacc], w=[r_sacc])

        def st5(i):
            st = steps[i]
            ie, ix, iw = i % NE, i % NX, i % NWT
            P.tt(DVE, wt[iw][:], e32[ie][:], ex[ix][:], ALU.mult, r=[re32[ie], rex[ix]], w=[rwt[iw]])
            if st["mj"] >= 0:
                m_sb = units[st["u"]][3]
                P.tt(POOL, wt[iw][:], wt[iw][:], m_sb[:, st["mj"], :], ALU.mult, r=[rwt[iw], r_m], w=[rwt[iw]])

        def st6(i):
            st = steps[i]
            iw = i % NWT
            P.mm(C.ps[OB][:, :], v[:, st["kb"], :], wt[iw][:], st["first"], st["last"], r=[r_v, rwt[iw]], w=[C.rps[OB]])
            if st["last"]:
                o_d = units[st["u"]][6]
                io = gcount[0] % 2
                gcount[0] += 1
                P.cp(DVE, ost[io][:], C.ps[OB][:, :], r=[C.rps[OB]], w=[r_ost[io]])
                P.dma(SP, o_d[:, st["g"] * 512:(st["g"] + 1) * 512], ost[io][:], r=[r_ost[io]], w=[r_out], own="r")

        stages = [st1, st2, st3, st4, st5, st6]
        bounds = [0] + [i for i in range(1, ns) if steps[i]["u"] != steps[i - 1]["u"]] + [ns]
        for lo, hi in zip(bounds[:-1], bounds[1:]):
            for t in range(lo, hi + len(stages) - 1):
                for kk in reversed(range(len(stages))):
                    i = t - kk
                    if lo <= i < hi:
                        stages[kk](i)
                yield

    def rw_chain():
        H3 = NHB
        BA, BB, BC = 5, 6, 7
        pA, rA = C.ps[BA], C.rps[BA]
        pB, rB = C.ps[BB], C.rps[BB]
        pC, rC = C.ps[BC], C.rps[BC]
        S32 = C.sb("S32", [64, H3, 64], F32)
        Sbf = C.sb("Sbf", [64, H3, 64], BF16)
        rS = C.res("S")
        P.op(DVE, lambda e: e.memset(S32[:], 0.0), w=[rS])
        P.op(DVE, lambda e: e.memset(Sbf[:], 0.0), w=[rS])
        msk = C.sb("rmsk", [64, H3 * SCL], F32)
        r_msk = C.res("rmsk")
        P.op(DVE, lambda e: e.memset(msk[:], 1.0), w=[r_msk])
        P.op(DVE, lambda e: e.memset(msk[:].rearrange("p (n c) -> p n c", c=CH)[:, :, 0:1], 0.0), w=[r_msk])
        mM3 = C.sb("mM3", [128, H3, 128], F32)
        mN3 = C.sb("mN3", [64, H3, 64], F32)
        id3 = C.sb("id3", [64, H3, 64], F32)
        r_k3 = C.res("k3")
        for h in range(H3):
            P.cp(DVE, mM3[:, h, :], maskM, r=[r_c2], w=[r_k3])
            P.cp(DVE, mN3[:, h, :], maskN, r=[r_c2], w=[r_k3])
            P.cp(DVE, id3[:, h, :], C.ident32[0:64, 0:64], r=[C.r_c32], w=[r_k3])
        fm = C.sb("fm", [64, H3, 4, SCL], BF16)
        ldb = C.sb("ldb", [64, H3 * SCL], F32)
        UV = C.sb("UV", [128, NCH, H3, 64], BF16)
        r_fm, r_ld, r_V, r_U = C.res("fm"), C.res("ld"), C.res("V"), C.res("U")
        cum = C.sb("cum", [64, H3 * SCL], F32)
        tA = C.sb("tA", [64, H3 * SCL], F32)
        tB = C.sb("tB", [64, H3 * SCL], F32)
        r_cum, r_tA, r_tB = C.res("cum"), C.res("tA"), C.res("tB")
        PC = C.sb("PC", [64, H3, NCH], F32)
        r_PC = C.res("PC")
        Q2 = C.sb("Q2", [64, H3, NCH, 2, CH], BF16)
        KB = C.sb("KB", [64, H3, NCH, 2, CH], BF16)
        KBb = C.sb("KBb", [64, H3, NCH, 2, CH], BF16)
        r_Q2, r_KB, r_KBb = C.res("Q2"), C.res("KB"), C.res("KBb")
        Y = C.sb("Y", [64, H3, SCL], F32)
        r_Y = C.res("Y")

        def wb2(name, shape, dt):
            return [C.sb(f"{name}_{i}", shape, dt) for i in range(2)], [C.res(name) for i in range(2)]
        Mm, r_Mm = wb2("Mm", [128, H3, 128], BF16)
        LTa, r_LTa = wb2("LTa", [64, H3, 128], F32)
        LTb, r_LTb = wb2("LTb", [64, H3, 128], F32)
        Na, r_Na = wb2("Na", [64, H3, 64], F32)
        Nb, r_Nb = wb2("Nb", [64, H3, 64], F32)
        Tt, r_Tt = wb2("Tt", [64, H3, 64], BF16)
        W2, r_W2 = wb2("W2", [64, H3, 64], F32)
        Xs, r_Xs = wb2("Xs", [64, H3, 64], BF16)
        KT, r_KT = wb2("KT", [128, H3, 64], BF16)

        def v4(ap):
            return ap.rearrange("p (h n c) -> p h n c", h=H3, c=CH)

        def f4(j):
            return fm[:, :, j, :].rearrange("p h (n c) -> p h n c", c=CH)

        def load(sc):
            ts_ = slice(sc * SCL, (sc + 1) * SCL)
            for h in range(H3):
                P.dma(SP, fm[:, h, :, :], rwf_d[h, :, :, ts_], w=[r_fm])
                P.dma(SP, ldb[:, h * SCL:(h + 1) * SCL], ld_d[h, :, ts_], w=[r_ld])
                P.dma(SP, UV[64:128, :, h, :], rwv_d[h, ts_, :].rearrange("(n c) v -> c n v", c=CH), w=[r_V])
            P.op(DVE, lambda e: e.tensor_tensor_scan(cum[:], msk[:], ldb[:], 0.0, ALU.mult, ALU.add), r=[r_msk, r_ld], w=[r_cum])
            P.act(tA[:], cum[:], AF.Exp, r=[r_cum], w=[r_tA])
            P.tt(DVE, Q2[:, :, :, 1, :], f4(0), v4(tA[:]), ALU.mult, r=[r_fm, r_tA], w=[r_Q2])
            P.tt(DVE, tB[:], cum[:], ldb[:], ALU.subtract, r=[r_cum, r_ld], w=[r_tB])
            P.act(tA[:], tB[:], AF.Exp, r=[r_tB], w=[r_tA])
            P.tt(DVE, Q2[:, :, :, 0, :], f4(2), v4(tA[:]), ALU.mult, r=[r_fm, r_tA], w=[r_Q2])
            P.act(tA[:], cum[:], AF.Exp, r=[r_cum], w=[r_tA], scale=-1.0)
            P.tt(DVE, KB[:, :, :, 0, :], f4(3), v4(tA[:]), ALU.mult, r=[r_fm, r_tA], w=[r_KB])
            P.tt(DVE, KB[:, :, :, 1, :], f4(1), v4(tA[:]), ALU.mult, r=[r_fm, r_tA], w=[r_KB])
            cumC = v4(cum[:])[:, :, :, CH - 1:CH]
            P.tt(DVE, v4(tB[:]), cumC.to_broadcast([64, H3, NCH, CH]), v4(cum[:]), ALU.subtract, r=[r_cum], w=[r_tB])
            P.act(tA[:], tB[:], AF.Exp, r=[r_tB], w=[r_tA])
            P.tt(DVE, KBb[:, :, :, 0, :], f4(3), v4(tA[:]), ALU.mult, r=[r_fm, r_tA], w=[r_KBb])
            P.tt(DVE, KBb[:, :, :, 1, :], f4(1), v4(tA[:]), ALU.mult, r=[r_fm, r_tA], w=[r_KBb])
            P.act(PC[:].rearrange("p h (n o) -> p h n o", o=1), cumC, AF.Exp, r=[r_cum], w=[r_PC])

        def flat(t4, h, n):
            return t4[:, h, n].rearrange("p a c -> p (a c)")

        def prep(n, i):
            for h in range(H3):
                P.mm(pA[:, h * 128:(h + 1) * 128], flat(KB, h, n), flat(Q2, h, n), True, True, r=[r_KB, r_Q2], w=[rA])
                P.mm(pB[0:64, h * 64:(h + 1) * 64], Q2[:, h, n, 0, :], KB[:, h, n, 0, :], True, True, r=[r_KB, r_Q2], w=[rB])
            pA3 = pA[:, 0:H3 * 128].rearrange("p (h c) -> p h c", h=H3)
            pB3 = pB[0:64, 0:H3 * 64].rearrange("p (h c) -> p h c", h=H3)
            P.tt(DVE, Mm[i][:], pA3, mM3[:], ALU.mult, r=[rA, r_k3], w=[r_Mm[i]])
            P.tt(DVE, LTa[i][:, :, 0:64], pA3[0:64, :, 0:64], mM3[0:64, :, 0:64], ALU.mult, r=[rA, r_k3], w=[r_LTa[i]])
            P.cp(DVE, LTa[i][:, :, 64:128], id3[:], r=[r_k3], w=[r_LTa[i]])
            P.tt(DVE, Na[i][:], pB3, mN3[:], ALU.mult, r=[rB, r_k3], w=[r_Na[i]])
            yield
            bufs = [(LTa[i], r_LTa[i], Na[i], r_Na[i]), (LTb[i], r_LTb[i], Nb[i], r_Nb[i])]
            for j in range(6):
                (lt, rlt, nn_, rnn) = bufs[j % 2]
                (lt2, rlt2, nn2, rnn2) = bufs[(j + 1) % 2]
                for h in range(H3):
                    P.mm(pA[0:64, h * 128:(h + 1) * 128], nn_[:, h, :], lt[:, h, :], True, True, r=[rnn, rlt], w=[rA])
                    if j < 5:
                        P.mm(pB[0:64, h * 64:(h + 1) * 64], lt[:, h, 0:64], nn_[:, h, :], True, True, r=[rnn, rlt], w=[rB])
                pA3h = pA[0:64, 0:H3 * 128].rearrange("p (h c) -> p h c", h=H3)
                if j < 5:
                    P.cp(DVE, lt2[:, :, 0:64], pA3h[:, :, 0:64], r=[rA], w=[rlt2])
                    P.cp(DVE, nn2[:], pB3, r=[rB], w=[rnn2])
                P.tt(DVE, lt2[:, :, 64:128], pA3h[:, :, 64:128], lt[:, :, 64:128], ALU.add, r=[rA, rlt], w=[rlt2])
                yield
            ltf, rltf = bufs[0][0], bufs[0][1]
            P.cp(DVE, Tt[i][:], ltf[:, :, 64:128], r=[rltf], w=[r_Tt[i]])
            for h in range(H3):
                P.mm(pB[0:64, h * 64:(h + 1) * 64], Mm[i][64:128, h, 0:64], UV[64:128, n, h, :], True, True, r=[r_Mm[i], r_V], w=[rB])
            P.cp(DVE, W2[i][:], pB3, r=[rB], w=[r_W2[i]])
            ptb = pA[:, 0:H3 * 32].bitcast(BF16)
            for h in range(H3):
                P.op(PE, lambda e, h=h: e.transpose(ptb[:, h * 64:(h + 1) * 64], flat(KBb, h, n), C.ident[0:64, 0:64]),
                     r=[r_KBb, C.r_cbf], w=[rA])
            P.cp(DVE, KT[i][:], ptb.rearrange("p (h c) -> p h c", h=H3), r=[rA], w=[r_KT[i]])
            yield

        def crit(n, i):
            cX = pC[0:64, 0:H3 * 64]
            cU = pC[0:64, H3 * 64:2 * H3 * 64]
            cX3 = cX.rearrange("p (h c) -> p h c", h=H3)
            cU3 = cU.rearrange("p (h c) -> p h c", h=H3)
            for h in range(H3):
                P.mm(cX[:, h * 64:(h + 1) * 64], Q2[:, h, n, 0, :], Sbf[:, h, :], True, True, r=[r_Q2, rS], w=[rC])
            P.tt(DVE, Xs[i][:], cX3, W2[i][:], ALU.add, r=[rC, r_W2[i]], w=[r_Xs[i]])
            for h in range(H3):
                P.mm(cU[:, h * 64:(h + 1) * 64], Tt[i][:, h, :], Xs[i][:, h, :], True, True, r=[r_Tt[i], r_Xs[i]], w=[rC])
            P.cp(DVE, UV[0:64, n, :, :], cU3, r=[rC], w=[r_U])
            yield
            for h in range(H3):
                P.mm(cX[:, h * 64:(h + 1) * 64], Sbf[:, h, :], Q2[:, h, n, 1, :], True, False, r=[rS, r_Q2], w=[rC])
                P.mm(cX[:, h * 64:(h + 1) * 64], UV[:, n, h, :], Mm[i][:, h, 64:128], False, True, r=[r_U, r_V, r_Mm[i]], w=[rC])
                P.mm(cU[:, h * 64:(h + 1) * 64], KT[i][:, h, :], UV[:, n, h, :], True, True, r=[r_KT[i], r_U, r_V], w=[rC])
            P.cp(ACT, Y[:, :, n * CH:(n + 1) * CH], cX3, r=[rC], w=[r_Y])
            P.tt(DVE, S32[:], S32[:], PC[:, :, n:n + 1].to_broadcast([64, H3, 64]), ALU.mult, r=[r_PC, rS], w=[rS])
            P.tt(DVE, S32[:], S32[:], cU3, ALU.add, r=[rC, rS], w=[rS])
            P.cp(DVE, Sbf[:], S32[:], r=[rS], w=[rS])
            yield

        for sc in range(nsc):
            load(sc)
            yield
            ovl = os.environ.get("RWOVL", "0") == "1"
            if not ovl:
                for n in range(NCH):
                    yield from prep(n, n % 2)
                    yield from crit(n, n % 2)
            else:
                yield from prep(0, 0)
                for n in range(NCH):
                    gc = crit(n, n % 2)
                    gp = prep(n + 1, (n + 1) % 2) if n + 1 < NCH else None
                    while gc is not None or gp is not None:
                        if gp is not None:
                            try:
                                next(gp)
                            except StopIteration:
                                gp = None
                        if gc is not None:
                            try:
                                next(gc)
                            except StopIteration:
                                gc = None
                        yield
            for h in range(H3):
                P.dma(SP, y_o[h, :, sc * SCL:(sc + 1) * SCL], Y[:, h, :], r=[r_Y], w=[r_out], own="r")

    sbq, rwq = None, None
    if do_sb:
        msb = C.sb("msb", [128, 12, 512], BF16)
        r_m = C.res("msb")
        P.dma(SP, msb[:, 0:4, :], mF_d, w=[r_m])
        P.dma(SP, msb[:, 4:12, :], mH_d, w=[r_m])
        sbq = sb_pipeline([(qF_d, kF_d, vF_d, msb[:, 0:4, :], T // 128, 4, oF_o),
                           (qH_d, kH_d, vH_d, msb[:, 4:12, :], T // 256, 8, oH_o)])
    if do_rw:
        rwq = rw_chain()
    gq = None
    gates_started = False
    while sbq is not None or rwq is not None or gq is not None:
        for _ in range(int(os.environ.get("SBR", "1"))):
            if sbq is not None:
                try:
                    next(sbq)
                except StopIteration:
                    sbq = None
        if sbq is None and do_sb and do_gates and not gates_started:
            gates_started = True
            gq = gates_chain()
        if rwq is not None:
            try:
                next(rwq)
            except StopIteration:
                rwq = None
        if gq is not None:
            try:
                next(gq)
            except StopIteration:
                gq = None
    return C.finish([r_out])


def _consts2():
    c = np.zeros((128, 512), np.float32)
    j = np.arange(128)[:, None]
    s_ = np.arange(128)[None, :]
    c[:, 0:128] = (j >= s_)
    c[:, 128:256] = (j < s_)
    rs = np.arange(128)[:, None] % 64
    ct = np.arange(128)[None, :]
    m = np.zeros((128, 128), np.float32)
    m[:, 0:64] = (rs < ct[:, 0:64])
    m[:, 64:128] = (rs <= (ct[:, 64:128] - 64))
    c[:, 256:384] = m
    tt_ = np.arange(64)[:, None]
    ss_ = np.arange(64)[None, :]
    c[0:64, 384:448] = (ss_ < tt_)
    return c


def _sb_masks(qblocks, kblocks):
    m = np.zeros((128, len(kblocks), len(qblocks) * 128), np.float32)
    key = np.arange(128)[:, None]
    qp = np.arange(128)[None, :]
    for j, kb in enumerate(kblocks):
        for i, qb in enumerate(qblocks):
            m[:, j, i * 128:(i + 1) * 128] = (kb * 128 + key) < (qb * 128 + qp)
    return m.astype(NPBF)


def sb_units(c):
    if c % 2 == 0:
        return 3 * c // 2, 3 * c // 2 + 1, 0
    return (3 * c + 1) // 2, (3 * c - 1) // 2, 1


def prep_stage_b(resA, inp=None):
    wg = _tile_lhsT(np.asarray(inp["w_in"][0][:, 10976:23264])) if inp is not None else None
    qT = np.concatenate([np.asarray(r["qT"]) for r in resA], axis=2)
    kT = np.concatenate([np.asarray(r["kT"]) for r in resA], axis=2)
    v = np.concatenate([np.asarray(r["v"]) for r in resA], axis=0)
    rwf = np.concatenate([np.asarray(r["rwf"]) for r in resA], axis=2)
    ld = np.concatenate([np.asarray(r["ld"]) for r in resA], axis=1)
    rwv = np.concatenate([np.asarray(r["rwv"]) for r in resA], axis=0)
    cst, cst2 = _consts(), _consts2()
    mF = _sb_masks([0, 1, 2, 3], [0, 1, 2, 3])
    maps = []
    for c in range(NCORES):
        hf, hh, par = sb_units(c)
        m = {"cst": cst, "cst2": cst2, "mF": mF}
        m["mH"] = _sb_masks([par, 2 + par, 4 + par, 6 + par], list(range(8)))
        m["qF"] = np.ascontiguousarray(qT[hf])
        m["kF"] = np.ascontiguousarray(kT[hf])
        m["vF"] = np.ascontiguousarray(v[:, hf * 128:(hf + 1) * 128])
        qh = qT[hh].reshape(128, T // 256, 2, 128)[:, :, par, :].reshape(128, T // 2)
        m["qH"] = np.ascontiguousarray(qh)
        m["kH"] = np.ascontiguousarray(kT[hh])
        m["vH"] = np.ascontiguousarray(v[:, hh * 128:(hh + 1) * 128])
        hs = [3 * c + i for i in range(NHB)]
        m["rwf"] = np.ascontiguousarray(np.stack([rwf[:, h * 64:(h + 1) * 64, :].transpose(1, 0, 2) for h in hs]))
        m["ld"] = np.ascontiguousarray(np.stack([ld[h * 64:(h + 1) * 64] for h in hs]))
        m["rwv"] = np.ascontiguousarray(np.stack([rwv[:, h * 64:(h + 1) * 64] for h in hs]))
        if wg is not None:
            m["wg"] = wg
            m["xn"] = np.asarray(resA[c]["xn"])
        maps.append(m)
    return maps


NPC = 32 + 32 + 32 + 2 + 2 + 12 + 12 + 1 + 1
NFT = DFF // 128
HT = 512


def build_stage_c(parts=("mem", "rw", "merge", "out", "ffn")):
    C = Ctx("C")
    P, nc = C.P, C.nc
    xT = C.din("xT", [128, KC, TL], F32)
    memT = C.din("memT", [128, KC, NMEM], F32)
    prm_d = C.din("prm", [128, NPC], F32)
    cst_d = C.din("cst", [128, 384], F32)
    wkvk_d = C.din("wkvk", [8, 128, KC * 128], F32)
    wkvv_d = C.din("wkvv", [2, 128, KC * 512], F32)
    wmq_d = C.din("wmq", [8, 128, KC * 128], F32)
    wu_d = C.din("wu", [32, 128, KC * 128], F32)
    wo_d = C.din("wo", [32, 128, KC * 128], F32)
    wfg_d = C.din("wfg", [NFT, 128, KC * 128], F32)
    wfu_d = C.din("wfu", [NFT, 128, KC * 128], F32)
    wfd_d = C.din("wfd", [32, 2, 128, 43 * 128], F32)
    osb_d = C.din("osb", [128, 12, TL], BF16)
    y_d = C.din("y", [RWW, TL], F32)
    g_d = C.din("g", [RWW, TL], F32)
    bon_d = C.din("bon", [RWW, TL], F32)
    out_o = C.dout("outT", [D, TL], F32)
    sg_s = C.din("sg", [96, 128, TL], BF16)
    xn_d = C.din("xn", [128, KC, TL], BF16)
    h1_s = C.dint("h1_s", [32, 128, TL], F32)
    r_out, r_sg, r_h1 = C.res("outs"), C.res("sg"), C.res("h1")

    load_consts(C, cst_d)
    prm = C.sb("prm", [128, NPC], F32)
    r_prm = C.res("prm")
    P.dma(SP, prm[:], prm_d, w=[r_prm])
    C.eps_ap = prm[:, NPC - 1:NPC]
    gneps_ap = prm[:, NPC - 2:NPC - 1]
    C.r_eps = r_prm
    g_attn, g_mem, g_ffn = prm[:, 0:32], prm[:, 32:64], prm[:, 64:96]
    o_mq, o_mk, o_lng, o_lnb = 96, 98, 100, 112
    one_col = C.c32[:, 0:1]

    NW = 3
    wb = [C.sb(f"wb{i}", [128, KC, 128], BF16) for i in range(NW)]
    rwb = [C.res("wb") for _ in range(NW)]
    wcount = [0]

    def load_w(src):
        i = wcount[0] % NW
        wcount[0] += 1
        P.dma(POOL, wb[i][:], src.rearrange("p (c m) -> p c m", m=128), w=[rwb[i]])
        return wb[i], rwb[i]

    halves = [(0, HT), (HT, HT)]
    omem = C.sb("omem", [128, 8, TL], BF16)
    r_omem = C.res("omem")

    with C.phase():
        xn = C.sb("xn", [128, KC, TL], BF16)
        r_xn = C.res("xn")
        P.dma(SP, xn[:], xn_d, w=[r_xn])

        def gemm_fm(wt, rw, banks, rhs3, r_rhs, segs):
            for b, (s, n) in zip(banks, segs):
                for c in range(KC):
                    P.mm(C.ps[b][:, 0:n], wt[:, c, :], rhs3[:, c, s:s + n], c == 0, c == KC - 1, r=[rw, r_rhs], w=[C.rps[b]])

        if "mem" in parts:
            with C.phase():
                mn = C.sb("mn", [128, KC, NMEM], BF16)
                r_mn = C.res("mn")
                with C.phase():
                    rmsnorm_T(C, memT, NMEM, g_mem, r_prm, mn, r_mn, "nm")
                mk32 = C.sb("mk32", [128, 8, NMEM], F32)
                r_mk32 = C.res("mk32")
                mksq = C.sb("mksq", [128, 8, NMEM], BF16)
                r_mksq = C.res("mksq")
                mkn = C.sb("mkn", [128, 8, NMEM], BF16)
                r_mkn = C.res("mkn")
                mrs = C.sb("mrs", [128, NMEM], F32)
                r_mrs = C.res("mrs")
                for j in range(8):
                    wt, rw = load_w(wkvk_d[j])
                    b = j % 2
                    gemm_fm(wt, rw, [b], mn, r_mn, [(0, NMEM)])
                    P.cp(ACT, mk32[:, j, :], C.ps[b][:, 0:NMEM], r=[C.rps[b]], w=[r_mk32])
                    P.act(mksq[:, j, :], mk32[:, j, :], AF.Square, r=[r_mk32], w=[r_mksq])
                for h in range(4):
                    for dt in range(2):
                        P.mm(C.ps[2][:, 0:NMEM], C.ones, mksq[:, 2 * h + dt, :], dt == 0, dt == 1, r=[r_mksq, C.r_cbf], w=[C.rps[2]])
                    P.act(mrs[:], C.ps[2][:, 0:NMEM], AF.Sqrt, r=[C.rps[2], r_prm], w=[r_mrs], bias=C.eps_ap, scale=1.0 / 256)
                    P.op(DVE, lambda e: e.reciprocal(mrs[:], mrs[:]), r=[r_mrs], w=[r_mrs])
                    for dt in range(2):
                        P.stt(DVE, mkn[:, 2 * h + dt, :], mk32[:, 2 * h + dt, :], prm[:, o_mk + dt:o_mk + dt + 1], mrs[:], ALU.mult, ALU.mult,
                              r=[r_mk32, r_mrs, r_prm], w=[r_mkn])
                mv = C.sb("mv", [128, 2, MEMW], BF16)
                r_mv = C.res("mv")
                with C.phase():
                    wvb = C.sb("wvb", [128, KC, 512], BF16)
                    rwvb = C.res("wvb")
                    for gc in range(2):
                        P.dma(POOL, wvb[:], wkvv_d[gc].rearrange("p (c m) -> p c m", m=512), w=[rwvb])
                        for mt in range(2):
                            b = 3 + mt
                            for c in range(KC):
                                P.mm(C.ps[b][:, :], mn[:, c, mt * 128:(mt + 1) * 128], wvb[:, c, :], c == 0, c == KC - 1, r=[rwvb, r_mn], w=[C.rps[b]])
                            P.cp(ACT, mv[:, mt, gc * 512:(gc + 1) * 512], C.ps[b][:, :], r=[C.rps[b]], w=[r_mv])
                mq32 = C.sb("mq32", [128, 2, TL], F32)
                r_mq32 = C.res("mq32")
                mqsq = C.sb("mqsq", [128, 2, TL], BF16)
                r_mqsq = C.res("mqsq")
                mqn = C.sb("mqn", [128, 2, TL], BF16)
                r_mqn = C.res("mqn")
                qrs = C.sb("qrs", [128, TL], F32)
                r_qrs = C.res("qrs")
                pT = C.sb("pT", [128, 2, HT], BF16)
                r_pT = C.res("pT")
                rden = C.sb("rden", [128, HT], F32)
                r_rden = C.res("rden")
                for h in range(4):
                    for dt in range(2):
                        wt, rw = load_w(wmq_d[2 * h + dt])
                        gemm_fm(wt, rw, [0, 1], xn, r_xn, halves)
                        for hh, (s, n) in enumerate(halves):
                            P.cp(ACT, mq32[:, dt, s:s + n], C.ps[hh][:, :], r=[C.rps[hh]], w=[r_mq32])
                        P.act(mqsq[:, dt, :], mq32[:, dt, :], AF.Square, r=[r_mq32], w=[r_mqsq])
                    for hh, (s, n) in enumerate(halves):
                        for dt in range(2):
                            P.mm(C.ps[2][:, :], C.ones, mqsq[:, dt, s:s + n], dt == 0, dt == 1, r=[r_mqsq, C.r_cbf], w=[C.rps[2]])
                        P.act(qrs[:, s:s + n], C.ps[2][:, :], AF.Sqrt, r=[C.rps[2], r_prm], w=[r_qrs], bias=C.eps_ap, scale=1.0 / 256)
                    P.op(DVE, lambda e: e.reciprocal(qrs[:], qrs[:]), r=[r_qrs], w=[r_qrs])
                    for dt in range(2):
                        P.stt(DVE, mqn[:, dt, :], mq32[:, dt, :], prm[:, o_mq + dt:o_mq + dt + 1], qrs[:], ALU.mult, ALU.mult,
                              r=[r_mq32, r_qrs, r_prm], w=[r_mqn])
                    for hh, (s, n) in enumerate(halves):
                        for mt in range(2):
                            for dt in range(2):
                                P.mm(C.ps[3][:, :], mkn[:, 2 * h + dt, mt * 128:(mt + 1) * 128], mqn[:, dt, s:s + n], dt == 0, dt == 1,
                                     r=[r_mkn, r_mqn], w=[C.rps[3]])
                            P.act(pT[:, mt, :], C.ps[3][:, :], AF.Exp, r=[C.rps[3]], w=[r_pT], scale=1.0 / 16)
                        for mt in range(2):
                            P.mm(C.ps[4][:, :], C.ones, pT[:, mt, :], mt == 0, mt == 1, r=[r_pT, C.r_cbf], w=[C.rps[4]])
                        P.op(DVE, lambda e: e.reciprocal(rden[:], C.ps[4][:, :]), r=[C.rps[4]], w=[r_rden])
                        for dt in range(2):
                            b = 5 + dt
                            for mt in range(2):
                                P.mm(C.ps[b][:, :], mv[:, mt, (2 * h + dt) * 128:(2 * h + dt + 1) * 128], pT[:, mt, :], mt == 0, mt == 1,
                                     r=[r_mv, r_pT], w=[C.rps[b]])
                            P.tt(DVE, omem[:, 2 * h + dt, s:s + n], C.ps[b][:, :], rden[:], ALU.mult, r=[C.rps[b], r_rden], w=[r_omem])

        if "gates" in parts:
            with C.phase():
                sgb = [C.sb(f"sgb{i}", [128, TL], BF16) for i in range(2)]
                rsgb = [C.res("sgb") for _ in range(2)]
                for j in range(96):
                    wt, rw = load_w(wg_d[j])
                    i = j % 2
                    mb = [0, 1] if i == 0 else [2, 3]
                    gemm_fm(wt, rw, mb, xn, r_xn, halves)
                    for hh, (s, n) in enumerate(halves):
                        P.act(sgb[i][:, s:s + n], C.ps[mb[hh]][:, :], AF.Sigmoid, r=[C.rps[mb[hh]]], w=[rsgb[i]])
                    P.dma(SP, sg_s[j], sgb[i][:], r=[rsgb[i]], w=[r_sg], own="r")

    with C.phase():
        osb = C.sb("osb", [128, 12, TL], BF16)
        r_osb = C.res("osb")
        orw = C.sb("orw", [128, 12, TL], BF16)
        r_orw = C.res("orw")
        mrg = C.sb("mrg", [128, KC, TL], BF16)
        r_mrg = C.res("mrg")
        P.dma(SP, osb[:], osb_d, w=[r_osb])
        if "rw" in parts:
            with C.phase():
                def f32b(n):
                    return C.sb(n, [128, TL], F32), C.res(n)
                yt, r_yt = f32b("yt")
                gt, r_gt = f32b("gt")
                bt, r_bt = f32b("bt")
                dd, r_dd = f32b("dd")
                rs, r_rs = f32b("rs")
                ybf = C.sb("ybf", [128, TL], BF16)
                r_ybf = C.res("ybf")
                for ct in range(12):
                    cs = slice(ct * 128, (ct + 1) * 128)
                    P.dma(SP, yt[:], y_d[cs, :], w=[r_yt])
                    P.dma(SP, gt[:], g_d[cs, :], w=[r_gt])
                    P.dma(SP, bt[:], bon_d[cs, :], w=[r_bt])
                    P.cp(ACT, ybf[:], yt[:], r=[r_yt], w=[r_ybf])
                    for hh, (s, n) in enumerate(halves):
                        P.mm(C.ps[hh][:, :], C.bones, ybf[:, s:s + n], True, True, r=[r_ybf, C.r_cbf], w=[C.rps[hh]])
                        P.stt(DVE, dd[:, s:s + n], C.ps[hh][:, :], -1.0 / 64, yt[:, s:s + n], ALU.mult, ALU.add, r=[C.rps[hh], r_yt], w=[r_dd])
                    P.act(ybf[:], dd[:], AF.Square, r=[r_dd], w=[r_ybf])
                    for hh, (s, n) in enumerate(halves):
                        P.mm(C.ps[2 + hh][:, :], C.bones, ybf[:, s:s + n], True, True, r=[r_ybf, C.r_cbf], w=[C.rps[2 + hh]])
                        P.act(rs[:, s:s + n], C.ps[2 + hh][:, :], AF.Sqrt, r=[C.rps[2 + hh], r_prm], w=[r_rs], bias=gneps_ap, scale=1.0 / 64)
                    P.op(DVE, lambda e: e.reciprocal(rs[:], rs[:]), r=[r_rs], w=[r_rs])
                    P.tt(DVE, dd[:], dd[:], rs[:], ALU.mult, r=[r_dd, r_rs], w=[r_dd])
                    P.ts(DVE, dd[:], dd[:], prm[:, o_lng + ct:o_lng + ct + 1], prm[:, o_lnb + ct:o_lnb + ct + 1], ALU.mult, ALU.add,
                         r=[r_dd, r_prm], w=[r_dd])
                    P.tt(POOL, dd[:], dd[:], bt[:], ALU.add, r=[r_dd, r_bt], w=[r_dd])
                    P.tt(DVE, orw[:, ct, :], dd[:], gt[:], ALU.mult, r=[r_dd, r_gt], w=[r_orw])

        if "merge" in parts:
            with C.phase():
                sgt = [C.sb(f"sgt{i}", [128, 3, TL], BF16) for i in range(2)]
                rsgt = [C.res("sgt") for _ in range(2)]
                m1 = C.sb("m1", [128, TL], F32)
                r_m1 = C.res("m1")
                m2 = C.sb("m2", [128, TL], F32)
                r_m2 = C.res("m2")
                srcs = [(osb, r_osb, 0, 12), (orw, r_orw, 12, 12), (omem, r_omem, 24, 8)]
                for j in range(32):
                    i = j % 2
                    wt, rw = load_w(wu_d[j])
                    for br in range(3):
                        P.dma(SP, sgt[i][:, br, :], sg_s[br * 32 + j], r=[r_sg], w=[rsgt[i]])
                    for br, (src, r_src, c0, ncnk) in enumerate(srcs):
                        for hh, (s, n) in enumerate(halves):
                            b = 2 * br + hh
                            for c in range(ncnk):
                                P.mm(C.ps[b][:, :], wt[:, c0 + c, :], src[:, c, s:s + n], c == 0, c == ncnk - 1, r=[rw, r_src], w=[C.rps[b]])
                    for hh, (s, n) in enumerate(halves):
                        P.tt(DVE, m1[:, s:s + n], C.ps[0 + hh][:, :], sgt[i][:, 0, s:s + n], ALU.mult, r=[C.rps[hh], rsgt[i]], w=[r_m1])
                        P.tt(DVE, m2[:, s:s + n], C.ps[2 + hh][:, :], sgt[i][:, 1, s:s + n], ALU.mult, r=[C.rps[2 + hh], rsgt[i]], w=[r_m2])
                    P.tt(POOL, m1[:], m1[:], m2[:], ALU.add, r=[r_m1, r_m2], w=[r_m1])
                    for hh, (s, n) in enumerate(halves):
                        P.tt(DVE, m2[:, s:s + n], C.ps[4 + hh][:, :], sgt[i][:, 2, s:s + n], ALU.mult, r=[C.rps[4 + hh], rsgt[i], r_m2], w=[r_m2])
                    P.tt(POOL, mrg[:, j, :], m1[:], m2[:], ALU.add, r=[r_m1, r_m2], w=[r_mrg])

        ssq_banks = [6, 7]
        if "out" in parts:
            with C.phase():
                xs = [C.sb(f"xs{i}", [128, TL], F32) for i in range(2)]
                rxs = [C.res("xs") for _ in range(2)]
                hsq = [C.sb(f"hsq{i}", [128, TL], BF16) for i in range(2)]
                rhsq = [C.res("hsq") for _ in range(2)]
                for j in range(32):
                    i = j % 2
                    wt, rw = load_w(wo_d[j])
                    P.dma(SP, xs[i][:], xT[:, j, :], w=[rxs[i]])
                    mb = [0, 1] if i == 0 else [2, 3]
                    for hh, (s, n) in enumerate(halves):
                        for c in range(KC):
                            P.mm(C.ps[mb[hh]][:, :], wt[:, c, :], mrg[:, c, s:s + n], c == 0, c == KC - 1, r=[rw, r_mrg], w=[C.rps[mb[hh]]])
                        P.tt(DVE, xs[i][:, s:s + n], C.ps[mb[hh]][:, :], xs[i][:, s:s + n], ALU.add, r=[C.rps[mb[hh]], rxs[i]], w=[rxs[i]])
                    P.dma(SP, h1_s[j], xs[i][:], r=[rxs[i]], w=[r_h1], own="r")
                    P.act(hsq[i][:], xs[i][:], AF.Square, r=[rxs[i]], w=[rhsq[i]])
                    for hh, (s, n) in enumerate(halves):
                        P.mm(C.ps[ssq_banks[hh]][:, :], C.ones, hsq[i][:, s:s + n], j == 0, j == 31, r=[rhsq[i], C.r_cbf], w=[C.rps[ssq_banks[hh]]])

    if "ffn" in parts:
        with C.phase():
            rstd = C.sb("frstd", [128, TL], F32)
            r_rstd = C.res("frstd")
            for hh, (s, n) in enumerate(halves):
                P.act(rstd[:, s:s + n], C.ps[ssq_banks[hh]][:, :], AF.Sqrt, r=[C.rps[ssq_banks[hh]], r_prm], w=[r_rstd], bias=C.eps_ap, scale=1.0 / D)
            P.op(DVE, lambda e: e.reciprocal(rstd[:], rstd[:]), r=[r_rstd], w=[r_rstd])
            hn = C.sb("hn", [128, KC, HT], BF16)
            r_hn = C.res("hn")
            actT = C.sb("actT", [128, NFT, HT], BF16)
            r_act = C.res("actT")
            hld = [C.sb(f"hld{i}", [128, HT], F32) for i in range(2)]
            rhld = [C.res("hld") for _ in range(2)]
            sgl = [C.sb(f"sgl{i}", [128, HT], F32) for i in range(2)]
            rsgl = [C.res("sgl") for _ in range(2)]
            wdb = [C.sb(f"wdb{i}", [128, 43, 128], BF16) for i in range(3)]
            rwdb = [C.res("wdb") for _ in range(3)]
            wdc = [0]
            for hh, (s, n) in enumerate(halves):
                for j in range(32):
                    i = j % 2
                    P.dma(SP, hld[i][:], h1_s[j][:, s:s + n], r=[r_h1], w=[rhld[i]])
                    P.stt(DVE, hn[:, j, :], hld[i][:], g_ffn[:, j:j + 1], rstd[:, s:s + n], ALU.mult, ALU.mult, r=[rhld[i], r_rstd, r_prm], w=[r_hn])
                for f in range(NFT):
                    i = f % 2
                    gb, ub = (0, 1) if i == 0 else (2, 3)
                    wt, rw = load_w(wfg_d[f])
                    for c in range(KC):
                        P.mm(C.ps[gb][:, :], wt[:, c, :], hn[:, c, :], c == 0, c == KC - 1, r=[rw, r_hn], w=[C.rps[gb]])
                    wt, rw = load_w(wfu_d[f])
                    for c in range(KC):
                        P.mm(C.ps[ub][:, :], wt[:, c, :], hn[:, c, :], c == 0, c == KC - 1, r=[rw, r_hn], w=[C.rps[ub]])
                    P.act(sgl[i][:], C.ps[gb][:, :], AF.Silu, r=[C.rps[gb]], w=[rsgl[i]])
                    P.tt(DVE, actT[:, f, :], sgl[i][:], C.ps[ub][:, :], ALU.mult, r=[rsgl[i], C.rps[ub]], w=[r_act])
                for j in range(32):
                    i = j % 2
                    b = 4 + i
                    for fg in range(2):
                        k = wdc[0] % 3
                        wdc[0] += 1
                        P.dma(POOL, wdb[k][:], wfd_d[j, fg].rearrange("p (c m) -> p c m", m=128), w=[rwdb[k]])
                        for c in range(43):
                            P.mm(C.ps[b][:, :], wdb[k][:, c, :], actT[:, fg * 43 + c, :], fg == 0 and c == 0, fg == 1 and c == 42,
                                 r=[rwdb[k], r_act], w=[C.rps[b]])
                    P.dma(SP, hld[i][:], h1_s[j][:, s:s + n], r=[r_h1], w=[rhld[i]])
                    P.tt(DVE, hld[i][:], C.ps[b][:, :], hld[i][:], ALU.add, r=[C.rps[b], rhld[i]], w=[rhld[i]])
                    P.dma(SP, out_o[j * 128:(j + 1) * 128, s:s + n], hld[i][:], r=[rhld[i]], w=[r_out], own="r")
    return C.finish([r_out])


def prep_stage_c(inp, resA, resB):
    x = inp["x"][0]
    w_in = inp["w_in"][0]
    shared = {"cst": _consts()}
    kv = inp["mem_w_kv"][0]
    shared["wkvk"] = _tile_lhsT(kv[:, 0:1024])
    shared["wkvv"] = _tile_lhsT(kv[:, 1024:2048], 512)
    shared["wmq"] = _tile_lhsT(w_in[:, 9952:10976])
    wu = np.concatenate([inp["w_sb_o"][0], inp["w_rw_o"][0], inp["w_mem_o"][0]], axis=0)
    shared["wu"] = _tile_lhsT(wu)
    shared["wo"] = _tile_lhsT(inp["w_out"][0])
    shared["wfg"] = _tile_lhsT(inp["w_gate"][0])
    shared["wfu"] = _tile_lhsT(inp["w_up"][0])
    wd = inp["w_down"][0].reshape(2, 43, 128, 32, 128).transpose(3, 0, 2, 1, 4)
    shared["wfd"] = np.ascontiguousarray(wd).reshape(32, 2, 128, 43 * 128)
    shared["memT"] = np.ascontiguousarray(inp["mem"][0].T.reshape(KC, 128, NMEM).transpose(1, 0, 2))
    prm = np.zeros((128, NPC), np.float32)
    prm[:, 0:32] = _cols(inp["attn_norm_g"][0], 32)
    prm[:, 32:64] = _cols(inp["mem_norm_g"][0], 32)
    prm[:, 64:96] = _cols(inp["ffn_norm_g"][0], 32)
    prm[:, 96:98] = _cols(inp["mem_q_norm_g"][0], 2)
    prm[:, 98:100] = _cols(inp["mem_k_norm_g"][0], 2)
    prm[:, 100:112] = _cols(inp["rw_ln_g"][0], 12)
    prm[:, 112:124] = _cols(inp["rw_ln_b"][0], 12)
    prm[:, NPC - 2] = GN_EPS
    prm[:, NPC - 1] = EPS
    shared["prm"] = prm
    osb = np.zeros((12, 128, T), NPBF)
    y = np.zeros((RWW, T), np.float32)
    for c in range(NCORES):
        hf, hh, par = sb_units(c)
        osb[hf] = np.asarray(resB[c]["oF"])
        oh = np.asarray(resB[c]["oH"]).reshape(128, T // 256, 128)
        osb[hh].reshape(128, T // 256, 2, 128)[:, :, par, :] = oh
        for i in range(NHB):
            h = 3 * c + i
            y[h * 64:(h + 1) * 64] = np.asarray(resB[c]["y"][i])
    maps = []
    for c in range(NCORES):
        ts_ = slice(c * TL, (c + 1) * TL)
        m = dict(shared)
        m["xT"] = np.ascontiguousarray(x[ts_].T.reshape(KC, 128, TL).transpose(1, 0, 2))
        m["osb"] = np.ascontiguousarray(osb[:, :, ts_].transpose(1, 0, 2))
        m["y"] = np.ascontiguousarray(y[:, ts_])
        m["g"] = np.asarray(resA[c]["g"])
        m["bon"] = np.asarray(resA[c]["bon"])
        m["xn"] = np.asarray(resA[c]["xn"])
        m["sg"] = np.asarray(resB[c]["sg"])
        maps.append(m)
    return maps


def kernel(**inp):
    inp = {k: np.asarray(v) for k, v in inp.items()}
    ids = list(range(NCORES))
    resA = run_bass_kernel_spmd(build_stage_a(), prep_stage_a(inp), core_ids=ids).results
    resB = run_bass_kernel_spmd(build_stage_b(), prep_stage_b(resA, inp), core_ids=ids).results
    resC = run_bass_kernel_spmd(build_stage_c(), prep_stage_c(inp, resA, resB), core_ids=ids).results
    out = np.concatenate([np.asarray(r["outT"]).T for r in resC], axis=0)
    return np.ascontiguousarray(out.reshape(1, T, D).astype(np.float32))
```

```python
import numpy as np
import ml_dtypes
import concourse.bass as bass
import concourse.mybir as mybir
from concourse.bass_utils import run_bass_kernel_spmd

F32 = mybir.dt.float32
BF16 = mybir.dt.bfloat16
AF = mybir.ActivationFunctionType
ALU = mybir.AluOpType
AX = mybir.AxisListType
NPBF = ml_dtypes.bfloat16

PE, DVE, ACT, POOL, SP = "tensor", "vector", "scalar", "gpsimd", "sync"
ENGS = [PE, DVE, ACT, POOL, SP]

NCORES = 8
D = 4096
T = 8192
TL = T // NCORES
KC = D // 128
SBW, RWW, MEMW = 1536, 1536, 1024
RW_SEG = 5344
IN_COLS = 23264
DFF = 11008
NMEM = 256
EPS = 1e-6
GN_EPS = 64e-5


class Res:
    __slots__ = ("name", "last_w", "reads", "dsem", "dcnt", "ws")

    def __init__(self, name):
        self.name = name
        self.ws = []
        self.last_w = None
        self.reads = []
        self.dsem = None
        self.dcnt = 0


class Prog:
    def __init__(self, nc):
        self.nc = nc
        self.q = {e: [] for e in ENGS}
        self.seq = {e: 0 for e in ENGS}
        self.waited = {e: {} for e in ENGS}
        self.esem = {}
        self.nsem = 0
        self.dma_owners = []

    def _newsem(self, name):
        self.nsem += 1
        return self.nc.alloc_semaphore(f"s{self.nsem}_{name}")

    def _esem(self, eng):
        if eng not in self.esem:
            self.esem[eng] = self._newsem("e_" + eng)
        return self.esem[eng]

    def _deps(self, eng, r, w, dma_sem=None):
        deps = []
        for b in r:
            if b.last_w is not None:
                deps.append(b.last_w)
            deps.extend(b.ws)
        for b in w:
            if b.last_w is not None:
                if not (dma_sem is not None and b.last_w[0] is dma_sem):
                    deps.append(b.last_w)
            deps.extend(b.reads)
        wd = self.waited[eng]
        best = {}
        for (sem, val, src) in deps:
            if src == eng and eng == PE:
                continue
            k = id(sem)
            if wd.get(k, 0) >= val:
                continue
            if k not in best or best[k][1] < val:
                best[k] = (sem, val)
        for k, (sem, val) in best.items():
            wd[k] = val
        return list(best.values())

    def op(self, eng, fn, r=(), w=()):
        waits = self._deps(eng, r, w)
        sem = self._esem(eng)
        self.seq[eng] += 1
        ev = (sem, self.seq[eng], eng)
        for b in r:
            b.reads.append(ev)
        for b in w:
            b.last_w = ev
            b.reads = []
        self.q[eng].append((waits, fn, (sem, 1)))
        return ev

    def dma(self, eng, out, in_, r=(), w=(), own="w", **kw):
        owner = w[0] if own == "w" else r[0]
        if owner.dsem is None:
            owner.dsem = self._newsem("d_" + owner.name)
            self.dma_owners.append(owner)
        sem = owner.dsem
        waits = self._deps(eng, r, w, dma_sem=sem)
        owner.dcnt += 16
        ev = (sem, owner.dcnt, "dma")
        for b in r:
            b.reads.append(ev)
        for b in w:
            if own == "r":
                b.ws = [x for x in b.ws if x[0] is not sem] + [ev]
            else:
                b.last_w = ev
                b.reads = []
        self.q[eng].append((waits, lambda e: e.dma_start(out=out, in_=in_, **kw), (sem, 16)))
        return ev

    def barrier(self):
        targets = [(sem, self.seq[e]) for e, sem in self.esem.items() if self.seq[e] > 0]
        targets += [(o.dsem, o.dcnt) for o in self.dma_owners if o.dcnt > 0]
        for eng in ENGS:
            wd = self.waited[eng]
            waits = []
            for sem, val in targets:
                if wd.get(id(sem), 0) >= val:
                    continue
                wd[id(sem)] = val
                waits.append((sem, val))
            if waits:
                self.q[eng].append((waits, None, None))

    def allgather(self, in_ap, out_ap, r, w):
        sem = self._newsem("cc")
        waits = self._deps(POOL, r, w)
        ev = (sem, 1, "cc")
        for b in r:
            b.reads.append(ev)
        for b in w:
            b.last_w = ev
            b.reads = []
        self.q[POOL].append((waits, lambda e: e.collective_compute(
            "AllGather", ALU.bypass, replica_groups=[list(range(NCORES))], ins=[in_ap.opt()], outs=[out_ap.opt()]), (sem, 1)))
        self.q[POOL].append(([(sem, 1)], None, None))
        self.waited[POOL][id(sem)] = 1
        return ev

    def wait_all(self, eng, ress):
        deps = []
        for b in ress:
            if b.last_w is not None:
                deps.append((b.last_w[0], b.last_w[1]))
            deps.extend((x[0], x[1]) for x in b.ws)
        self.q[eng].append((deps, None, None))

    def replay(self, block):
        for eng in ENGS:
            items = self.q[eng]
            if not items:
                continue

            def body(e, items=items):
                for waits, fn, inc in items:
                    for sem, val in waits:
                        e.wait_ge(sem, val)
                    if fn is not None:
                        fn(e).then_inc(inc[0], inc[1])

            getattr(block, eng)(body)

    def mm(self, out, lhsT, rhs, start, stop, r, w):
        return self.op(PE, lambda e: e.matmul(out, lhsT=lhsT, rhs=rhs, start=start, stop=stop), r=r, w=w)

    def act(self, out, in_, func, r, w, bias=None, scale=None, eng=ACT):
        kw = {}
        if bias is not None:
            kw["bias"] = bias
        if scale is not None:
            kw["scale"] = scale
        return self.op(eng, lambda e: e.activation(out, in_, func, **kw), r=r, w=w)

    def tt(self, eng, out, a, b, op, r, w):
        return self.op(eng, lambda e: e.tensor_tensor(out, a, b, op), r=r, w=w)

    def ts(self, eng, out, a, s1, s2, op0, op1, r, w):
        if op1 is None:
            return self.op(eng, lambda e: e.tensor_scalar(out, a, s1, None, op0), r=r, w=w)
        return self.op(eng, lambda e: e.tensor_scalar(out, a, s1, s2, op0, op1), r=r, w=w)

    def stt(self, eng, out, a, s, b, op0, op1, r, w):
        return self.op(eng, lambda e: e.scalar_tensor_tensor(out, a, s, b, op0, op1), r=r, w=w)

    def cp(self, eng, out, in_, r, w):
        if eng == ACT:
            return self.op(eng, lambda e: e.copy(out, in_), r=r, w=w)
        return self.op(eng, lambda e: e.tensor_copy(out, in_), r=r, w=w)


class Ctx:
    def __init__(self, name):
        self.nc = bass.Bass("TRN2", target_bir_lowering=False)
        self.P = Prog(self.nc)
        self.nres = 0
        self.stacks = []
        nc = self.nc
        self.ps = [nc.alloc_psum_tensor(f"psb{i}", [128, 512], F32) for i in range(8)]
        self.rps = [Res(f"ps{i}") for i in range(8)]

    def res(self, name):
        self.nres += 1
        return Res(f"{name}{self.nres}")

    def sb(self, name, shape, dt):
        self.nres += 1
        nm = f"s_{name}_{self.nres}"
        if self.stacks:
            return self.stacks[-1].enter_context(self.nc.sbuf_tensor(nm, list(shape), dt))
        return self.nc.alloc_sbuf_tensor(nm, list(shape), dt)

    def phase(self):
        return _Phase(self)

    def din(self, name, shape, dt):
        return self.nc.dram_tensor(name, list(shape), dt, kind="ExternalInput").ap()

    def dout(self, name, shape, dt):
        return self.nc.dram_tensor(name, list(shape), dt, kind="ExternalOutput").ap()

    def dint(self, name, shape, dt):
        return self.nc.dram_tensor(name, list(shape), dt).ap()

    def finish(self, out_ress):
        self.P.wait_all(SP, out_ress)
        self.P.q[SP].append(([(o.dsem, o.dcnt) for o in self.P.dma_owners if o.dcnt > 0], None, None))
        with self.nc.Block() as block:
            self.P.replay(block)
        return self.nc


class _Phase:
    def __init__(self, C):
        self.C = C

    def __enter__(self):
        import contextlib
        self.st = contextlib.ExitStack()
        self.C.stacks.append(self.st)
        return self

    def __exit__(self, *a):
        self.C.P.barrier()
        self.C.stacks.pop()
        self.st.close()
        return False


def load_consts(C, cst_ap):
    P = C.P
    c32 = C.sb("c32", [128, 384], F32)
    cbf = C.sb("cbf", [128, 384], BF16)
    r32, rbf = C.res("c32"), C.res("cbf")
    P.dma(SP, c32[:], cst_ap, w=[r32])
    P.cp(DVE, cbf[:], c32[:], r=[r32], w=[rbf])
    C.c32, C.cbf, C.r_c32, C.r_cbf = c32, cbf, r32, rbf
    C.ones = cbf[:, 0:128]
    C.bones = cbf[:, 128:256]
    C.ident = cbf[:, 256:384]
    C.ident32 = c32[:, 256:384]


def rmsnorm_T(C, xT_ap, ntok, g_ap, r_g, out_bf, r_out, name):
    P = C.P
    nseg = [(s, min(512, ntok - s)) for s in range(0, ntok, 512)]
    assert len(nseg) <= 3
    xs = [C.sb(f"{name}_x{i}", [128, ntok], F32) for i in range(2)]
    rx = [C.res(f"{name}_x") for _ in range(2)]
    sq = [C.sb(f"{name}_sq{i}", [128, ntok], BF16) for i in range(2)]
    rsq = [C.res(f"{name}_sq") for _ in range(2)]
    rstd = C.sb(f"{name}_rstd", [128, ntok], F32)
    r_rstd = C.res(f"{name}_rstd")
    banks = [5, 6, 7][:len(nseg)]
    for c in range(KC):
        i = c % 2
        P.dma(SP, xs[i][:], xT_ap[:, c, :], w=[rx[i]])
        P.act(sq[i][:], xs[i][:], AF.Square, r=[rx[i]], w=[rsq[i]])
        for b, (s, n) in zip(banks, nseg):
            P.mm(C.ps[b][:, 0:n], C.ones, sq[i][:, s:s + n], c == 0, c == KC - 1, r=[rsq[i], C.r_cbf], w=[C.rps[b]])
    for b, (s, n) in zip(banks, nseg):
        P.act(rstd[:, s:s + n], C.ps[b][:, 0:n], AF.Sqrt, r=[C.rps[b], C.r_eps], w=[r_rstd], bias=C.eps_ap, scale=1.0 / D)
    P.op(DVE, lambda e: e.reciprocal(rstd[:], rstd[:]), r=[r_rstd], w=[r_rstd])
    for c in range(KC):
        i = c % 2
        P.dma(SP, xs[i][:], xT_ap[:, c, :], w=[rx[i]])
        P.stt(DVE, out_bf[:, c, :], xs[i][:], g_ap[:, c:c + 1], rstd[:], ALU.mult, ALU.mult, r=[rx[i], r_rstd, r_g], w=[r_out])


NPA = 32 + 2 + 42 + 5 * 12 + 1


def build_stage_a():
    C = Ctx("A")
    P, nc = C.P, C.nc
    NT = TL + 1
    xT = C.din("xT", [128, KC, NT], F32)
    prm_d = C.din("prm", [128, NPA], F32)
    cst_d = C.din("cst", [128, 384], F32)
    wqk_d = C.din("wqk", [24, 128, KC * 128], F32)
    wv_d = C.din("wv", [3, 128, KC * 512], F32)
    wrw_d = C.din("wrw", [42, 128, KC * 128], F32)
    lora_d = C.din("lora", [128, 6 * 1536], F32)
    qT_o = C.dout("qT", [12, 128, TL], BF16)
    kT_o = C.dout("kT", [12, 128, TL], BF16)
    v_o = C.dout("v", [TL, SBW], BF16)
    rwf_o = C.dout("rwf", [4, RWW, TL], BF16)
    rwv_o = C.dout("rwv", [TL, RWW], BF16)
    ld_o = C.dout("ld", [RWW, TL], F32)
    g_o = C.dout("g", [RWW, TL], F32)
    bon_o = C.dout("bon", [RWW, TL], F32)
    xn_o = C.dout("xn", [128, KC, TL], BF16)
    r_out = C.res("outs")

    load_consts(C, cst_d)
    prm = C.sb("prm", [128, NPA], F32)
    r_prm = C.res("prm")
    P.dma(SP, prm[:], prm_d, w=[r_prm])
    C.eps_ap = prm[:, NPA - 1:NPA]
    C.r_eps = r_prm
    g_attn = prm[:, 0:32]
    o_mix, o_w0, o_a0, o_kk, o_ka, o_rk = 34, 76, 88, 100, 112, 124

    xn = C.sb("xn", [128, KC, NT], BF16)
    r_xn = C.res("xn")
    NW = 3
    wb = [C.sb(f"wb{i}", [128, KC, 128], BF16) for i in range(NW)]
    rwb = [C.res("wb") for _ in range(NW)]
    with C.phase():
        rmsnorm_T(C, xT, NT, g_attn, r_prm, xn, r_xn, "na")
    P.dma(SP, xn_o, xn[:, :, 1:NT], r=[r_xn], w=[r_out], own="r")
    wcount = [0]

    def load_w(src):
        i = wcount[0] % NW
        wcount[0] += 1
        P.dma(POOL, wb[i][:], src.rearrange("p (c m) -> p c m", m=128), w=[rwb[i]])
        return wb[i], rwb[i]

    def gemm_fm(wt, rw, banks, segs):
        for b, (s, n) in zip(banks, segs):
            for c in range(KC):
                P.mm(C.ps[b][:, 0:n], wt[:, c, :], xn[:, c, s:s + n], c == 0, c == KC - 1, r=[rw, r_xn], w=[C.rps[b]])

    seg_main = [(1, 512), (513, 512)]
    seg_halo = [(0, 1)]

    with C.phase():
        qk32 = [C.sb(f"qk32_{i}", [128, TL], F32) for i in range(2)]
        rqk32 = [C.res("qk32") for _ in range(2)]
        qksq = [C.sb(f"qksq_{i}", [128, TL], BF16) for i in range(2)]
        rqksq = [C.res("qksq") for _ in range(2)]
        qkr = [C.sb(f"qkr_{i}", [128, TL], F32) for i in range(2)]
        rqkr = [C.res("qkr") for _ in range(2)]
        qkb = [C.sb(f"qkb_{i}", [128, TL], BF16) for i in range(2)]
        rqkb = [C.res("qkb") for _ in range(2)]
        def qk_gemm(j):
            wt, rw = load_w(wqk_d[j])
            gemm_fm(wt, rw, [0, 1] if j % 2 == 0 else [2, 3], seg_main)

        def qk_post(j):
            i = j % 2
            mb = [0, 1] if i == 0 else [2, 3]
            for h, b in enumerate(mb):
                P.cp(ACT, qk32[i][:, h * 512:(h + 1) * 512], C.ps[b][:, :], r=[C.rps[b]], w=[rqk32[i]])
            P.act(qksq[i][:], qk32[i][:], AF.Square, r=[rqk32[i]], w=[rqksq[i]])
            for h in range(2):
                P.mm(C.ps[4][:, :], C.ones, qksq[i][:, h * 512:(h + 1) * 512], True, True, r=[rqksq[i], C.r_cbf], w=[C.rps[4]])
                P.act(qkr[i][:, h * 512:(h + 1) * 512], C.ps[4][:, :], AF.Sqrt, r=[C.rps[4], r_prm], w=[rqkr[i]], bias=C.eps_ap, scale=1.0 / 128)
            P.op(DVE, lambda e, i=i: e.reciprocal(qkr[i][:], qkr[i][:]), r=[rqkr[i]], w=[rqkr[i]])
            gcol = prm[:, 32:33] if j < 12 else prm[:, 33:34]
            P.stt(DVE, qkb[i][:], qk32[i][:], gcol, qkr[i][:], ALU.mult, ALU.mult, r=[rqk32[i], rqkr[i], r_prm], w=[rqkb[i]])
            dst = qT_o[j] if j < 12 else kT_o[j - 12]
            P.dma(SP, dst, qkb[i][:], r=[rqkb[i]], w=[r_out], own="r")

        qk_gemm(0)
        for j in range(24):
            if j + 1 < 24:
                qk_gemm(j + 1)
            qk_post(j)

    with C.phase():
        wvbs = [C.sb(f"wvb{i}", [128, KC, 512], BF16) for i in range(2)]
        rwvbs = [C.res("wvb") for _ in range(2)]
        vst = [C.sb(f"vst{i}", [128, 512], BF16) for i in range(2)]
        rvst = [C.res("vst") for _ in range(2)]
        cnt = 0
        P.dma(POOL, wvbs[0][:], wv_d[0].rearrange("p (c m) -> p c m", m=512), w=[rwvbs[0]])
        for gcol in range(3):
            wvb, rwvb = wvbs[gcol % 2], rwvbs[gcol % 2]
            if gcol + 1 < 3:
                P.dma(POOL, wvbs[(gcol + 1) % 2][:], wv_d[gcol + 1].rearrange("p (c m) -> p c m", m=512), w=[rwvbs[(gcol + 1) % 2]])
            for tt in range(8):
                b = cnt % 4
                i = cnt % 2
                cnt += 1
                for c in range(KC):
                    P.mm(C.ps[b][:, :], xn[:, c, 1 + tt * 128:1 + (tt + 1) * 128], wvb[:, c, :], c == 0, c == KC - 1, r=[rwvb, r_xn], w=[C.rps[b]])
                P.cp(ACT if cnt % 2 else DVE, vst[i][:], C.ps[b][:, :], r=[C.rps[b]], w=[rvst[i]])
                P.dma(SP, v_o[tt * 128:(tt + 1) * 128, gcol * 512:(gcol + 1) * 512], vst[i][:], r=[rvst[i]], w=[r_out], own="r")

    rwph = C.phase()
    rwph.__enter__()
    lorab = C.sb("lorab", [128, 6, 1536], BF16)
    r_lorab = C.res("lorab")
    with C.phase():
        lora32 = C.sb("lora32", [128, 1536], F32)
        r_l32 = C.res("l32")
        for q in range(6):
            P.dma(SP, lora32[:], lora_d[:, q * 1536:(q + 1) * 1536], w=[r_l32])
            P.cp(DVE, lorab[:, q, :], lora32[:], r=[r_l32], w=[r_lorab])

    seg32 = [C.sb(f"seg32_{i}", [128, NT], F32) for i in range(2)]
    rseg = [C.res("seg") for _ in range(2)]
    tmp = [C.sb(f"tsd_{i}", [128, TL], F32) for i in range(2)]
    rtmp = [C.res("tsd") for _ in range(2)]
    scount = [0]

    def rw_tile(jt, out_ap, r_o, post=None):
        wt, rw = load_w(wrw_d[jt])
        i = scount[0] % 2
        scount[0] += 1
        mb = [0, 1] if i == 0 else [2, 3]
        gemm_fm(wt, rw, mb, seg_main)
        gemm_fm(wt, rw, [4], seg_halo)
        P.cp(ACT, seg32[i][:, 1:513], C.ps[mb[0]][:, :], r=[C.rps[mb[0]]], w=[rseg[i]])
        P.cp(ACT, seg32[i][:, 513:1025], C.ps[mb[1]][:, :], r=[C.rps[mb[1]]], w=[rseg[i]])
        P.cp(DVE, seg32[i][:, 0:1], C.ps[4][:, 0:1], r=[C.rps[4]], w=[rseg[i]])
        P.tt(DVE, tmp[i][:], seg32[i][:, 0:TL], seg32[i][:, 1:NT], ALU.subtract, r=[rseg[i]], w=[rtmp[i]])
        P.stt(DVE, out_ap, tmp[i][:], prm[:, o_mix + jt:o_mix + jt + 1], seg32[i][:, 1:NT], ALU.mult, ALU.add,
              r=[rtmp[i], rseg[i], r_prm], w=[r_o])

    linb = C.sb("linb", [128, 6, TL], BF16)
    r_linb = C.res("linb")
    with C.phase():
        lin = C.sb("lin", [128, 6, TL], F32)
        r_lin = C.res("lin")
        for q in range(6):
            rw_tile(36 + q, lin[:, q, :], r_lin)
        P.act(linb[:, 0, :], lin[:, 0, :], AF.Tanh, r=[r_lin], w=[r_linb])
        P.cp(DVE, linb[:, 1, :], lin[:, 1, :], r=[r_lin], w=[r_linb])
        for q in range(2, 6):
            P.act(linb[:, q, :], lin[:, q, :], AF.Sigmoid, r=[r_lin], w=[r_linb])

    def f32buf(n):
        return C.sb(n, [128, TL], F32), C.res(n)

    def bfbuf(n):
        return C.sb(n, [128, TL], BF16), C.res(n)

    def dbl(fn, n):
        return [fn(f"{n}{i}") for i in range(2)]
    rr2, kk2, vv2, aa2 = dbl(f32buf, "rr"), dbl(f32buf, "kk"), dbl(f32buf, "vv"), dbl(f32buf, "aa")
    ld1, gt1 = f32buf("ldt"), f32buf("gt")
    ld2, gt2 = [ld1, ld1], [gt1, gt1]
    t1, r_t1 = f32buf("t1")
    t2, r_t2 = f32buf("t2")
    bon, r_bon = f32buf("bon")
    ob = [bfbuf(f"ob{i}") for i in range(5)]
    sqb, r_sqb = bfbuf("sqb")
    vT = [C.sb(f"vT{i}", [128, 128], BF16) for i in range(2)]
    rvT = [C.res("vT") for _ in range(2)]

    def bufs_for(ct):
        i = ct % 2
        return rr2[i], kk2[i], vv2[i], aa2[i], ld2[i], gt2[i]

    def rw_part1(ct):
        (rr, r_rr), (kk_, r_k), (vv, r_v), (aa, r_a), (ldt, r_ld), (gt, r_g) = bufs_for(ct)
        cs = slice(ct * 128, (ct + 1) * 128)
        cs = slice(ct * 128, (ct + 1) * 128)
        for h in range(2):
            hs = slice(h * 512, (h + 1) * 512)
            P.mm(C.ps[5][:, :], lorab[:, 0, cs], linb[:, 0, hs], True, True, r=[r_lorab, r_linb], w=[C.rps[5]])
            P.act(ldt[:, hs], C.ps[5][:, :], AF.Sigmoid, r=[C.rps[5], r_prm], w=[r_ld], bias=prm[:, o_w0 + ct:o_w0 + ct + 1], scale=1.0)
            P.mm(C.ps[6][:, :], lorab[:, 1, cs], linb[:, 1, hs], True, True, r=[r_lorab, r_linb], w=[C.rps[6]])
            P.act(aa[:, hs], C.ps[6][:, :], AF.Sigmoid, r=[C.rps[6], r_prm], w=[r_a], bias=prm[:, o_a0 + ct:o_a0 + ct + 1], scale=1.0)
            for q in range(4):
                P.mm(C.ps[7][:, :], lorab[:, 2 + q, cs], linb[:, 2 + q, hs], q == 0, q == 3, r=[r_lorab, r_linb], w=[C.rps[7]])
            P.cp(DVE, gt[:, hs], C.ps[7][:, :], r=[C.rps[7]], w=[r_g])
        P.ts(DVE, ldt[:], ldt[:], -float(np.exp(-0.5)), None, ALU.mult, None, r=[r_ld], w=[r_ld])
        P.dma(SP, ld_o[cs, :], ldt[:], r=[r_ld], w=[r_out], own="r")
        P.dma(SP, g_o[cs, :], gt[:], r=[r_g], w=[r_out], own="r")
        rw_tile(ct, rr[:], r_rr)
        rw_tile(12 + ct, kk_[:], r_k)
        rw_tile(24 + ct, vv[:], r_v)

    def rw_part2(ct):
        (rr, r_rr), (kk_, r_k), (vv, r_v), (aa, r_a), (ldt, r_ld), (gt, r_g) = bufs_for(ct)
        cs = slice(ct * 128, (ct + 1) * 128)
        P.cp(ACT, ob[0][0][:], rr[:], r=[r_rr], w=[ob[0][1]])
        P.dma(SP, rwf_o[0, cs, :], ob[0][0][:], r=[ob[0][1]], w=[r_out], own="r")
        P.ts(DVE, t1[:], kk_[:], prm[:, o_kk + ct:o_kk + ct + 1], None, ALU.mult, None, r=[r_k, r_prm], w=[r_t1])
        P.act(sqb[:], t1[:], AF.Square, r=[r_t1], w=[r_sqb])
        for h in range(2):
            hs = slice(h * 512, (h + 1) * 512)
            P.mm(C.ps[5][:, :], C.bones, sqb[:, hs], True, True, r=[r_sqb, C.r_cbf], w=[C.rps[5]])
            P.act(t2[:, hs], C.ps[5][:, :], AF.Sqrt, r=[C.rps[5]], w=[r_t2])
        P.ts(DVE, t2[:], t2[:], 1e-12, None, ALU.max, None, r=[r_t2], w=[r_t2])
        P.op(DVE, lambda e: e.reciprocal(t2[:], t2[:]), r=[r_t2], w=[r_t2])
        P.tt(DVE, t1[:], t1[:], t2[:], ALU.mult, r=[r_t1, r_t2], w=[r_t1])
        P.ts(DVE, ob[2][0][:], t1[:], -1.0, None, ALU.mult, None, r=[r_t1], w=[ob[2][1]])
        P.tt(DVE, ob[3][0][:], t1[:], aa[:], ALU.mult, r=[r_t1, r_a], w=[ob[3][1]])
        P.dma(SP, rwf_o[2, cs, :], ob[2][0][:], r=[ob[2][1]], w=[r_out], own="r")
        P.dma(SP, rwf_o[3, cs, :], ob[3][0][:], r=[ob[3][1]], w=[r_out], own="r")
        P.ts(DVE, t2[:], aa[:], -1.0, prm[:, o_ka + ct:o_ka + ct + 1], ALU.add, ALU.mult, r=[r_a, r_prm, r_t2], w=[r_t2])
        P.stt(DVE, kk_[:], t2[:], 1.0, kk_[:], ALU.add, ALU.mult, r=[r_t2, r_k], w=[r_k])
        P.cp(ACT, ob[1][0][:], kk_[:], r=[r_k], w=[ob[1][1]])
        P.dma(SP, rwf_o[1, cs, :], ob[1][0][:], r=[ob[1][1]], w=[r_out], own="r")
        P.stt(DVE, t1[:], rr[:], prm[:, o_rk + ct:o_rk + ct + 1], kk_[:], ALU.mult, ALU.mult, r=[r_rr, r_k, r_prm, r_t1], w=[r_t1])
        P.cp(ACT, sqb[:], t1[:], r=[r_t1], w=[r_sqb])
        for h in range(2):
            hs = slice(h * 512, (h + 1) * 512)
            P.mm(C.ps[6][:, :], C.bones, sqb[:, hs], True, True, r=[r_sqb, C.r_cbf], w=[C.rps[6]])
            P.tt(DVE, bon[:, hs], C.ps[6][:, :], vv[:, hs], ALU.mult, r=[C.rps[6], r_v], w=[r_bon])
        P.dma(SP, bon_o[cs, :], bon[:], r=[r_bon], w=[r_out], own="r")
        P.cp(ACT, ob[4][0][:], vv[:], r=[r_v], w=[ob[4][1]])
        for tt in range(8):
            i = tt % 2
            P.op(PE, lambda e, tt=tt: e.transpose(C.ps[7][:, 0:64].bitcast(BF16), ob[4][0][:, tt * 128:(tt + 1) * 128], C.ident),
                 r=[ob[4][1], C.r_cbf], w=[C.rps[7]])
            P.cp(DVE, vT[i][:], C.ps[7][:, 0:64].bitcast(BF16), r=[C.rps[7]], w=[rvT[i]])
            P.dma(SP, rwv_o[tt * 128:(tt + 1) * 128, cs], vT[i][:], r=[rvT[i]], w=[r_out], own="r")


    rw_part1(0)
    for ct in range(12):
        if ct + 1 < 12:
            rw_part1(ct + 1)
        rw_part2(ct)
    rwph.__exit__(None, None, None)
    return C.finish([r_out])


def _tile_lhsT(w, ncols=128):
    K, M = w.shape
    assert K % 128 == 0 and M % ncols == 0
    kc = K // 128
    a = w.reshape(kc, 128, M // ncols, ncols).transpose(2, 1, 0, 3)
    return np.ascontiguousarray(a).reshape(M // ncols, 128, kc * ncols)


def _pad_cols(w, m):
    if w.shape[1] == m:
        return w
    out = np.zeros((w.shape[0], m), w.dtype)
    out[:, :w.shape[1]] = w
    return out


def _cols(v, n):
    return np.ascontiguousarray(v.reshape(n, 128).T)


def _consts():
    c = np.zeros((128, 384), np.float32)
    c[:, 0:128] = 1.0
    c[0:64, 128:192] = 1.0
    c[64:128, 192:256] = 1.0
    c[:, 256:384] = np.eye(128, dtype=np.float32)
    return c


def prep_stage_a(inp):
    x = inp["x"][0]
    w_in = inp["w_in"][0]
    shared = {}
    shared["cst"] = _consts()
    shared["wqk"] = _tile_lhsT(w_in[:, 0:3072])
    shared["wv"] = _tile_lhsT(w_in[:, 3072:4608], 512)
    shared["wrw"] = _tile_lhsT(_pad_cols(w_in[:, 4608:4608 + RW_SEG], 42 * 128))
    lora = np.zeros((128, 6 * 1536), np.float32)
    lora[:, 0:1536] = inp["rw_w_up"][0]
    lora[:, 1536:3072] = inp["rw_a_up"][0]
    gup = np.zeros((512, 1536), np.float32)
    gup[:480] = inp["rw_g_up"][0]
    for q in range(4):
        lora[:, (2 + q) * 1536:(3 + q) * 1536] = gup[q * 128:(q + 1) * 128]
    shared["lora"] = lora
    prm = np.zeros((128, NPA), np.float32)
    prm[:, 0:32] = _cols(inp["attn_norm_g"][0], 32)
    prm[:, 32] = inp["sb_q_norm_g"][0]
    prm[:, 33] = inp["sb_k_norm_g"][0]
    mix = np.zeros(42 * 128, np.float32)
    mix[:RW_SEG] = inp["rw_mix"][0]
    prm[:, 34:76] = _cols(mix, 42)
    prm[:, 76:88] = _cols(inp["rw_w0"][0], 12)
    prm[:, 88:100] = _cols(inp["rw_a0"][0], 12)
    prm[:, 100:112] = _cols(inp["rw_k_k"][0], 12)
    prm[:, 112:124] = _cols(inp["rw_k_a"][0], 12)
    prm[:, 124:136] = _cols(inp["rw_r_k"][0].reshape(-1), 12)
    prm[:, NPA - 1] = EPS
    shared["prm"] = prm
    maps = []
    for c in range(NCORES):
        xc = np.zeros((TL + 1, D), np.float32)
        lo = c * TL - 1
        if c == 0:
            xc[1:] = x[0:TL]
        else:
            xc[:] = x[lo:lo + TL + 1]
        xT = np.ascontiguousarray(xc.T.reshape(KC, 128, TL + 1).transpose(1, 0, 2))
        m = dict(shared)
        m["xT"] = xT
        maps.append(m)
    return maps


CH = 64
SCL = 512
NCH = SCL // CH
NHB = 3


def build_stage_b(do_sb=True, do_rw=True, nsc=T // SCL, ngl=None, do_gates=True):
    C = Ctx("B")
    P, nc = C.P, C.nc
    cst_d = C.din("cst", [128, 384], F32)
    cst2_d = C.din("cst2", [128, 512], F32)
    qF_d = C.din("qF", [128, T], BF16)
    kF_d = C.din("kF", [128, T], BF16)
    vF_d = C.din("vF", [T, 128], BF16)
    qH_d = C.din("qH", [128, T // 2], BF16)
    kH_d = C.din("kH", [128, T], BF16)
    vH_d = C.din("vH", [T, 128], BF16)
    mF_d = C.din("mF", [128, 4, 512], BF16)
    mH_d = C.din("mH", [128, 8, 512], BF16)
    rwf_d = C.din("rwf", [NHB, 64, 4, T], BF16)
    ld_d = C.din("ld", [NHB, 64, T], F32)
    rwv_d = C.din("rwv", [NHB, T, 64], BF16)
    xn_d = C.din("xn", [128, KC, TL], BF16)
    wg_d = C.din("wg", [96, 128, KC * 128], F32)
    sg_o = C.dout("sg", [96, 128, TL], BF16)
    oF_o = C.dout("oF", [128, T], BF16)
    oH_o = C.dout("oH", [128, T // 2], BF16)
    y_o = C.dout("y", [NHB, 64, T], F32)
    r_out = C.res("outs")

    load_consts(C, cst_d)
    c2 = C.sb("c2", [128, 512], F32)
    c2b = C.sb("c2b", [128, 512], BF16)
    r_c2, r_c2b = C.res("c2"), C.res("c2b")
    P.dma(SP, c2[:], cst2_d, w=[r_c2])
    P.cp(DVE, c2b[:], c2[:], r=[r_c2], w=[r_c2b])
    tri, trip = c2b[:, 0:128], c2b[:, 128:256]
    maskM = c2[:, 256:384]
    maskN = c2[0:64, 384:448]
    one_col = C.c32[:, 0:1]

    slots = [(C.ps[b][:, 0:128], C.rps[b]) for b in (5, 6, 7)]
    sl_i = [0]

    def slot():
        s = slots[sl_i[0] % len(slots)]
        sl_i[0] += 1
        return s

    sbshare = {}

    def gates_chain(per_yield=14):
        qk, v = sbshare["qk"], sbshare["v"]
        r_q, r_k, r_v = sbshare["r"]
        ost, r_ost = sbshare["ost"], sbshare["r_ost"]
        xnh = qk[:, :].rearrange("p (c t) -> p c t", c=KC)
        r_xnh = C.res("xnh")
        wbs = [v[:, 0:KC, :], v[:, KC:2 * KC, :]]
        rwbs = [C.res("gwb") for _ in range(2)]
        cnt = 0
        wi = 0
        for hh in range(2):
            P.dma(SP, xnh, xn_d[:, :, hh * 512:(hh + 1) * 512], w=[r_xnh, r_q, r_k])
            for j in range(96):
                i = wi % 2
                P.dma(POOL, wbs[i], wg_d[j].rearrange("p (c m) -> p c m", m=128), w=[rwbs[i], r_v] if wi < 2 else [rwbs[i]])
                wi += 1
                b = j % 4
                for c in range(KC):
                    P.mm(C.ps[b][:, :], wbs[i][:, c, :], xnh[:, c, :], c == 0, c == KC - 1, r=[rwbs[i], r_xnh], w=[C.rps[b]])
                    cnt += 1
                    if cnt % per_yield == 0:
                        yield
                io = j % 2
                P.act(ost[io][:], C.ps[b][:, :], AF.Sigmoid, r=[C.rps[b]], w=[r_ost[io]])
                P.dma(SP, sg_o[j][:, hh * 512:(hh + 1) * 512], ost[io][:], r=[r_ost[io]], w=[r_out], own="r")

    def sb_pipeline(units):
        ZB, RB, OB = [0, 1], [2, 3], 4
        qk = C.sb("qk", [128, 2 * T], BF16)
        q, k = qk[:, 0:T], qk[:, T:2 * T]
        v = C.sb("v", [128, T // 128, 128], BF16)
        r_q, r_k, r_v = C.res("q"), C.res("k"), C.res("v")
        sbshare.update(qk=qk, v=v, r=(r_q, r_k, r_v))
        NE, NS, NX, NWT = 6, 4, 3, 3
        e32 = [C.sb(f"e32_{i}", [128, 512], F32) for i in range(NE)]
        re32 = [C.res("e32") for _ in range(NE)]
        sp = [C.sb(f"sp_{i}", [128, 512], BF16) for i in range(NS)]
        rsp = [C.res("sp") for _ in range(NS)]
        ex = [C.sb(f"ex_{i}", [128, 512], F32) for i in range(NX)]
        rex = [C.res("ex") for _ in range(NX)]
        wt = [C.sb(f"w_{i}", [128, 512], BF16) for i in range(NWT)]
        rwt = [C.res("w") for _ in range(NWT)]
        ost = [C.sb(f"ost{i}", [128, 512], BF16) for i in range(2)]
        r_ost = [C.res("ost") for _ in range(2)]
        sbshare.update(ost=ost, r_ost=r_ost)
        sacc = C.sb("sacc", [128, 512], F32)
        r_sacc = C.res("sacc")
        steps = []
        for ui, (q_d, k_d, v_d, m_sb, nq_blocks, kpg, o_d) in enumerate(units):
            ngroups = nq_blocks // 4 if ngl is None else ngl
            for g in range(ngroups):
                nkb = kpg * (g + 1)
                for idx, kb in enumerate(reversed(range(nkb))):
                    steps.append(dict(u=ui, g=g, kb=kb, first=idx == 0, last=idx == nkb - 1, mj=kb - (nkb - kpg)))
        ns = len(steps)
        gcount = [0]

        def load_unit(ui):
            q_d, k_d, v_d, m_sb, nq_blocks, kpg, o_d = units[ui]
            P.dma(SP, q[:, 0:nq_blocks * 128], q_d, w=[r_q])
            P.dma(SP, k[:], k_d, w=[r_k])
            P.dma(SP, v[:], v_d.rearrange("(b p) d -> p b d", p=128), w=[r_v])

        def st1(i):
            st = steps[i]
            if i == 0 or steps[i - 1]["u"] != st["u"]:
                load_unit(st["u"])
            zb = ZB[i % 2]
            P.mm(C.ps[zb][:, :], k[:, st["kb"] * 128:(st["kb"] + 1) * 128], q[:, st["g"] * 512:(st["g"] + 1) * 512], True, True,
                 r=[r_k, r_q], w=[C.rps[zb]])

        def st2(i):
            st = steps[i]
            zb = ZB[i % 2]
            ie, is_ = i % NE, i % NS
            P.act(e32[ie][:], C.ps[zb][:, :], AF.Exp, r=[C.rps[zb]], w=[re32[ie]], scale=float(128 ** -0.5))
            P.act(sp[is_][:], e32[ie][:], AF.Ln, r=[re32[ie], C.r_c32], w=[rsp[is_]], bias=one_col, scale=1.0)
            if st["mj"] >= 0:
                m_sb = units[st["u"]][3]
                P.tt(POOL, sp[is_][:], sp[is_][:], m_sb[:, st["mj"], :], ALU.mult, r=[rsp[is_], r_m], w=[rsp[is_]])

        def st3(i):
            st = steps[i]
            rb = RB[i % 2]
            is_ = i % NS
            P.mm(C.ps[rb][:, :], tri, sp[is_][:], True, st["first"], r=[rsp[is_], r_c2b], w=[C.rps[rb]])
            if not st["first"]:
                P.mm(C.ps[rb][:, :], C.c32[:, 0:128], sacc[:], False, True, r=[r_sacc, C.r_c32], w=[C.rps[rb]])

        def st4(i):
            st = steps[i]
            rb = RB[i % 2]
            is_, ix = i % NS, i % NX
            P.act(ex[ix][:], C.ps[rb][:, :], AF.Exp, r=[C.rps[rb]], w=[rex[ix]], scale=-1.0)
            if not st["last"]:
                if st["first"]:
                    P.cp(POOL, sacc[:], sp[is_][:], r=[rsp[is_]], w=[r_sacc])
                else:
                    P.tt(POOL, sacc[:], sacc[:], sp[is_][:], ALU.add, r=[rsp[is_], r_sacc], w=[r_sacc])

        def st5(i):
            st = steps[i]
            ie, ix, iw = i % NE, i % NX, i % NWT
            P.tt(DVE, wt[iw][:], e32[ie][:], ex[ix][:], ALU.mult, r=[re32[ie], rex[ix]], w=[rwt[iw]])
            if st["mj"] >= 0:
                m_sb = units[st["u"]][3]
                P.tt(POOL, wt[iw][:], wt[iw][:], m_sb[:, st["mj"], :], ALU.mult, r=[rwt[iw], r_m], w=[rwt[iw]])

        def st6(i):
            st = steps[i]
            iw = i % NWT
            P.mm(C.ps[OB][:, :], v[:, st["kb"], :], wt[iw][:], st["first"], st["last"], r=[r_v, rwt[iw]], w=[C.rps[OB]])
            if st["last"]:
                o_d = units[st["u"]][6]
                io = gcount[0] % 2
                gcount[0] += 1
                P.cp(DVE, ost[io][:], C.ps[OB][:, :], r=[C.rps[OB]], w=[r_ost[io]])
                P.dma(SP, o_d[:, st["g"] * 512:(st["g"] + 1) * 512], ost[io][:], r=[r_ost[io]], w=[r_out], own="r")

        stages = [st1, st2, st3, st4, st5, st6]
        bounds = [0] + [i for i in range(1, ns) if steps[i]["u"] != steps[i - 1]["u"]] + [ns]
        for lo, hi in zip(bounds[:-1], bounds[1:]):
            for t in range(lo, hi + len(stages) - 1):
                for kk in reversed(range(len(stages))):
                    i = t - kk
                    if lo <= i < hi:
                        stages[kk](i)
                yield

    def rw_chain():
        H3 = NHB
        BA, BB, BC = 5, 6, 7
        pA, rA = C.ps[BA], C.rps[BA]
        pB, rB = C.ps[BB], C.rps[BB]
        pC, rC = C.ps[BC], C.rps[BC]
        S32 = C.sb("S32", [64, H3, 64], F32)
        Sbf = C.sb("Sbf", [64, H3, 64], BF16)
        rS = C.res("S")
        P.op(DVE, lambda e: e.memset(S32[:], 0.0), w=[rS])
        P.op(DVE, lambda e: e.memset(Sbf[:], 0.0), w=[rS])
        msk = C.sb("rmsk", [64, H3 * SCL], F32)
        r_msk = C.res("rmsk")
        P.op(DVE, lambda e: e.memset(msk[:], 1.0), w=[r_msk])
        P.op(DVE, lambda e: e.memset(msk[:].rearrange("p (n c) -> p n c", c=CH)[:, :, 0:1], 0.0), w=[r_msk])
        mM3 = C.sb("mM3", [128, H3, 128], F32)
        mN3 = C.sb("mN3", [64, H3, 64], F32)
        id3 = C.sb("id3", [64, H3, 64], F32)
        r_k3 = C.res("k3")
        for h in range(H3):
            P.cp(DVE, mM3[:, h, :], maskM, r=[r_c2], w=[r_k3])
            P.cp(DVE, mN3[:, h, :], maskN, r=[r_c2], w=[r_k3])
            P.cp(DVE, id3[:, h, :], C.ident32[0:64, 0:64], r=[C.r_c32], w=[r_k3])
        fm = C.sb("fm", [64, H3, 4, SCL], BF16)
        ldb = C.sb("ldb", [64, H3 * SCL], F32)
        UV = C.sb("UV", [128, NCH, H3, 64], BF16)
        r_fm, r_ld, r_V, r_U = C.res("fm"), C.res("ld"), C.res("V"), C.res("U")
        cum = C.sb("cum", [64, H3 * SCL], F32)
        tA = C.sb("tA", [64, H3 * SCL], F32)
        tB = C.sb("tB", [64, H3 * SCL], F32)
        r_cum, r_tA, r_tB = C.res("cum"), C.res("tA"), C.res("tB")
        PC = C.sb("PC", [64, H3, NCH], F32)
        r_PC = C.res("PC")
        Q2 = C.sb("Q2", [64, H3, NCH, 2, CH], BF16)
        KB = C.sb("KB", [64, H3, NCH, 2, CH], BF16)
        KBb = C.sb("KBb", [64, H3, NCH, 2, CH], BF16)
        r_Q2, r_KB, r_KBb = C.res("Q2"), C.res("KB"), C.res("KBb")
        Y = C.sb("Y", [64, H3, SCL], F32)
        r_Y = C.res("Y")

        def wb2(name, shape, dt):
            return [C.sb(f"{name}_{i}", shape, dt) for i in range(2)], [C.res(name) for i in range(2)]
        Mm, r_Mm = wb2("Mm", [128, H3, 128], BF16)
        LTa, r_LTa = wb2("LTa", [64, H3, 128], F32)
        LTb, r_LTb = wb2("LTb", [64, H3, 128], F32)
        Na, r_Na = wb2("Na", [64, H3, 64], F32)
        Nb, r_Nb = wb2("Nb", [64, H3, 64], F32)
        Tt, r_Tt = wb2("Tt", [64, H3, 64], BF16)
        W2, r_W2 = wb2("W2", [64, H3, 64], F32)
        Xs, r_Xs = wb2("Xs", [64, H3, 64], BF16)
        KT, r_KT = wb2("KT", [128, H3, 64], BF16)

        def v4(ap):
            return ap.rearrange("p (h n c) -> p h n c", h=H3, c=CH)

        def f4(j):
            return fm[:, :, j, :].rearrange("p h (n c) -> p h n c", c=CH)

        def load(sc):
            ts_ = slice(sc * SCL, (sc + 1) * SCL)
            for h in range(H3):
                P.dma(SP, fm[:, h, :, :], rwf_d[h, :, :, ts_], w=[r_fm])
                P.dma(SP, ldb[:, h * SCL:(h + 1) * SCL], ld_d[h, :, ts_], w=[r_ld])
                P.dma(SP, UV[64:128, :, h, :], rwv_d[h, ts_, :].rearrange("(n c) v -> c n v", c=CH), w=[r_V])
            P.op(DVE, lambda e: e.tensor_tensor_scan(cum[:], msk[:], ldb[:], 0.0, ALU.mult, ALU.add), r=[r_msk, r_ld], w=[r_cum])
            P.act(tA[:], cum[:], AF.Exp, r=[r_cum], w=[r_tA])
            P.tt(DVE, Q2[:, :, :, 1, :], f4(0), v4(tA[:]), ALU.mult, r=[r_fm, r_tA], w=[r_Q2])
            P.tt(DVE, tB[:], cum[:], ldb[:], ALU.subtract, r=[r_cum, r_ld], w=[r_tB])
            P.act(tA[:], tB[:], AF.Exp, r=[r_tB], w=[r_tA])
            P.tt(DVE, Q2[:, :, :, 0, :], f4(2), v4(tA[:]), ALU.mult, r=[r_fm, r_tA], w=[r_Q2])
            P.act(tA[:], cum[:], AF.Exp, r=[r_cum], w=[r_tA], scale=-1.0)
            P.tt(DVE, KB[:, :, :, 0, :], f4(3), v4(tA[:]), ALU.mult, r=[r_fm, r_tA], w=[r_KB])
            P.tt(DVE, KB[:, :, :, 1, :], f4(1), v4(tA[:]), ALU.mult, r=[r_fm, r_tA], w=[r_KB])
            cumC = v4(cum[:])[:, :, :, CH - 1:CH]
            P.tt(DVE, v4(tB[:]), cumC.to_broadcast([64, H3, NCH, CH]), v4(cum[:]), ALU.subtract, r=[r_cum], w=[r_tB])
            P.act(tA[:], tB[:], AF.Exp, r=[r_tB], w=[r_tA])
            P.tt(DVE, KBb[:, :, :, 0, :], f4(3), v4(tA[:]), ALU.mult, r=[r_fm, r_tA], w=[r_KBb])
            P.tt(DVE, KBb[:, :, :, 1, :], f4(1), v4(tA[:]), ALU.mult, r=[r_fm, r_tA], w=[r_KBb])
            P.act(PC[:].rearrange("p h (n o) -> p h n o", o=1), cumC, AF.Exp, r=[r_cum], w=[r_PC])

        def flat(t4, h, n):
            return t4[:, h, n].rearrange("p a c -> p (a c)")

        def prep(n, i):
            for h in range(H3):
                P.mm(pA[:, h * 128:(h + 1) * 128], flat(KB, h, n), flat(Q2, h, n), True, True, r=[r_KB, r_Q2], w=[rA])
                P.mm(pB[0:64, h * 64:(h + 1) * 64], Q2[:, h, n, 0, :], KB[:, h, n, 0, :], True, True, r=[r_KB, r_Q2], w=[rB])
            pA3 = pA[:, 0:H3 * 128].rearrange("p (h c) -> p h c", h=H3)
            pB3 = pB[0:64, 0:H3 * 64].rearrange("p (h c) -> p h c", h=H3)
            P.tt(DVE, Mm[i][:], pA3, mM3[:], ALU.mult, r=[rA, r_k3], w=[r_Mm[i]])
            P.tt(DVE, LTa[i][:, :, 0:64], pA3[0:64, :, 0:64], mM3[0:64, :, 0:64], ALU.mult, r=[rA, r_k3], w=[r_LTa[i]])
            P.cp(DVE, LTa[i][:, :, 64:128], id3[:], r=[r_k3], w=[r_LTa[i]])
            P.tt(DVE, Na[i][:], pB3, mN3[:], ALU.mult, r=[rB, r_k3], w=[r_Na[i]])
            yield
            bufs = [(LTa[i], r_LTa[i], Na[i], r_Na[i]), (LTb[i], r_LTb[i], Nb[i], r_Nb[i])]
            for j in range(6):
                (lt, rlt, nn_, rnn) = bufs[j % 2]
                (lt2, rlt2, nn2, rnn2) = bufs[(j + 1) % 2]
                for h in range(H3):
                    P.mm(pA[0:64, h * 128:(h + 1) * 128], nn_[:, h, :], lt[:, h, :], True, True, r=[rnn, rlt], w=[rA])
                    if j < 5:
                        P.mm(pB[0:64, h * 64:(h + 1) * 64], lt[:, h, 0:64], nn_[:, h, :], True, True, r=[rnn, rlt], w=[rB])
                pA3h = pA[0:64, 0:H3 * 128].rearrange("p (h c) -> p h c", h=H3)
                if j < 5:
                    P.cp(DVE, lt2[:, :, 0:64], pA3h[:, :, 0:64], r=[rA], w=[rlt2])
                    P.cp(DVE, nn2[:], pB3, r=[rB], w=[rnn2])
                P.tt(DVE, lt2[:, :, 64:128], pA3h[:, :, 64:128], lt[:, :, 64:128], ALU.add, r=[rA, rlt], w=[rlt2])
                yield
            ltf, rltf = bufs[0][0], bufs[0][1]
            P.cp(DVE, Tt[i][:], ltf[:, :, 64:128], r=[rltf], w=[r_Tt[i]])
            for h in range(H3):
                P.mm(pB[0:64, h * 64:(h + 1) * 64], Mm[i][64:128, h, 0:64], UV[64:128, n, h, :], True, True, r=[r_Mm[i], r_V], w=[rB])
            P.cp(DVE, W2[i][:], pB3, r=[rB], w=[r_W2[i]])
            ptb = pA[:, 0:H3 * 32].bitcast(BF16)
            for h in range(H3):
                P.op(PE, lambda e, h=h: e.transpose(ptb[:, h * 64:(h + 1) * 64], flat(KBb, h, n), C.ident[0:64, 0:64]),
                     r=[r_KBb, C.r_cbf], w=[rA])
            P.cp(DVE, KT[i][:], ptb.rearrange("p (h c) -> p h c", h=H3), r=[rA], w=[r_KT[i]])
            yield

        def crit(n, i):
            cX = pC[0:64, 0:H3 * 64]
            cU = pC[0:64, H3 * 64:2 * H3 * 64]
            cX3 = cX.rearrange("p (h c) -> p h c", h=H3)
            cU3 = cU.rearrange("p (h c) -> p h c", h=H3)
            for h in range(H3):
                P.mm(cX[:, h * 64:(h + 1) * 64], Q2[:, h, n, 0, :], Sbf[:, h, :], True, True, r=[r_Q2, rS], w=[rC])
            P.tt(DVE, Xs[i][:], cX3, W2[i][:], ALU.add, r=[rC, r_W2[i]], w=[r_Xs[i]])
            for h in range(H3):
                P.mm(cU[:, h * 64:(h + 1) * 64], Tt[i][:, h, :], Xs[i][:, h, :], True, True, r=[r_Tt[i], r_Xs[i]], w=[rC])
            P.cp(DVE, UV[0:64, n, :, :], cU3, r=[rC], w=[r_U])
            yield
            for h in range(H3):
                P.mm(cX[:, h * 64:(h + 1) * 64], Sbf[:, h, :], Q2[:, h, n, 1, :], True, False, r=[rS, r_Q2], w=[rC])
                P.mm(cX[:, h * 64:(h + 1) * 64], UV[:, n, h, :], Mm[i][:, h, 64:128], False, True, r=[r_U, r_V, r_Mm[i]], w=[rC])
                P.mm(cU[:, h * 64:(h + 1) * 64], KT[i][:, h, :], UV[:, n, h, :], True, True, r=[r_KT[i], r_U, r_V], w=[rC])
            P.cp(ACT, Y[:, :, n * CH:(n + 1) * CH], cX3, r=[rC], w=[r_Y])
            P.tt(DVE, S32[:], S32[:], PC[:, :, n:n + 1].to_broadcast([64, H3, 64]), ALU.mult, r=[r_PC, rS], w=[rS])
            P.tt(DVE, S32[:], S32[:], cU3, ALU.add, r=[rC, rS], w=[rS])
            P.cp(DVE, Sbf[:], S32[:], r=[rS], w=[rS])
            yield

        for sc in range(nsc):
            load(sc)
            yield
            ovl = False
            if not ovl:
                for n in range(NCH):
                    yield from prep(n, n % 2)
                    yield from crit(n, n % 2)
            else:
                yield from prep(0, 0)
                for n in range(NCH):
                    gc = crit(n, n % 2)
                    gp = prep(n + 1, (n + 1) % 2) if n + 1 < NCH else None
                    while gc is not None or gp is not None:
                        if gp is not None:
                            try:
                                next(gp)
                            except StopIteration:
                                gp = None
                        if gc is not None:
                            try:
                                next(gc)
                            except StopIteration:
                                gc = None
                        yield
            for h in range(H3):
                P.dma(SP, y_o[h, :, sc * SCL:(sc + 1) * SCL], Y[:, h, :], r=[r_Y], w=[r_out], own="r")

    sbq, rwq = None, None
    if do_sb:
        msb = C.sb("msb", [128, 12, 512], BF16)
        r_m = C.res("msb")
        P.dma(SP, msb[:, 0:4, :], mF_d, w=[r_m])
        P.dma(SP, msb[:, 4:12, :], mH_d, w=[r_m])
        sbq = sb_pipeline([(qF_d, kF_d, vF_d, msb[:, 0:4, :], T // 128, 4, oF_o),
                           (qH_d, kH_d, vH_d, msb[:, 4:12, :], T // 256, 8, oH_o)])
    if do_rw:
        rwq = rw_chain()
    gq = None
    gates_started = False
    while sbq is not None or rwq is not None or gq is not None:
        for _ in range(1):
            if sbq is not None:
                try:
                    next(sbq)
                except StopIteration:
                    sbq = None
        if sbq is None and do_sb and do_gates and not gates_started:
            gates_started = True
            gq = gates_chain()
        if rwq is not None:
            try:
                next(rwq)
            except StopIteration:
                rwq = None
        if gq is not None:
            try:
                next(gq)
            except StopIteration:
                gq = None
    return C.finish([r_out])


def _consts2():
    c = np.zeros((128, 512), np.float32)
    j = np.arange(128)[:, None]
    s_ = np.arange(128)[None, :]
    c[:, 0:128] = (j >= s_)
    c[:, 128:256] = (j < s_)
    rs = np.arange(128)[:, None] % 64
    ct = np.arange(128)[None, :]
    m = np.zeros((128, 128), np.float32)
    m[:, 0:64] = (rs < ct[:, 0:64])
    m[:, 64:128] = (rs <= (ct[:, 64:128] - 64))
    c[:, 256:384] = m
    tt_ = np.arange(64)[:, None]
    ss_ = np.arange(64)[None, :]
    c[0:64, 384:448] = (ss_ < tt_)
    return c


def _sb_masks(qblocks, kblocks):
    m = np.zeros((128, len(kblocks), len(qblocks) * 128), np.float32)
    key = np.arange(128)[:, None]
    qp = np.arange(128)[None, :]
    for j, kb in enumerate(kblocks):
        for i, qb in enumerate(qblocks):
            m[:, j, i * 128:(i + 1) * 128] = (kb * 128 + key) < (qb * 128 + qp)
    return m.astype(NPBF)


def sb_units(c):
    if c % 2 == 0:
        return 3 * c // 2, 3 * c // 2 + 1, 0
    return (3 * c + 1) // 2, (3 * c - 1) // 2, 1


def prep_stage_b(resA, inp=None):
    wg = _tile_lhsT(np.asarray(inp["w_in"][0][:, 10976:23264])) if inp is not None else None
    qT = np.concatenate([np.asarray(r["qT"]) for r in resA], axis=2)
    kT = np.concatenate([np.asarray(r["kT"]) for r in resA], axis=2)
    v = np.concatenate([np.asarray(r["v"]) for r in resA], axis=0)
    rwf = np.concatenate([np.asarray(r["rwf"]) for r in resA], axis=2)
    ld = np.concatenate([np.asarray(r["ld"]) for r in resA], axis=1)
    rwv = np.concatenate([np.asarray(r["rwv"]) for r in resA], axis=0)
    cst, cst2 = _consts(), _consts2()
    mF = _sb_masks([0, 1, 2, 3], [0, 1, 2, 3])
    maps = []
    for c in range(NCORES):
        hf, hh, par = sb_units(c)
        m = {"cst": cst, "cst2": cst2, "mF": mF}
        m["mH"] = _sb_masks([par, 2 + par, 4 + par, 6 + par], list(range(8)))
        m["qF"] = np.ascontiguousarray(qT[hf])
        m["kF"] = np.ascontiguousarray(kT[hf])
        m["vF"] = np.ascontiguousarray(v[:, hf * 128:(hf + 1) * 128])
        qh = qT[hh].reshape(128, T // 256, 2, 128)[:, :, par, :].reshape(128, T // 2)
        m["qH"] = np.ascontiguousarray(qh)
        m["kH"] = np.ascontiguousarray(kT[hh])
        m["vH"] = np.ascontiguousarray(v[:, hh * 128:(hh + 1) * 128])
        hs = [3 * c + i for i in range(NHB)]
        m["rwf"] = np.ascontiguousarray(np.stack([rwf[:, h * 64:(h + 1) * 64, :].transpose(1, 0, 2) for h in hs]))
        m["ld"] = np.ascontiguousarray(np.stack([ld[h * 64:(h + 1) * 64] for h in hs]))
        m["rwv"] = np.ascontiguousarray(np.stack([rwv[:, h * 64:(h + 1) * 64] for h in hs]))
        if wg is not None:
            m["wg"] = wg
            m["xn"] = np.asarray(resA[c]["xn"])
        maps.append(m)
    return maps


NPC = 32 + 32 + 32 + 2 + 2 + 12 + 12 + 1 + 1
NFT = DFF // 128
HT = 512


def build_stage_c(parts=("mem", "rw", "merge", "out", "ffn")):
    C = Ctx("C")
    P, nc = C.P, C.nc
    xT = C.din("xT", [128, KC, TL], F32)
    memT = C.din("memT", [128, KC, NMEM], F32)
    prm_d = C.din("prm", [128, NPC], F32)
    cst_d = C.din("cst", [128, 384], F32)
    wkvk_d = C.din("wkvk", [8, 128, KC * 128], F32)
    wkvv_d = C.din("wkvv", [2, 128, KC * 512], F32)
    wmq_d = C.din("wmq", [8, 128, KC * 128], F32)
    wu_d = C.din("wu", [32, 128, KC * 128], F32)
    wo_d = C.din("wo", [32, 128, KC * 128], F32)
    wfg_d = C.din("wfg", [NFT, 128, KC * 128], F32)
    wfu_d = C.din("wfu", [NFT, 128, KC * 128], F32)
    wfd_d = C.din("wfd", [32, 2, 128, 43 * 128], F32)
    osb_d = C.din("osb", [128, 12, TL], BF16)
    y_d = C.din("y", [RWW, TL], F32)
    g_d = C.din("g", [RWW, TL], F32)
    bon_d = C.din("bon", [RWW, TL], F32)
    out_o = C.dout("outT", [D, TL], F32)
    sg_s = C.din("sg", [96, 128, TL], BF16)
    xn_d = C.din("xn", [128, KC, TL], BF16)
    h1_s = C.dint("h1_s", [32, 128, TL], F32)
    r_out, r_sg, r_h1 = C.res("outs"), C.res("sg"), C.res("h1")

    load_consts(C, cst_d)
    prm = C.sb("prm", [128, NPC], F32)
    r_prm = C.res("prm")
    P.dma(SP, prm[:], prm_d, w=[r_prm])
    C.eps_ap = prm[:, NPC - 1:NPC]
    gneps_ap = prm[:, NPC - 2:NPC - 1]
    C.r_eps = r_prm
    g_attn, g_mem, g_ffn = prm[:, 0:32], prm[:, 32:64], prm[:, 64:96]
    o_mq, o_mk, o_lng, o_lnb = 96, 98, 100, 112
    one_col = C.c32[:, 0:1]

    NW = 3
    wb = [C.sb(f"wb{i}", [128, KC, 128], BF16) for i in range(NW)]
    rwb = [C.res("wb") for _ in range(NW)]
    wcount = [0]

    def load_w(src):
        i = wcount[0] % NW
        wcount[0] += 1
        P.dma(POOL, wb[i][:], src.rearrange("p (c m) -> p c m", m=128), w=[rwb[i]])
        return wb[i], rwb[i]

    halves = [(0, HT), (HT, HT)]
    omem = C.sb("omem", [128, 8, TL], BF16)
    r_omem = C.res("omem")

    with C.phase():
        xn = C.sb("xn", [128, KC, TL], BF16)
        r_xn = C.res("xn")
        P.dma(SP, xn[:], xn_d, w=[r_xn])

        def gemm_fm(wt, rw, banks, rhs3, r_rhs, segs):
            for b, (s, n) in zip(banks, segs):
                for c in range(KC):
                    P.mm(C.ps[b][:, 0:n], wt[:, c, :], rhs3[:, c, s:s + n], c == 0, c == KC - 1, r=[rw, r_rhs], w=[C.rps[b]])

        if "mem" in parts:
            with C.phase():
                mn = C.sb("mn", [128, KC, NMEM], BF16)
                r_mn = C.res("mn")
                with C.phase():
                    rmsnorm_T(C, memT, NMEM, g_mem, r_prm, mn, r_mn, "nm")
                mk32 = C.sb("mk32", [128, 8, NMEM], F32)
                r_mk32 = C.res("mk32")
                mksq = C.sb("mksq", [128, 8, NMEM], BF16)
                r_mksq = C.res("mksq")
                mkn = C.sb("mkn", [128, 8, NMEM], BF16)
                r_mkn = C.res("mkn")
                mrs = C.sb("mrs", [128, NMEM], F32)
                r_mrs = C.res("mrs")
                for j in range(8):
                    wt, rw = load_w(wkvk_d[j])
                    b = j % 2
                    gemm_fm(wt, rw, [b], mn, r_mn, [(0, NMEM)])
                    P.cp(ACT, mk32[:, j, :], C.ps[b][:, 0:NMEM], r=[C.rps[b]], w=[r_mk32])
                    P.act(mksq[:, j, :], mk32[:, j, :], AF.Square, r=[r_mk32], w=[r_mksq])
                for h in range(4):
                    for dt in range(2):
                        P.mm(C.ps[2][:, 0:NMEM], C.ones, mksq[:, 2 * h + dt, :], dt == 0, dt == 1, r=[r_mksq, C.r_cbf], w=[C.rps[2]])
                    P.act(mrs[:], C.ps[2][:, 0:NMEM], AF.Sqrt, r=[C.rps[2], r_prm], w=[r_mrs], bias=C.eps_ap, scale=1.0 / 256)
                    P.op(DVE, lambda e: e.reciprocal(mrs[:], mrs[:]), r=[r_mrs], w=[r_mrs])
                    for dt in range(2):
                        P.stt(DVE, mkn[:, 2 * h + dt, :], mk32[:, 2 * h + dt, :], prm[:, o_mk + dt:o_mk + dt + 1], mrs[:], ALU.mult, ALU.mult,
                              r=[r_mk32, r_mrs, r_prm], w=[r_mkn])
                mv = C.sb("mv", [128, 2, MEMW], BF16)
                r_mv = C.res("mv")
                with C.phase():
                    wvb = C.sb("wvb", [128, KC, 512], BF16)
                    rwvb = C.res("wvb")
                    for gc in range(2):
                        P.dma(POOL, wvb[:], wkvv_d[gc].rearrange("p (c m) -> p c m", m=512), w=[rwvb])
                        for mt in range(2):
                            b = 3 + mt
                            for c in range(KC):
                                P.mm(C.ps[b][:, :], mn[:, c, mt * 128:(mt + 1) * 128], wvb[:, c, :], c == 0, c == KC - 1, r=[rwvb, r_mn], w=[C.rps[b]])
                            P.cp(ACT, mv[:, mt, gc * 512:(gc + 1) * 512], C.ps[b][:, :], r=[C.rps[b]], w=[r_mv])
                mq32 = C.sb("mq32", [128, 2, TL], F32)
                r_mq32 = C.res("mq32")
                mqsq = C.sb("mqsq", [128, 2, TL], BF16)
                r_mqsq = C.res("mqsq")
                mqn = C.sb("mqn", [128, 2, TL], BF16)
                r_mqn = C.res("mqn")
                qrs = C.sb("qrs", [128, TL], F32)
                r_qrs = C.res("qrs")
                pT = C.sb("pT", [128, 2, HT], BF16)
                r_pT = C.res("pT")
                rden = C.sb("rden", [128, HT], F32)
                r_rden = C.res("rden")
                for h in range(4):
                    for dt in range(2):
                        wt, rw = load_w(wmq_d[2 * h + dt])
                        gemm_fm(wt, rw, [0, 1], xn, r_xn, halves)
                        for hh, (s, n) in enumerate(halves):
                            P.cp(ACT, mq32[:, dt, s:s + n], C.ps[hh][:, :], r=[C.rps[hh]], w=[r_mq32])
                        P.act(mqsq[:, dt, :], mq32[:, dt, :], AF.Square, r=[r_mq32], w=[r_mqsq])
                    for hh, (s, n) in enumerate(halves):
                        for dt in range(2):
                            P.mm(C.ps[2][:, :], C.ones, mqsq[:, dt, s:s + n], dt == 0, dt == 1, r=[r_mqsq, C.r_cbf], w=[C.rps[2]])
                        P.act(qrs[:, s:s + n], C.ps[2][:, :], AF.Sqrt, r=[C.rps[2], r_prm], w=[r_qrs], bias=C.eps_ap, scale=1.0 / 256)
                    P.op(DVE, lambda e: e.reciprocal(qrs[:], qrs[:]), r=[r_qrs], w=[r_qrs])
                    for dt in range(2):
                        P.stt(DVE, mqn[:, dt, :], mq32[:, dt, :], prm[:, o_mq + dt:o_mq + dt + 1], qrs[:], ALU.mult, ALU.mult,
                              r=[r_mq32, r_qrs, r_prm], w=[r_mqn])
                    for hh, (s, n) in enumerate(halves):
                        for mt in range(2):
                            for dt in range(2):
                                P.mm(C.ps[3][:, :], mkn[:, 2 * h + dt, mt * 128:(mt + 1) * 128], mqn[:, dt, s:s + n], dt == 0, dt == 1,
                                     r=[r_mkn, r_mqn], w=[C.rps[3]])
                            P.act(pT[:, mt, :], C.ps[3][:, :], AF.Exp, r=[C.rps[3]], w=[r_pT], scale=1.0 / 16)
                        for mt in range(2):
                            P.mm(C.ps[4][:, :], C.ones, pT[:, mt, :], mt == 0, mt == 1, r=[r_pT, C.r_cbf], w=[C.rps[4]])
                        P.op(DVE, lambda e: e.reciprocal(rden[:], C.ps[4][:, :]), r=[C.rps[4]], w=[r_rden])
                        for dt in range(2):
                            b = 5 + dt
                            for mt in range(2):
                                P.mm(C.ps[b][:, :], mv[:, mt, (2 * h + dt) * 128:(2 * h + dt + 1) * 128], pT[:, mt, :], mt == 0, mt == 1,
                                     r=[r_mv, r_pT], w=[C.rps[b]])
                            P.tt(DVE, omem[:, 2 * h + dt, s:s + n], C.ps[b][:, :], rden[:], ALU.mult, r=[C.rps[b], r_rden], w=[r_omem])


    with C.phase():
        osb = C.sb("osb", [128, 12, TL], BF16)
        r_osb = C.res("osb")
        orw = C.sb("orw", [128, 12, TL], BF16)
        r_orw = C.res("orw")
        mrg = C.sb("mrg", [128, KC, TL], BF16)
        r_mrg = C.res("mrg")
        P.dma(SP, osb[:], osb_d, w=[r_osb])
        if "rw" in parts:
            with C.phase():
                def f32b(n):
                    return C.sb(n, [128, TL], F32), C.res(n)
                yt, r_yt = f32b("yt")
                gt, r_gt = f32b("gt")
                bt, r_bt = f32b("bt")
                dd, r_dd = f32b("dd")
                rs, r_rs = f32b("rs")
                ybf = C.sb("ybf", [128, TL], BF16)
                r_ybf = C.res("ybf")
                for ct in range(12):
                    cs = slice(ct * 128, (ct + 1) * 128)
                    P.dma(SP, yt[:], y_d[cs, :], w=[r_yt])
                    P.dma(SP, gt[:], g_d[cs, :], w=[r_gt])
                    P.dma(SP, bt[:], bon_d[cs, :], w=[r_bt])
                    P.cp(ACT, ybf[:], yt[:], r=[r_yt], w=[r_ybf])
                    for hh, (s, n) in enumerate(halves):
                        P.mm(C.ps[hh][:, :], C.bones, ybf[:, s:s + n], True, True, r=[r_ybf, C.r_cbf], w=[C.rps[hh]])
                        P.stt(DVE, dd[:, s:s + n], C.ps[hh][:, :], -1.0 / 64, yt[:, s:s + n], ALU.mult, ALU.add, r=[C.rps[hh], r_yt], w=[r_dd])
                    P.act(ybf[:], dd[:], AF.Square, r=[r_dd], w=[r_ybf])
                    for hh, (s, n) in enumerate(halves):
                        P.mm(C.ps[2 + hh][:, :], C.bones, ybf[:, s:s + n], True, True, r=[r_ybf, C.r_cbf], w=[C.rps[2 + hh]])
                        P.act(rs[:, s:s + n], C.ps[2 + hh][:, :], AF.Sqrt, r=[C.rps[2 + hh], r_prm], w=[r_rs], bias=gneps_ap, scale=1.0 / 64)
                    P.op(DVE, lambda e: e.reciprocal(rs[:], rs[:]), r=[r_rs], w=[r_rs])
                    P.tt(DVE, dd[:], dd[:], rs[:], ALU.mult, r=[r_dd, r_rs], w=[r_dd])
                    P.ts(DVE, dd[:], dd[:], prm[:, o_lng + ct:o_lng + ct + 1], prm[:, o_lnb + ct:o_lnb + ct + 1], ALU.mult, ALU.add,
                         r=[r_dd, r_prm], w=[r_dd])
                    P.tt(POOL, dd[:], dd[:], bt[:], ALU.add, r=[r_dd, r_bt], w=[r_dd])
                    P.tt(DVE, orw[:, ct, :], dd[:], gt[:], ALU.mult, r=[r_dd, r_gt], w=[r_orw])

        if "merge" in parts:
            with C.phase():
                sgt = [C.sb(f"sgt{i}", [128, 3, TL], BF16) for i in range(2)]
                rsgt = [C.res("sgt") for _ in range(2)]
                m1 = C.sb("m1", [128, TL], F32)
                r_m1 = C.res("m1")
                m2 = C.sb("m2", [128, TL], F32)
                r_m2 = C.res("m2")
                srcs = [(osb, r_osb, 0, 12), (orw, r_orw, 12, 12), (omem, r_omem, 24, 8)]
                for j in range(32):
                    i = j % 2
                    wt, rw = load_w(wu_d[j])
                    for br in range(3):
                        P.dma(SP, sgt[i][:, br, :], sg_s[br * 32 + j], r=[r_sg], w=[rsgt[i]])
                    for br, (src, r_src, c0, ncnk) in enumerate(srcs):
                        for hh, (s, n) in enumerate(halves):
                            b = 2 * br + hh
                            for c in range(ncnk):
                                P.mm(C.ps[b][:, :], wt[:, c0 + c, :], src[:, c, s:s + n], c == 0, c == ncnk - 1, r=[rw, r_src], w=[C.rps[b]])
                    for hh, (s, n) in enumerate(halves):
                        P.tt(DVE, m1[:, s:s + n], C.ps[0 + hh][:, :], sgt[i][:, 0, s:s + n], ALU.mult, r=[C.rps[hh], rsgt[i]], w=[r_m1])
                        P.tt(DVE, m2[:, s:s + n], C.ps[2 + hh][:, :], sgt[i][:, 1, s:s + n], ALU.mult, r=[C.rps[2 + hh], rsgt[i]], w=[r_m2])
                    P.tt(POOL, m1[:], m1[:], m2[:], ALU.add, r=[r_m1, r_m2], w=[r_m1])
                    for hh, (s, n) in enumerate(halves):
                        P.tt(DVE, m2[:, s:s + n], C.ps[4 + hh][:, :], sgt[i][:, 2, s:s + n], ALU.mult, r=[C.rps[4 + hh], rsgt[i], r_m2], w=[r_m2])
                    P.tt(POOL, mrg[:, j, :], m1[:], m2[:], ALU.add, r=[r_m1, r_m2], w=[r_mrg])

        ssq_banks = [6, 7]
        if "out" in parts:
            with C.phase():
                xs = [C.sb(f"xs{i}", [128, TL], F32) for i in range(2)]
                rxs = [C.res("xs") for _ in range(2)]
                hsq = [C.sb(f"hsq{i}", [128, TL], BF16) for i in range(2)]
                rhsq = [C.res("hsq") for _ in range(2)]
                for j in range(32):
                    i = j % 2
                    wt, rw = load_w(wo_d[j])
                    P.dma(SP, xs[i][:], xT[:, j, :], w=[rxs[i]])
                    mb = [0, 1] if i == 0 else [2, 3]
                    for hh, (s, n) in enumerate(halves):
                        for c in range(KC):
                            P.mm(C.ps[mb[hh]][:, :], wt[:, c, :], mrg[:, c, s:s + n], c == 0, c == KC - 1, r=[rw, r_mrg], w=[C.rps[mb[hh]]])
                        P.tt(DVE, xs[i][:, s:s + n], C.ps[mb[hh]][:, :], xs[i][:, s:s + n], ALU.add, r=[C.rps[mb[hh]], rxs[i]], w=[rxs[i]])
                    P.dma(SP, h1_s[j], xs[i][:], r=[rxs[i]], w=[r_h1], own="r")
                    P.act(hsq[i][:], xs[i][:], AF.Square, r=[rxs[i]], w=[rhsq[i]])
                    for hh, (s, n) in enumerate(halves):
                        P.mm(C.ps[ssq_banks[hh]][:, :], C.ones, hsq[i][:, s:s + n], j == 0, j == 31, r=[rhsq[i], C.r_cbf], w=[C.rps[ssq_banks[hh]]])

    if "ffn" in parts:
        with C.phase():
            rstd = C.sb("frstd", [128, TL], F32)
            r_rstd = C.res("frstd")
            for hh, (s, n) in enumerate(halves):
                P.act(rstd[:, s:s + n], C.ps[ssq_banks[hh]][:, :], AF.Sqrt, r=[C.rps[ssq_banks[hh]], r_prm], w=[r_rstd], bias=C.eps_ap, scale=1.0 / D)
            P.op(DVE, lambda e: e.reciprocal(rstd[:], rstd[:]), r=[r_rstd], w=[r_rstd])
            hn = C.sb("hn", [128, KC, HT], BF16)
            r_hn = C.res("hn")
            actT = C.sb("actT", [128, NFT, HT], BF16)
            r_act = C.res("actT")
            hld = [C.sb(f"hld{i}", [128, HT], F32) for i in range(2)]
            rhld = [C.res("hld") for _ in range(2)]
            sgl = [C.sb(f"sgl{i}", [128, HT], F32) for i in range(2)]
            rsgl = [C.res("sgl") for _ in range(2)]
            wdb = [C.sb(f"wdb{i}", [128, 43, 128], BF16) for i in range(3)]
            rwdb = [C.res("wdb") for _ in range(3)]
            wdc = [0]
            for hh, (s, n) in enumerate(halves):
                for j in range(32):
                    i = j % 2
                    P.dma(SP, hld[i][:], h1_s[j][:, s:s + n], r=[r_h1], w=[rhld[i]])
                    P.stt(DVE, hn[:, j, :], hld[i][:], g_ffn[:, j:j + 1], rstd[:, s:s + n], ALU.mult, ALU.mult, r=[rhld[i], r_rstd, r_prm], w=[r_hn])
                for f in range(NFT):
                    i = f % 2
                    gb, ub = (0, 1) if i == 0 else (2, 3)
                    wt, rw = load_w(wfg_d[f])
                    for c in range(KC):
                        P.mm(C.ps[gb][:, :], wt[:, c, :], hn[:, c, :], c == 0, c == KC - 1, r=[rw, r_hn], w=[C.rps[gb]])
                    wt, rw = load_w(wfu_d[f])
                    for c in range(KC):
                        P.mm(C.ps[ub][:, :], wt[:, c, :], hn[:, c, :], c == 0, c == KC - 1, r=[rw, r_hn], w=[C.rps[ub]])
                    P.act(sgl[i][:], C.ps[gb][:, :], AF.Silu, r=[C.rps[gb]], w=[rsgl[i]])
                    P.tt(DVE, actT[:, f, :], sgl[i][:], C.ps[ub][:, :], ALU.mult, r=[rsgl[i], C.rps[ub]], w=[r_act])
                for j in range(32):
                    i = j % 2
                    b = 4 + i
                    for fg in range(2):
                        k = wdc[0] % 3
                        wdc[0] += 1
                        P.dma(POOL, wdb[k][:], wfd_d[j, fg].rearrange("p (c m) -> p c m", m=128), w=[rwdb[k]])
                        for c in range(43):
                            P.mm(C.ps[b][:, :], wdb[k][:, c, :], actT[:, fg * 43 + c, :], fg == 0 and c == 0, fg == 1 and c == 42,
                                 r=[rwdb[k], r_act], w=[C.rps[b]])
                    P.dma(SP, hld[i][:], h1_s[j][:, s:s + n], r=[r_h1], w=[rhld[i]])
                    P.tt(DVE, hld[i][:], C.ps[b][:, :], hld[i][:], ALU.add, r=[C.rps[b], rhld[i]], w=[rhld[i]])
                    P.dma(SP, out_o[j * 128:(j + 1) * 128, s:s + n], hld[i][:], r=[rhld[i]], w=[r_out], own="r")
    return C.finish([r_out])


def prep_stage_c(inp, resA, resB):
    x = inp["x"][0]
    w_in = inp["w_in"][0]
    shared = {"cst": _consts()}
    kv = inp["mem_w_kv"][0]
    shared["wkvk"] = _tile_lhsT(kv[:, 0:1024])
    shared["wkvv"] = _tile_lhsT(kv[:, 1024:2048], 512)
    shared["wmq"] = _tile_lhsT(w_in[:, 9952:10976])
    wu = np.concatenate([inp["w_sb_o"][0], inp["w_rw_o"][0], inp["w_mem_o"][0]], axis=0)
    shared["wu"] = _tile_lhsT(wu)
    shared["wo"] = _tile_lhsT(inp["w_out"][0])
    shared["wfg"] = _tile_lhsT(inp["w_gate"][0])
    shared["wfu"] = _tile_lhsT(inp["w_up"][0])
    wd = inp["w_down"][0].reshape(2, 43, 128, 32, 128).transpose(3, 0, 2, 1, 4)
    shared["wfd"] = np.ascontiguousarray(wd).reshape(32, 2, 128, 43 * 128)
    shared["memT"] = np.ascontiguousarray(inp["mem"][0].T.reshape(KC, 128, NMEM).transpose(1, 0, 2))
    prm = np.zeros((128, NPC), np.float32)
    prm[:, 0:32] = _cols(inp["attn_norm_g"][0], 32)
    prm[:, 32:64] = _cols(inp["mem_norm_g"][0], 32)
    prm[:, 64:96] = _cols(inp["ffn_norm_g"][0], 32)
    prm[:, 96:98] = _cols(inp["mem_q_norm_g"][0], 2)
    prm[:, 98:100] = _cols(inp["mem_k_norm_g"][0], 2)
    prm[:, 100:112] = _cols(inp["rw_ln_g"][0], 12)
    prm[:, 112:124] = _cols(inp["rw_ln_b"][0], 12)
    prm[:, NPC - 2] = GN_EPS
    prm[:, NPC - 1] = EPS
    shared["prm"] = prm
    osb = np.zeros((12, 128, T), NPBF)
    y = np.zeros((RWW, T), np.float32)
    for c in range(NCORES):
        hf, hh, par = sb_units(c)
        osb[hf] = np.asarray(resB[c]["oF"])
        oh = np.asarray(resB[c]["oH"]).reshape(128, T // 256, 128)
        osb[hh].reshape(128, T // 256, 2, 128)[:, :, par, :] = oh
        for i in range(NHB):
            h = 3 * c + i
            y[h * 64:(h + 1) * 64] = np.asarray(resB[c]["y"][i])
    maps = []
    for c in range(NCORES):
        ts_ = slice(c * TL, (c + 1) * TL)
        m = dict(shared)
        m["xT"] = np.ascontiguousarray(x[ts_].T.reshape(KC, 128, TL).transpose(1, 0, 2))
        m["osb"] = np.ascontiguousarray(osb[:, :, ts_].transpose(1, 0, 2))
        m["y"] = np.ascontiguousarray(y[:, ts_])
        m["g"] = np.asarray(resA[c]["g"])
        m["bon"] = np.asarray(resA[c]["bon"])
        m["xn"] = np.asarray(resA[c]["xn"])
        m["sg"] = np.asarray(resB[c]["sg"])
        maps.append(m)
    return maps


def kernel(**inp):
    inp = {k: np.asarray(v) for k, v in inp.items()}
    ids = list(range(NCORES))
    resA = run_bass_kernel_spmd(build_stage_a(), prep_stage_a(inp), core_ids=ids).results
    resB = run_bass_kernel_spmd(build_stage_b(), prep_stage_b(resA, inp), core_ids=ids).results
    resC = run_bass_kernel_spmd(build_stage_c(), prep_stage_c(inp, resA, resB), core_ids=ids).results
    out = np.concatenate([np.asarray(r["outT"]).T for r in resC], axis=0)
    return np.ascontiguousarray(out.reshape(1, T, D).astype(np.float32))
```

```python
import numpy as np
import ml_dtypes
import concourse.bass as bass
import concourse.mybir as mybir
from concourse.bass_utils import run_bass_kernel_spmd

F32 = mybir.dt.float32
BF16 = mybir.dt.bfloat16
AF = mybir.ActivationFunctionType
ALU = mybir.AluOpType
AX = mybir.AxisListType
NPBF = ml_dtypes.bfloat16

PE, DVE, ACT, POOL, SP = "tensor", "vector", "scalar", "gpsimd", "sync"
ENGS = [PE, DVE, ACT, POOL, SP]

NCORES = 8
D = 4096
T = 8192
TL = T // NCORES
KC = D // 128
SBW, RWW, MEMW = 1536, 1536, 1024
RW_SEG = 5344
IN_COLS = 23264
DFF = 11008
NMEM = 256
EPS = 1e-6
GN_EPS = 64e-5


class Res:
    __slots__ = ("name", "last_w", "reads", "dsem", "dcnt", "ws")

    def __init__(self, name):
        self.name = name
        self.ws = []
        self.last_w = None
        self.reads = []
        self.dsem = None
        self.dcnt = 0


class Prog:
    def __init__(self, nc):
        self.nc = nc
        self.q = {e: [] for e in ENGS}
        self.seq = {e: 0 for e in ENGS}
        self.waited = {e: {} for e in ENGS}
        self.esem = {}
        self.nsem = 0
        self.dma_owners = []

    def _newsem(self, name):
        self.nsem += 1
        return self.nc.alloc_semaphore(f"s{self.nsem}_{name}")

    def _esem(self, eng):
        if eng not in self.esem:
            self.esem[eng] = self._newsem("e_" + eng)
        return self.esem[eng]

    def _deps(self, eng, r, w, dma_sem=None):
        deps = []
        for b in r:
            if b.last_w is not None:
                deps.append(b.last_w)
            deps.extend(b.ws)
        for b in w:
            if b.last_w is not None:
                if not (dma_sem is not None and b.last_w[0] is dma_sem):
                    deps.append(b.last_w)
            deps.extend(b.reads)
        wd = self.waited[eng]
        best = {}
        for (sem, val, src) in deps:
            if src == eng and eng == PE:
                continue
            k = id(sem)
            if wd.get(k, 0) >= val:
                continue
            if k not in best or best[k][1] < val:
                best[k] = (sem, val)
        for k, (sem, val) in best.items():
            wd[k] = val
        return list(best.values())

    def op(self, eng, fn, r=(), w=()):
        waits = self._deps(eng, r, w)
        sem = self._esem(eng)
        self.seq[eng] += 1
        ev = (sem, self.seq[eng], eng)
        for b in r:
            b.reads.append(ev)
        for b in w:
            b.last_w = ev
            b.reads = []
        self.q[eng].append((waits, fn, (sem, 1)))
        return ev

    def dma(self, eng, out, in_, r=(), w=(), own="w", **kw):
        owner = w[0] if own == "w" else r[0]
        if owner.dsem is None:
            owner.dsem = self._newsem("d_" + owner.name)
            self.dma_owners.append(owner)
        sem = owner.dsem
        waits = self._deps(eng, r, w, dma_sem=sem)
        owner.dcnt += 16
        ev = (sem, owner.dcnt, "dma")
        for b in r:
            b.reads.append(ev)
        for b in w:
            if own == "r":
                b.ws = [x for x in b.ws if x[0] is not sem] + [ev]
            else:
                b.last_w = ev
                b.reads = []
        self.q[eng].append((waits, lambda e: e.dma_start(out=out, in_=in_, **kw), (sem, 16)))
        return ev

    def barrier(self):
        targets = [(sem, self.seq[e]) for e, sem in self.esem.items() if self.seq[e] > 0]
        targets += [(o.dsem, o.dcnt) for o in self.dma_owners if o.dcnt > 0]
        for eng in ENGS:
            wd = self.waited[eng]
            waits = []
            for sem, val in targets:
                if wd.get(id(sem), 0) >= val:
                    continue
                wd[id(sem)] = val
                waits.append((sem, val))
            if waits:
                self.q[eng].append((waits, None, None))

    def allgather(self, in_ap, out_ap, r, w):
        sem = self._newsem("cc")
        waits = self._deps(POOL, r, w)
        ev = (sem, 1, "cc")
        for b in r:
            b.reads.append(ev)
        for b in w:
            b.last_w = ev
            b.reads = []
        self.q[POOL].append((waits, lambda e: e.collective_compute(
            "AllGather", ALU.bypass, replica_groups=[list(range(NCORES))], ins=[in_ap.opt()], outs=[out_ap.opt()]), (sem, 1)))
        self.q[POOL].append(([(sem, 1)], None, None))
        self.waited[POOL][id(sem)] = 1
        return ev

    def wait_all(self, eng, ress):
        deps = []
        for b in ress:
            if b.last_w is not None:
                deps.append((b.last_w[0], b.last_w[1]))
            deps.extend((x[0], x[1]) for x in b.ws)
        self.q[eng].append((deps, None, None))

    def replay(self, block):
        for eng in ENGS:
            items = self.q[eng]
            if not items:
                continue

            def body(e, items=items):
                for waits, fn, inc in items:
                    for sem, val in waits:
                        e.wait_ge(sem, val)
                    if fn is not None:
                        fn(e).then_inc(inc[0], inc[1])

            getattr(block, eng)(body)

    def mm(self, out, lhsT, rhs, start, stop, r, w):
        return self.op(PE, lambda e: e.matmul(out, lhsT=lhsT, rhs=rhs, start=start, stop=stop), r=r, w=w)

    def act(self, out, in_, func, r, w, bias=None, scale=None, eng=ACT):
        kw = {}
        if bias is not None:
            kw["bias"] = bias
        if scale is not None:
            kw["scale"] = scale
        return self.op(eng, lambda e: e.activation(out, in_, func, **kw), r=r, w=w)

    def tt(self, eng, out, a, b, op, r, w):
        return self.op(eng, lambda e: e.tensor_tensor(out, a, b, op), r=r, w=w)

    def ts(self, eng, out, a, s1, s2, op0, op1, r, w):
        if op1 is None:
            return self.op(eng, lambda e: e.tensor_scalar(out, a, s1, None, op0), r=r, w=w)
        return self.op(eng, lambda e: e.tensor_scalar(out, a, s1, s2, op0, op1), r=r, w=w)

    def stt(self, eng, out, a, s, b, op0, op1, r, w):
        return self.op(eng, lambda e: e.scalar_tensor_tensor(out, a, s, b, op0, op1), r=r, w=w)

    def cp(self, eng, out, in_, r, w):
        if eng == ACT:
            return self.op(eng, lambda e: e.copy(out, in_), r=r, w=w)
        return self.op(eng, lambda e: e.tensor_copy(out, in_), r=r, w=w)


class Ctx:
    def __init__(self, name):
        self.nc = bass.Bass("TRN2", target_bir_lowering=False)
        self.P = Prog(self.nc)
        self.nres = 0
        self.stacks = []
        nc = self.nc
        self.ps = [nc.alloc_psum_tensor(f"psb{i}", [128, 512], F32) for i in range(8)]
        self.rps = [Res(f"ps{i}") for i in range(8)]

    def res(self, name):
        self.nres += 1
        return Res(f"{name}{self.nres}")

    def sb(self, name, shape, dt):
        self.nres += 1
        nm = f"s_{name}_{self.nres}"
        if self.stacks:
            return self.stacks[-1].enter_context(self.nc.sbuf_tensor(nm, list(shape), dt))
        return self.nc.alloc_sbuf_tensor(nm, list(shape), dt)

    def phase(self):
        return _Phase(self)

    def din(self, name, shape, dt):
        return self.nc.dram_tensor(name, list(shape), dt, kind="ExternalInput").ap()

    def dout(self, name, shape, dt):
        return self.nc.dram_tensor(name, list(shape), dt, kind="ExternalOutput").ap()

    def dint(self, name, shape, dt):
        return self.nc.dram_tensor(name, list(shape), dt).ap()

    def finish(self, out_ress):
        self.P.wait_all(SP, out_ress)
        self.P.q[SP].append(([(o.dsem, o.dcnt) for o in self.P.dma_owners if o.dcnt > 0], None, None))
        with self.nc.Block() as block:
            self.P.replay(block)
        return self.nc


class _Phase:
    def __init__(self, C):
        self.C = C

    def __enter__(self):
        import contextlib
        self.st = contextlib.ExitStack()
        self.C.stacks.append(self.st)
        return self

    def __exit__(self, *a):
        self.C.P.barrier()
        self.C.stacks.pop()
        self.st.close()
        return False


def load_consts(C, cst_ap):
    P = C.P
    c32 = C.sb("c32", [128, 384], F32)
    cbf = C.sb("cbf", [128, 384], BF16)
    r32, rbf = C.res("c32"), C.res("cbf")
    P.dma(SP, c32[:], cst_ap, w=[r32])
    P.cp(DVE, cbf[:], c32[:], r=[r32], w=[rbf])
    C.c32, C.cbf, C.r_c32, C.r_cbf = c32, cbf, r32, rbf
    C.ones = cbf[:, 0:128]
    C.bones = cbf[:, 128:256]
    C.ident = cbf[:, 256:384]
    C.ident32 = c32[:, 256:384]


def rmsnorm_T(C, xT_ap, ntok, g_ap, r_g, out_bf, r_out, name):
    P = C.P
    nseg = [(s, min(512, ntok - s)) for s in range(0, ntok, 512)]
    assert len(nseg) <= 3
    xs = [C.sb(f"{name}_x{i}", [128, ntok], F32) for i in range(2)]
    rx = [C.res(f"{name}_x") for _ in range(2)]
    sq = [C.sb(f"{name}_sq{i}", [128, ntok], BF16) for i in range(2)]
    rsq = [C.res(f"{name}_sq") for _ in range(2)]
    rstd = C.sb(f"{name}_rstd", [128, ntok], F32)
    r_rstd = C.res(f"{name}_rstd")
    banks = [5, 6, 7][:len(nseg)]
    for c in range(KC):
        i = c % 2
        P.dma(SP, xs[i][:], xT_ap[:, c, :], w=[rx[i]])
        P.act(sq[i][:], xs[i][:], AF.Square, r=[rx[i]], w=[rsq[i]])
        for b, (s, n) in zip(banks, nseg):
            P.mm(C.ps[b][:, 0:n], C.ones, sq[i][:, s:s + n], c == 0, c == KC - 1, r=[rsq[i], C.r_cbf], w=[C.rps[b]])
    for b, (s, n) in zip(banks, nseg):
        P.act(rstd[:, s:s + n], C.ps[b][:, 0:n], AF.Sqrt, r=[C.rps[b], C.r_eps], w=[r_rstd], bias=C.eps_ap, scale=1.0 / D)
    P.op(DVE, lambda e: e.reciprocal(rstd[:], rstd[:]), r=[r_rstd], w=[r_rstd])
    for c in range(KC):
        i = c % 2
        P.dma(SP, xs[i][:], xT_ap[:, c, :], w=[rx[i]])
        P.stt(DVE, out_bf[:, c, :], xs[i][:], g_ap[:, c:c + 1], rstd[:], ALU.mult, ALU.mult, r=[rx[i], r_rstd, r_g], w=[r_out])


NPA = 32 + 2 + 42 + 5 * 12 + 1


def build_stage_a():
    C = Ctx("A")
    P, nc = C.P, C.nc
    NT = TL + 1
    xT = C.din("xT", [128, KC, NT], F32)
    prm_d = C.din("prm", [128, NPA], F32)
    cst_d = C.din("cst", [128, 384], F32)
    wqk_d = C.din("wqk", [24, 128, KC * 128], F32)
    wv_d = C.din("wv", [3, 128, KC * 512], F32)
    wrw_d = C.din("wrw", [42, 128, KC * 128], F32)
    lora_d = C.din("lora", [128, 6 * 1536], F32)
    qT_o = C.dout("qT", [12, 128, TL], BF16)
    kT_o = C.dout("kT", [12, 128, TL], BF16)
    v_o = C.dout("v", [TL, SBW], BF16)
    rwf_o = C.dout("rwf", [4, RWW, TL], BF16)
    rwv_o = C.dout("rwv", [TL, RWW], BF16)
    ld_o = C.dout("ld", [RWW, TL], F32)
    g_o = C.dout("g", [RWW, TL], F32)
    bon_o = C.dout("bon", [RWW, TL], F32)
    xn_o = C.dout("xn", [128, KC, TL], BF16)
    r_out = C.res("outs")

    load_consts(C, cst_d)
    prm = C.sb("prm", [128, NPA], F32)
    r_prm = C.res("prm")
    P.dma(SP, prm[:], prm_d, w=[r_prm])
    C.eps_ap = prm[:, NPA - 1:NPA]
    C.r_eps = r_prm
    g_attn = prm[:, 0:32]
    o_mix, o_w0, o_a0, o_kk, o_ka, o_rk = 34, 76, 88, 100, 112, 124

    xn = C.sb("xn", [128, KC, NT], BF16)
    r_xn = C.res("xn")
    NW = 3
    wb = [C.sb(f"wb{i}", [128, KC, 128], BF16) for i in range(NW)]
    rwb = [C.res("wb") for _ in range(NW)]
    with C.phase():
        rmsnorm_T(C, xT, NT, g_attn, r_prm, xn, r_xn, "na")
    P.dma(SP, xn_o, xn[:, :, 1:NT], r=[r_xn], w=[r_out], own="r")
    wcount = [0]

    def load_w(src):
        i = wcount[0] % NW
        wcount[0] += 1
        P.dma(POOL, wb[i][:], src.rearrange("p (c m) -> p c m", m=128), w=[rwb[i]])
        return wb[i], rwb[i]

    def gemm_fm(wt, rw, banks, segs):
        for b, (s, n) in zip(banks, segs):
            for c in range(KC):
                P.mm(C.ps[b][:, 0:n], wt[:, c, :], xn[:, c, s:s + n], c == 0, c == KC - 1, r=[rw, r_xn], w=[C.rps[b]])

    seg_main = [(1, 512), (513, 512)]
    seg_halo = [(0, 1)]

    with C.phase():
        qk32 = [C.sb(f"qk32_{i}", [128, TL], F32) for i in range(2)]
        rqk32 = [C.res("qk32") for _ in range(2)]
        qksq = [C.sb(f"qksq_{i}", [128, TL], BF16) for i in range(2)]
        rqksq = [C.res("qksq") for _ in range(2)]
        qkr = [C.sb(f"qkr_{i}", [128, TL], F32) for i in range(2)]
        rqkr = [C.res("qkr") for _ in range(2)]
        qkb = [C.sb(f"qkb_{i}", [128, TL], BF16) for i in range(2)]
        rqkb = [C.res("qkb") for _ in range(2)]
        def qk_gemm(j):
            wt, rw = load_w(wqk_d[j])
            gemm_fm(wt, rw, [0, 1] if j % 2 == 0 else [2, 3], seg_main)

        def qk_post(j):
            i = j % 2
            mb = [0, 1] if i == 0 else [2, 3]
            for h, b in enumerate(mb):
                P.cp(ACT, qk32[i][:, h * 512:(h + 1) * 512], C.ps[b][:, :], r=[C.rps[b]], w=[rqk32[i]])
            P.act(qksq[i][:], qk32[i][:], AF.Square, r=[rqk32[i]], w=[rqksq[i]])
            for h in range(2):
                P.mm(C.ps[4][:, :], C.ones, qksq[i][:, h * 512:(h + 1) * 512], True, True, r=[rqksq[i], C.r_cbf], w=[C.rps[4]])
                P.act(qkr[i][:, h * 512:(h + 1) * 512], C.ps[4][:, :], AF.Sqrt, r=[C.rps[4], r_prm], w=[rqkr[i]], bias=C.eps_ap, scale=1.0 / 128)
            P.op(DVE, lambda e, i=i: e.reciprocal(qkr[i][:], qkr[i][:]), r=[rqkr[i]], w=[rqkr[i]])
            gcol = prm[:, 32:33] if j < 12 else prm[:, 33:34]
            P.stt(DVE, qkb[i][:], qk32[i][:], gcol, qkr[i][:], ALU.mult, ALU.mult, r=[rqk32[i], rqkr[i], r_prm], w=[rqkb[i]])
            dst = qT_o[j] if j < 12 else kT_o[j - 12]
            P.dma(SP, dst, qkb[i][:], r=[rqkb[i]], w=[r_out], own="r")

        qk_gemm(0)
        for j in range(24):
            if j + 1 < 24:
                qk_gemm(j + 1)
            qk_post(j)

    with C.phase():
        wvbs = [C.sb(f"wvb{i}", [128, KC, 512], BF16) for i in range(2)]
        rwvbs = [C.res("wvb") for _ in range(2)]
        vst = [C.sb(f"vst{i}", [128, 512], BF16) for i in range(2)]
        rvst = [C.res("vst") for _ in range(2)]
        cnt = 0
        P.dma(POOL, wvbs[0][:], wv_d[0].rearrange("p (c m) -> p c m", m=512), w=[rwvbs[0]])
        for gcol in range(3):
            wvb, rwvb = wvbs[gcol % 2], rwvbs[gcol % 2]
            if gcol + 1 < 3:
                P.dma(POOL, wvbs[(gcol + 1) % 2][:], wv_d[gcol + 1].rearrange("p (c m) -> p c m", m=512), w=[rwvbs[(gcol + 1) % 2]])
            for tt in range(8):
                b = cnt % 4
                i = cnt % 2
                cnt += 1
                for c in range(KC):
                    P.mm(C.ps[b][:, :], xn[:, c, 1 + tt * 128:1 + (tt + 1) * 128], wvb[:, c, :], c == 0, c == KC - 1, r=[rwvb, r_xn], w=[C.rps[b]])
                P.cp(ACT if cnt % 2 else DVE, vst[i][:], C.ps[b][:, :], r=[C.rps[b]], w=[rvst[i]])
                P.dma(SP, v_o[tt * 128:(tt + 1) * 128, gcol * 512:(gcol + 1) * 512], vst[i][:], r=[rvst[i]], w=[r_out], own="r")

    rwph = C.phase()
    rwph.__enter__()
    lorab = C.sb("lorab", [128, 6, 1536], BF16)
    r_lorab = C.res("lorab")
    with C.phase():
        lora32 = C.sb("lora32", [128, 1536], F32)
        r_l32 = C.res("l32")
        for q in range(6):
            P.dma(SP, lora32[:], lora_d[:, q * 1536:(q + 1) * 1536], w=[r_l32])
            P.cp(DVE, lorab[:, q, :], lora32[:], r=[r_l32], w=[r_lorab])

    seg32 = [C.sb(f"seg32_{i}", [128, NT], F32) for i in range(2)]
    rseg = [C.res("seg") for _ in range(2)]
    tmp = [C.sb(f"tsd_{i}", [128, TL], F32) for i in range(2)]
    rtmp = [C.res("tsd") for _ in range(2)]
    scount = [0]

    def rw_tile(jt, out_ap, r_o, post=None):
        wt, rw = load_w(wrw_d[jt])
        i = scount[0] % 2
        scount[0] += 1
        mb = [0, 1] if i == 0 else [2, 3]
        gemm_fm(wt, rw, mb, seg_main)
        gemm_fm(wt, rw, [4], seg_halo)
        P.cp(ACT, seg32[i][:, 1:513], C.ps[mb[0]][:, :], r=[C.rps[mb[0]]], w=[rseg[i]])
        P.cp(ACT, seg32[i][:, 513:1025], C.ps[mb[1]][:, :], r=[C.rps[mb[1]]], w=[rseg[i]])
        P.cp(DVE, seg32[i][:, 0:1], C.ps[4][:, 0:1], r=[C.rps[4]], w=[rseg[i]])
        P.tt(DVE, tmp[i][:], seg32[i][:, 0:TL], seg32[i][:, 1:NT], ALU.subtract, r=[rseg[i]], w=[rtmp[i]])
        P.stt(DVE, out_ap, tmp[i][:], prm[:, o_mix + jt:o_mix + jt + 1], seg32[i][:, 1:NT], ALU.mult, ALU.add,
              r=[rtmp[i], rseg[i], r_prm], w=[r_o])

    linb = C.sb("linb", [128, 6, TL], BF16)
    r_linb = C.res("linb")
    with C.phase():
        lin = C.sb("lin", [128, 6, TL], F32)
        r_lin = C.res("lin")
        for q in range(6):
            rw_tile(36 + q, lin[:, q, :], r_lin)
        P.act(linb[:, 0, :], lin[:, 0, :], AF.Tanh, r=[r_lin], w=[r_linb])
        P.cp(DVE, linb[:, 1, :], lin[:, 1, :], r=[r_lin], w=[r_linb])
        for q in range(2, 6):
            P.act(linb[:, q, :], lin[:, q, :], AF.Sigmoid, r=[r_lin], w=[r_linb])

    def f32buf(n):
        return C.sb(n, [128, TL], F32), C.res(n)

    def bfbuf(n):
        return C.sb(n, [128, TL], BF16), C.res(n)

    def dbl(fn, n):
        return [fn(f"{n}{i}") for i in range(2)]
    rr2, kk2, vv2, aa2 = dbl(f32buf, "rr"), dbl(f32buf, "kk"), dbl(f32buf, "vv"), dbl(f32buf, "aa")
    ld1, gt1 = f32buf("ldt"), f32buf("gt")
    ld2, gt2 = [ld1, ld1], [gt1, gt1]
    t1, r_t1 = f32buf("t1")
    t2, r_t2 = f32buf("t2")
    bon, r_bon = f32buf("bon")
    ob = [bfbuf(f"ob{i}") for i in range(5)]
    sqb, r_sqb = bfbuf("sqb")
    vT = [C.sb(f"vT{i}", [128, 128], BF16) for i in range(2)]
    rvT = [C.res("vT") for _ in range(2)]

    def bufs_for(ct):
        i = ct % 2
        return rr2[i], kk2[i], vv2[i], aa2[i], ld2[i], gt2[i]

    def rw_part1(ct):
        (rr, r_rr), (kk_, r_k), (vv, r_v), (aa, r_a), (ldt, r_ld), (gt, r_g) = bufs_for(ct)
        cs = slice(ct * 128, (ct + 1) * 128)
        cs = slice(ct * 128, (ct + 1) * 128)
        for h in range(2):
            hs = slice(h * 512, (h + 1) * 512)
            P.mm(C.ps[5][:, :], lorab[:, 0, cs], linb[:, 0, hs], True, True, r=[r_lorab, r_linb], w=[C.rps[5]])
            P.act(ldt[:, hs], C.ps[5][:, :], AF.Sigmoid, r=[C.rps[5], r_prm], w=[r_ld], bias=prm[:, o_w0 + ct:o_w0 + ct + 1], scale=1.0)
            P.mm(C.ps[6][:, :], lorab[:, 1, cs], linb[:, 1, hs], True, True, r=[r_lorab, r_linb], w=[C.rps[6]])
            P.act(aa[:, hs], C.ps[6][:, :], AF.Sigmoid, r=[C.rps[6], r_prm], w=[r_a], bias=prm[:, o_a0 + ct:o_a0 + ct + 1], scale=1.0)
            for q in range(4):
                P.mm(C.ps[7][:, :], lorab[:, 2 + q, cs], linb[:, 2 + q, hs], q == 0, q == 3, r=[r_lorab, r_linb], w=[C.rps[7]])
            P.cp(DVE, gt[:, hs], C.ps[7][:, :], r=[C.rps[7]], w=[r_g])
        P.ts(DVE, ldt[:], ldt[:], -float(np.exp(-0.5)), None, ALU.mult, None, r=[r_ld], w=[r_ld])
        P.dma(SP, ld_o[cs, :], ldt[:], r=[r_ld], w=[r_out], own="r")
        P.dma(SP, g_o[cs, :], gt[:], r=[r_g], w=[r_out], own="r")
        rw_tile(ct, rr[:], r_rr)
        rw_tile(12 + ct, kk_[:], r_k)
        rw_tile(24 + ct, vv[:], r_v)

    def rw_part2(ct):
        (rr, r_rr), (kk_, r_k), (vv, r_v), (aa, r_a), (ldt, r_ld), (gt, r_g) = bufs_for(ct)
        cs = slice(ct * 128, (ct + 1) * 128)
        P.cp(ACT, ob[0][0][:], rr[:], r=[r_rr], w=[ob[0][1]])
        P.dma(SP, rwf_o[0, cs, :], ob[0][0][:], r=[ob[0][1]], w=[r_out], own="r")
        P.ts(DVE, t1[:], kk_[:], prm[:, o_kk + ct:o_kk + ct + 1], None, ALU.mult, None, r=[r_k, r_prm], w=[r_t1])
        P.act(sqb[:], t1[:], AF.Square, r=[r_t1], w=[r_sqb])
        for h in range(2):
            hs = slice(h * 512, (h + 1) * 512)
            P.mm(C.ps[5][:, :], C.bones, sqb[:, hs], True, True, r=[r_sqb, C.r_cbf], w=[C.rps[5]])
            P.act(t2[:, hs], C.ps[5][:, :], AF.Sqrt, r=[C.rps[5]], w=[r_t2])
        P.ts(DVE, t2[:], t2[:], 1e-12, None, ALU.max, None, r=[r_t2], w=[r_t2])
        P.op(DVE, lambda e: e.reciprocal(t2[:], t2[:]), r=[r_t2], w=[r_t2])
        P.tt(DVE, t1[:], t1[:], t2[:], ALU.mult, r=[r_t1, r_t2], w=[r_t1])
        P.ts(DVE, ob[2][0][:], t1[:], -1.0, None, ALU.mult, None, r=[r_t1], w=[ob[2][1]])
        P.tt(DVE, ob[3][0][:], t1[:], aa[:], ALU.mult, r=[r_t1, r_a], w=[ob[3][1]])
        P.dma(SP, rwf_o[2, cs, :], ob[2][0][:], r=[ob[2][1]], w=[r_out], own="r")
        P.dma(SP, rwf_o[3, cs, :], ob[3][0][:], r=[ob[3][1]], w=[r_out], own="r")
        P.ts(DVE, t2[:], aa[:], -1.0, prm[:, o_ka + ct:o_ka + ct + 1], ALU.add, ALU.mult, r=[r_a, r_prm, r_t2], w=[r_t2])
        P.stt(DVE, kk_[:], t2[:], 1.0, kk_[:], ALU.add, ALU.mult, r=[r_t2, r_k], w=[r_k])
        P.cp(ACT, ob[1][0][:], kk_[:], r=[r_k], w=[ob[1][1]])
        P.dma(SP, rwf_o[1, cs, :], ob[1][0][:], r=[ob[1][1]], w=[r_out], own="r")
        P.stt(DVE, t1[:], rr[:], prm[:, o_rk + ct:o_rk + ct + 1], kk_[:], ALU.mult, ALU.mult, r=[r_rr, r_k, r_prm, r_t1], w=[r_t1])
        P.cp(ACT, sqb[:], t1[:], r=[r_t1], w=[r_sqb])
        for h in range(2):
            hs = slice(h * 512, (h + 1) * 512)
            P.mm(C.ps[6][:, :], C.bones, sqb[:, hs], True, True, r=[r_sqb, C.r_cbf], w=[C.rps[6]])
            P.tt(DVE, bon[:, hs], C.ps[6][:, :], vv[:, hs], ALU.mult, r=[C.rps[6], r_v], w=[r_bon])
        P.dma(SP, bon_o[cs, :], bon[:], r=[r_bon], w=[r_out], own="r")
        P.cp(ACT, ob[4][0][:], vv[:], r=[r_v], w=[ob[4][1]])
        for tt in range(8):
            i = tt % 2
            P.op(PE, lambda e, tt=tt: e.transpose(C.ps[7][:, 0:64].bitcast(BF16), ob[4][0][:, tt * 128:(tt + 1) * 128], C.ident),
                 r=[ob[4][1], C.r_cbf], w=[C.rps[7]])
            P.cp(DVE, vT[i][:], C.ps[7][:, 0:64].bitcast(BF16), r=[C.rps[7]], w=[rvT[i]])
            P.dma(SP, rwv_o[tt * 128:(tt + 1) * 128, cs], vT[i][:], r=[rvT[i]], w=[r_out], own="r")


    rw_part1(0)
    for ct in range(12):
        if ct + 1 < 12:
            rw_part1(ct + 1)
        rw_part2(ct)
    rwph.__exit__(None, None, None)
    return C.finish([r_out])


def _tile_lhsT(w, ncols=128):
    K, M = w.shape
    assert K % 128 == 0 and M % ncols == 0
    kc = K // 128
    a = w.reshape(kc, 128, M // ncols, ncols).transpose(2, 1, 0, 3)
    return np.ascontiguousarray(a).reshape(M // ncols, 128, kc * ncols)


def _pad_cols(w, m):
    if w.shape[1] == m:
        return w
    out = np.zeros((w.shape[0], m), w.dtype)
    out[:, :w.shape[1]] = w
    return out


def _cols(v, n):
    return np.ascontiguousarray(v.reshape(n, 128).T)


def _consts():
    c = np.zeros((128, 384), np.float32)
    c[:, 0:128] = 1.0
    c[0:64, 128:192] = 1.0
    c[64:128, 192:256] = 1.0
    c[:, 256:384] = np.eye(128, dtype=np.float32)
    return c


def prep_stage_a(inp):
    x = inp["x"][0]
    w_in = inp["w_in"][0]
    shared = {}
    shared["cst"] = _consts()
    shared["wqk"] = _tile_lhsT(w_in[:, 0:3072])
    shared["wv"] = _tile_lhsT(w_in[:, 3072:4608], 512)
    shared["wrw"] = _tile_lhsT(_pad_cols(w_in[:, 4608:4608 + RW_SEG], 42 * 128))
    lora = np.zeros((128, 6 * 1536), np.float32)
    lora[:, 0:1536] = inp["rw_w_up"][0]
    lora[:, 1536:3072] = inp["rw_a_up"][0]
    gup = np.zeros((512, 1536), np.float32)
    gup[:480] = inp["rw_g_up"][0]
    for q in range(4):
        lora[:, (2 + q) * 1536:(3 + q) * 1536] = gup[q * 128:(q + 1) * 128]
    shared["lora"] = lora
    prm = np.zeros((128, NPA), np.float32)
    prm[:, 0:32] = _cols(inp["attn_norm_g"][0], 32)
    prm[:, 32] = inp["sb_q_norm_g"][0]
    prm[:, 33] = inp["sb_k_norm_g"][0]
    mix = np.zeros(42 * 128, np.float32)
    mix[:RW_SEG] = inp["rw_mix"][0]
    prm[:, 34:76] = _cols(mix, 42)
    prm[:, 76:88] = _cols(inp["rw_w0"][0], 12)
    prm[:, 88:100] = _cols(inp["rw_a0"][0], 12)
    prm[:, 100:112] = _cols(inp["rw_k_k"][0], 12)
    prm[:, 112:124] = _cols(inp["rw_k_a"][0], 12)
    prm[:, 124:136] = _cols(inp["rw_r_k"][0].reshape(-1), 12)
    prm[:, NPA - 1] = EPS
    shared["prm"] = prm
    maps = []
    for c in range(NCORES):
        xc = np.zeros((TL + 1, D), np.float32)
        lo = c * TL - 1
        if c == 0:
            xc[1:] = x[0:TL]
        else:
            xc[:] = x[lo:lo + TL + 1]
        xT = np.ascontiguousarray(xc.T.reshape(KC, 128, TL + 1).transpose(1, 0, 2))
        m = dict(shared)
        m["xT"] = xT
        maps.append(m)
    return maps


CH = 64
SCL = 512
NCH = SCL // CH
NHB = 3


def build_stage_b(do_sb=True, do_rw=True, nsc=T // SCL, ngl=None, do_gates=True):
    C = Ctx("B")
    P, nc = C.P, C.nc
    cst_d = C.din("cst", [128, 384], F32)
    cst2_d = C.din("cst2", [128, 512], F32)
    qF_d = C.din("qF", [128, T], BF16)
    kF_d = C.din("kF", [128, T], BF16)
    vF_d = C.din("vF", [T, 128], BF16)
    qH_d = C.din("qH", [128, T // 2], BF16)
    kH_d = C.din("kH", [128, T], BF16)
    vH_d = C.din("vH", [T, 128], BF16)
    mF_d = C.din("mF", [128, 4, 512], BF16)
    mH_d = C.din("mH", [128, 8, 512], BF16)
    rwf_d = C.din("rwf", [NHB, 64, 4, T], BF16)
    ld_d = C.din("ld", [NHB, 64, T], F32)
    rwv_d = C.din("rwv", [NHB, T, 64], BF16)
    xn_d = C.din("xn", [128, KC, TL], BF16)
    wg_d = C.din("wg", [96, 128, KC * 128], F32)
    sg_o = C.dout("sg", [96, 128, TL], BF16)
    oF_o = C.dout("oF", [128, T], BF16)
    oH_o = C.dout("oH", [128, T // 2], BF16)
    y_o = C.dout("y", [NHB, 64, T], F32)
    r_out = C.res("outs")

    load_consts(C, cst_d)
    c2 = C.sb("c2", [128, 512], F32)
    c2b = C.sb("c2b", [128, 512], BF16)
    r_c2, r_c2b = C.res("c2"), C.res("c2b")
    P.dma(SP, c2[:], cst2_d, w=[r_c2])
    P.cp(DVE, c2b[:], c2[:], r=[r_c2], w=[r_c2b])
    tri, trip = c2b[:, 0:128], c2b[:, 128:256]
    maskM = c2[:, 256:384]
    maskN = c2[0:64, 384:448]
    one_col = C.c32[:, 0:1]

    slots = [(C.ps[b][:, 0:128], C.rps[b]) for b in (5, 6, 7)]
    sl_i = [0]

    def slot():
        s = slots[sl_i[0] % len(slots)]
        sl_i[0] += 1
        return s

    sbshare = {}

    def gates_chain(per_yield=14):
        qk, v = sbshare["qk"], sbshare["v"]
        r_q, r_k, r_v = sbshare["r"]
        ost, r_ost = sbshare["ost"], sbshare["r_ost"]
        xnh = qk[:, :].rearrange("p (c t) -> p c t", c=KC)
        r_xnh = C.res("xnh")
        wbs = [v[:, 0:KC, :], v[:, KC:2 * KC, :]]
        rwbs = [C.res("gwb") for _ in range(2)]
        cnt = 0
        wi = 0
        for hh in range(2):
            P.dma(SP, xnh, xn_d[:, :, hh * 512:(hh + 1) * 512], w=[r_xnh, r_q, r_k])
            for j in range(96):
                i = wi % 2
                P.dma(POOL, wbs[i], wg_d[j].rearrange("p (c m) -> p c m", m=128), w=[rwbs[i], r_v] if wi < 2 else [rwbs[i]])
                wi += 1
                b = j % 4
                for c in range(KC):
                    P.mm(C.ps[b][:, :], wbs[i][:, c, :], xnh[:, c, :], c == 0, c == KC - 1, r=[rwbs[i], r_xnh], w=[C.rps[b]])
                    cnt += 1
                    if cnt % per_yield == 0:
                        yield
                io = j % 2
                P.act(ost[io][:], C.ps[b][:, :], AF.Sigmoid, r=[C.rps[b]], w=[r_ost[io]])
                P.dma(SP, sg_o[j][:, hh * 512:(hh + 1) * 512], ost[io][:], r=[r_ost[io]], w=[r_out], own="r")

    def sb_pipeline(units):
        ZB, RB, OB = [0, 1], [2, 3], 4
        qk = C.sb("qk", [128, 2 * T], BF16)
        q, k = qk[:, 0:T], qk[:, T:2 * T]
        v = C.sb("v", [128, T // 128, 128], BF16)
        r_q, r_k, r_v = C.res("q"), C.res("k"), C.res("v")
        sbshare.update(qk=qk, v=v, r=(r_q, r_k, r_v))
        NE, NS, NX, NWT = 6, 4, 3, 3
        e32 = [C.sb(f"e32_{i}", [128, 512], F32) for i in range(NE)]
        re32 = [C.res("e32") for _ in range(NE)]
        sp = [C.sb(f"sp_{i}", [128, 512], BF16) for i in range(NS)]
        rsp = [C.res("sp") for _ in range(NS)]
        ex = [C.sb(f"ex_{i}", [128, 512], F32) for i in range(NX)]
        rex = [C.res("ex") for _ in range(NX)]
        wt = [C.sb(f"w_{i}", [128, 512], BF16) for i in range(NWT)]
        rwt = [C.res("w") for _ in range(NWT)]
        ost = [C.sb(f"ost{i}", [128, 512], BF16) for i in range(2)]
        r_ost = [C.res("ost") for _ in range(2)]
        sbshare.update(ost=ost, r_ost=r_ost)
        sacc = C.sb("sacc", [128, 512], F32)
        r_sacc = C.res("sacc")
        steps = []
        for ui, (q_d, k_d, v_d, m_sb, nq_blocks, kpg, o_d) in enumerate(units):
            ngroups = nq_blocks // 4 if ngl is None else ngl
            for g in range(ngroups):
                nkb = kpg * (g + 1)
                for idx, kb in enumerate(reversed(range(nkb))):
                    steps.append(dict(u=ui, g=g, kb=kb, first=idx == 0, last=idx == nkb - 1, mj=kb - (nkb - kpg)))
        ns = len(steps)
        gcount = [0]

        def load_unit(ui):
            q_d, k_d, v_d, m_sb, nq_blocks, kpg, o_d = units[ui]
            P.dma(SP, q[:, 0:nq_blocks * 128], q_d, w=[r_q])
            P.dma(SP, k[:], k_d, w=[r_k])
            P.dma(SP, v[:], v_d.rearrange("(b p) d -> p b d", p=128), w=[r_v])

        def st1(i):
            st = steps[i]
            if i == 0 or steps[i - 1]["u"] != st["u"]:
                load_unit(st["u"])
            zb = ZB[i % 2]
            P.mm(C.ps[zb][:, :], k[:, st["kb"] * 128:(st["kb"] + 1) * 128], q[:, st["g"] * 512:(st["g"] + 1) * 512], True, True,
                 r=[r_k, r_q], w=[C.rps[zb]])

        def st2(i):
            st = steps[i]
            zb = ZB[i % 2]
            ie, is_ = i % NE, i % NS
            P.act(e32[ie][:], C.ps[zb][:, :], AF.Exp, r=[C.rps[zb]], w=[re32[ie]], scale=float(128 ** -0.5))
            P.act(sp[is_][:], e32[ie][:], AF.Ln, r=[re32[ie], C.r_c32], w=[rsp[is_]], bias=one_col, scale=1.0)
            if st["mj"] >= 0:
                m_sb = units[st["u"]][3]
                P.tt(POOL, sp[is_][:], sp[is_][:], m_sb[:, st["mj"], :], ALU.mult, r=[rsp[is_], r_m], w=[rsp[is_]])

        def st3(i):
            st = steps[i]
            rb = RB[i % 2]
            is_ = i % NS
            P.mm(C.ps[rb][:, :], tri, sp[is_][:], True, st["first"], r=[rsp[is_], r_c2b], w=[C.rps[rb]])
            if not st["first"]:
                P.mm(C.ps[rb][:, :], C.c32[:, 0:128], sacc[:], False, True, r=[r_sacc, C.r_c32], w=[C.rps[rb]])

        def st4(i):
            st = steps[i]
            rb = RB[i % 2]
            is_, ix = i % NS, i % NX
            P.act(ex[ix][:], C.ps[rb][:, :], AF.Exp, r=[C.rps[rb]], w=[rex[ix]], scale=-1.0)
            if not st["last"]:
                if st["first"]:
                    P.cp(POOL, sacc[:], sp[is_][:], r=[rsp[is_]], w=[r_sacc])
                else:
                    P.tt(POOL, sacc[:], sacc[:], sp[is_][:], ALU.add, r=[rsp[is_], r_sacc], w=[r_sacc])

        def st5(i):
            st = steps[i]
            ie, ix, iw = i % NE, i % NX, i % NWT
            P.tt(DVE, wt[iw][:], e32[ie][:], ex[ix][:], ALU.mult, r=[re32[ie], rex[ix]], w=[rwt[iw]])
            if st["mj"] >= 0:
                m_sb = units[st["u"]][3]
                P.tt(POOL, wt[iw][:], wt[iw][:], m_sb[:, st["mj"], :], ALU.mult, r=[rwt[iw], r_m], w=[rwt[iw]])

        def st6(i):
            st = steps[i]
            iw = i % NWT
            P.mm(C.ps[OB][:, :], v[:, st["kb"], :], wt[iw][:], st["first"], st["last"], r=[r_v, rwt[iw]], w=[C.rps[OB]])
            if st["last"]:
                o_d = units[st["u"]][6]
                io = gcount[0] % 2
                gcount[0] += 1
                P.cp(DVE, ost[io][:], C.ps[OB][:, :], r=[C.rps[OB]], w=[r_ost[io]])
                P.dma(SP, o_d[:, st["g"] * 512:(st["g"] + 1) * 512], ost[io][:], r=[r_ost[io]], w=[r_out], own="r")

        stages = [st1, st2, st3, st4, st5, st6]
        bounds = [0] + [i for i in range(1, ns) if steps[i]["u"] != steps[i - 1]["u"]] + [ns]
        for lo, hi in zip(bounds[:-1], bounds[1:]):
            for t in range(lo, hi + len(stages) - 1):
                for kk in reversed(range(len(stages))):
                    i = t - kk
                    if lo <= i < hi:
                        stages[kk](i)
                yield

    def rw_chain():
        H3 = NHB
        BA, BB, BC = 5, 6, 7
        pA, rA = C.ps[BA], C.rps[BA]
        pB, rB = C.ps[BB], C.rps[BB]
        pC, rC = C.ps[BC], C.rps[BC]
        S32 = C.sb("S32", [64, H3, 64], F32)
        Sbf = C.sb("Sbf", [64, H3, 64], BF16)
        rS = C.res("S")
        P.op(DVE, lambda e: e.memset(S32[:], 0.0), w=[rS])
        P.op(DVE, lambda e: e.memset(Sbf[:], 0.0), w=[rS])
        msk = C.sb("rmsk", [64, H3 * SCL], F32)
        r_msk = C.res("rmsk")
        P.op(DVE, lambda e: e.memset(msk[:], 1.0), w=[r_msk])
        P.op(DVE, lambda e: e.memset(msk[:].rearrange("p (n c) -> p n c", c=CH)[:, :, 0:1], 0.0), w=[r_msk])
        mM3 = C.sb("mM3", [128, H3, 128], F32)
        mN3 = C.sb("mN3", [64, H3, 64], F32)
        id3 = C.sb("id3", [64, H3, 64], F32)
        r_k3 = C.res("k3")
        for h in range(H3):
            P.cp(DVE, mM3[:, h, :], maskM, r=[r_c2], w=[r_k3])
            P.cp(DVE, mN3[:, h, :], maskN, r=[r_c2], w=[r_k3])
            P.cp(DVE, id3[:, h, :], C.ident32[0:64, 0:64], r=[C.r_c32], w=[r_k3])
        fm = C.sb("fm", [64, H3, 4, SCL], BF16)
        ldb = C.sb("ldb", [64, H3 * SCL], F32)
        UVs = [C.sb(f"UV{i}", [128, NCH, H3, 64], BF16) for i in range(2)]
        r_fm, r_ld = C.res("fm"), C.res("ld")
        r_Vs, r_Us = [C.res("V") for _ in range(2)], [C.res("U") for _ in range(2)]
        cur_sc = [0]
        cum = C.sb("cum", [64, H3 * SCL], F32)
        tA = C.sb("tA", [64, H3 * SCL], F32)
        tB = C.sb("tB", [64, H3 * SCL], F32)
        r_cum, r_tA, r_tB = C.res("cum"), C.res("tA"), C.res("tB")
        PC = C.sb("PC", [64, H3, NCH], F32)
        r_PC = C.res("PC")
        Q2 = C.sb("Q2", [64, H3, NCH, 2, CH], BF16)
        KB = C.sb("KB", [64, H3, NCH, 2, CH], BF16)
        KBb = C.sb("KBb", [64, H3, NCH, 2, CH], BF16)
        r_Q2, r_KB, r_KBb = C.res("Q2"), C.res("KB"), C.res("KBb")
        Y = C.sb("Y", [64, H3, SCL], F32)
        r_Y = C.res("Y")

        def wb2(name, shape, dt):
            return [C.sb(f"{name}_{i}", shape, dt) for i in range(2)], [C.res(name) for i in range(2)]
        Mm, r_Mm = wb2("Mm", [128, H3, 128], BF16)
        LTa, r_LTa = wb2("LTa", [64, H3, 128], F32)
        LTb, r_LTb = wb2("LTb", [64, H3, 128], F32)
        Na, r_Na = wb2("Na", [64, H3, 64], F32)
        Nb, r_Nb = wb2("Nb", [64, H3, 64], F32)
        Tt, r_Tt = wb2("Tt", [64, H3, 64], BF16)
        W2, r_W2 = wb2("W2", [64, H3, 64], F32)
        Xs, r_Xs = wb2("Xs", [64, H3, 64], BF16)
        KT, r_KT = wb2("KT", [128, H3, 64], BF16)

        def v4(ap):
            return ap.rearrange("p (h n c) -> p h n c", h=H3, c=CH)

        def f4(j):
            return fm[:, :, j, :].rearrange("p h (n c) -> p h n c", c=CH)

        def load_dma(sc):
            ts_ = slice(sc * SCL, (sc + 1) * SCL)
            for h in range(H3):
                P.dma(SP, fm[:, h, :, :], rwf_d[h, :, :, ts_], w=[r_fm])
                P.dma(SP, ldb[:, h * SCL:(h + 1) * SCL], ld_d[h, :, ts_], w=[r_ld])
                P.dma(SP, UVs[sc % 2][64:128, :, h, :], rwv_d[h, ts_, :].rearrange("(n c) v -> c n v", c=CH), w=[r_Vs[sc % 2]])

        def load(sc):
            P.op(DVE, lambda e: e.tensor_tensor_scan(cum[:], msk[:], ldb[:], 0.0, ALU.mult, ALU.add), r=[r_msk, r_ld], w=[r_cum])
            P.act(tA[:], cum[:], AF.Exp, r=[r_cum], w=[r_tA])
            P.tt(DVE, Q2[:, :, :, 1, :], f4(0), v4(tA[:]), ALU.mult, r=[r_fm, r_tA], w=[r_Q2])
            P.tt(DVE, tB[:], cum[:], ldb[:], ALU.subtract, r=[r_cum, r_ld], w=[r_tB])
            P.act(tA[:], tB[:], AF.Exp, r=[r_tB], w=[r_tA])
            P.tt(DVE, Q2[:, :, :, 0, :], f4(2), v4(tA[:]), ALU.mult, r=[r_fm, r_tA], w=[r_Q2])
            P.act(tA[:], cum[:], AF.Exp, r=[r_cum], w=[r_tA], scale=-1.0)
            P.tt(DVE, KB[:, :, :, 0, :], f4(3), v4(tA[:]), ALU.mult, r=[r_fm, r_tA], w=[r_KB])
            P.tt(DVE, KB[:, :, :, 1, :], f4(1), v4(tA[:]), ALU.mult, r=[r_fm, r_tA], w=[r_KB])
            cumC = v4(cum[:])[:, :, :, CH - 1:CH]
            P.tt(DVE, v4(tB[:]), cumC.to_broadcast([64, H3, NCH, CH]), v4(cum[:]), ALU.subtract, r=[r_cum], w=[r_tB])
            P.act(tA[:], tB[:], AF.Exp, r=[r_tB], w=[r_tA])
            P.tt(DVE, KBb[:, :, :, 0, :], f4(3), v4(tA[:]), ALU.mult, r=[r_fm, r_tA], w=[r_KBb])
            P.tt(DVE, KBb[:, :, :, 1, :], f4(1), v4(tA[:]), ALU.mult, r=[r_fm, r_tA], w=[r_KBb])
            P.act(PC[:].rearrange("p h (n o) -> p h n o", o=1), cumC, AF.Exp, r=[r_cum], w=[r_PC])

        def flat(t4, h, n):
            return t4[:, h, n].rearrange("p a c -> p (a c)")

        def prep(n, i):
            UV, r_V = UVs[cur_sc[0] % 2], r_Vs[cur_sc[0] % 2]
            for h in range(H3):
                P.mm(pA[:, h * 128:(h + 1) * 128], flat(KB, h, n), flat(Q2, h, n), True, True, r=[r_KB, r_Q2], w=[rA])
                P.mm(pB[0:64, h * 64:(h + 1) * 64], Q2[:, h, n, 0, :], KB[:, h, n, 0, :], True, True, r=[r_KB, r_Q2], w=[rB])
            pA3 = pA[:, 0:H3 * 128].rearrange("p (h c) -> p h c", h=H3)
            pB3 = pB[0:64, 0:H3 * 64].rearrange("p (h c) -> p h c", h=H3)
            P.tt(DVE, Mm[i][:], pA3, mM3[:], ALU.mult, r=[rA, r_k3], w=[r_Mm[i]])
            P.tt(DVE, LTa[i][:, :, 0:64], pA3[0:64, :, 0:64], mM3[0:64, :, 0:64], ALU.mult, r=[rA, r_k3], w=[r_LTa[i]])
            P.cp(DVE, LTa[i][:, :, 64:128], id3[:], r=[r_k3], w=[r_LTa[i]])
            P.tt(DVE, Na[i][:], pB3, mN3[:], ALU.mult, r=[rB, r_k3], w=[r_Na[i]])
            yield
            bufs = [(LTa[i], r_LTa[i], Na[i], r_Na[i]), (LTb[i], r_LTb[i], Nb[i], r_Nb[i])]
            for j in range(6):
                (lt, rlt, nn_, rnn) = bufs[j % 2]
                (lt2, rlt2, nn2, rnn2) = bufs[(j + 1) % 2]
                for h in range(H3):
                    P.mm(pA[0:64, h * 128:(h + 1) * 128], nn_[:, h, :], lt[:, h, :], True, True, r=[rnn, rlt], w=[rA])
                    if j < 5:
                        P.mm(pB[0:64, h * 64:(h + 1) * 64], lt[:, h, 0:64], nn_[:, h, :], True, True, r=[rnn, rlt], w=[rB])
                pA3h = pA[0:64, 0:H3 * 128].rearrange("p (h c) -> p h c", h=H3)
                if j < 5:
                    P.cp(DVE, lt2[:, :, 0:64], pA3h[:, :, 0:64], r=[rA], w=[rlt2])
                    P.cp(DVE, nn2[:], pB3, r=[rB], w=[rnn2])
                P.tt(DVE, lt2[:, :, 64:128], pA3h[:, :, 64:128], lt[:, :, 64:128], ALU.add, r=[rA, rlt], w=[rlt2])
                yield
            ltf, rltf = bufs[0][0], bufs[0][1]
            P.cp(DVE, Tt[i][:], ltf[:, :, 64:128], r=[rltf], w=[r_Tt[i]])
            for h in range(H3):
                P.mm(pB[0:64, h * 64:(h + 1) * 64], Mm[i][64:128, h, 0:64], UV[64:128, n, h, :], True, True, r=[r_Mm[i], r_V], w=[rB])
            P.cp(DVE, W2[i][:], pB3, r=[rB], w=[r_W2[i]])
            ptb = pA[:, 0:H3 * 32].bitcast(BF16)
            for h in range(H3):
                P.op(PE, lambda e, h=h: e.transpose(ptb[:, h * 64:(h + 1) * 64], flat(KBb, h, n), C.ident[0:64, 0:64]),
                     r=[r_KBb, C.r_cbf], w=[rA])
            P.cp(DVE, KT[i][:], ptb.rearrange("p (h c) -> p h c", h=H3), r=[rA], w=[r_KT[i]])
            yield

        def crit(n, i):
            UV, r_V, r_U = UVs[cur_sc[0] % 2], r_Vs[cur_sc[0] % 2], r_Us[cur_sc[0] % 2]
            cX = pC[0:64, 0:H3 * 64]
            cU = pC[0:64, H3 * 64:2 * H3 * 64]
            cX3 = cX.rearrange("p (h c) -> p h c", h=H3)
            cU3 = cU.rearrange("p (h c) -> p h c", h=H3)
            for h in range(H3):
                P.mm(cX[:, h * 64:(h + 1) * 64], Q2[:, h, n, 0, :], Sbf[:, h, :], True, True, r=[r_Q2, rS], w=[rC])
            P.tt(DVE, Xs[i][:], cX3, W2[i][:], ALU.add, r=[rC, r_W2[i]], w=[r_Xs[i]])
            for h in range(H3):
                P.mm(cU[:, h * 64:(h + 1) * 64], Tt[i][:, h, :], Xs[i][:, h, :], True, True, r=[r_Tt[i], r_Xs[i]], w=[rC])
            P.cp(DVE, UV[0:64, n, :, :], cU3, r=[rC], w=[r_U])
            yield
            for h in range(H3):
                P.mm(cX[:, h * 64:(h + 1) * 64], Sbf[:, h, :], Q2[:, h, n, 1, :], True, False, r=[rS, r_Q2], w=[rC])
                P.mm(cX[:, h * 64:(h + 1) * 64], UV[:, n, h, :], Mm[i][:, h, 64:128], False, True, r=[r_U, r_V, r_Mm[i]], w=[rC])
                P.mm(cU[:, h * 64:(h + 1) * 64], KT[i][:, h, :], UV[:, n, h, :], True, True, r=[r_KT[i], r_U, r_V], w=[rC])
            P.cp(ACT, Y[:, :, n * CH:(n + 1) * CH], cX3, r=[rC], w=[r_Y])
            P.tt(DVE, S32[:], S32[:], PC[:, :, n:n + 1].to_broadcast([64, H3, 64]), ALU.mult, r=[r_PC, rS], w=[rS])
            P.tt(DVE, S32[:], S32[:], cU3, ALU.add, r=[rC, rS], w=[rS])
            P.cp(DVE, Sbf[:], S32[:], r=[rS], w=[rS])
            yield

        load_dma(0)
        for sc in range(nsc):
            cur_sc[0] = sc
            load(sc)
            if sc + 1 < nsc:
                load_dma(sc + 1)
            yield
            ovl = False
            if not ovl:
                for n in range(NCH):
                    yield from prep(n, n % 2)
                    yield from crit(n, n % 2)
            else:
                yield from prep(0, 0)
                for n in range(NCH):
                    gc = crit(n, n % 2)
                    gp = prep(n + 1, (n + 1) % 2) if n + 1 < NCH else None
                    while gc is not None or gp is not None:
                        if gp is not None:
                            try:
                                next(gp)
                            except StopIteration:
                                gp = None
                        if gc is not None:
                            try:
                                next(gc)
                            except StopIteration:
                                gc = None
                        yield
            for h in range(H3):
                P.dma(SP, y_o[h, :, sc * SCL:(sc + 1) * SCL], Y[:, h, :], r=[r_Y], w=[r_out], own="r")

    sbq, rwq = None, None
    if do_sb:
        msb = C.sb("msb", [128, 12, 512], BF16)
        r_m = C.res("msb")
        P.dma(SP, msb[:, 0:4, :], mF_d, w=[r_m])
        P.dma(SP, msb[:, 4:12, :], mH_d, w=[r_m])
        sbq = sb_pipeline([(qF_d, kF_d, vF_d, msb[:, 0:4, :], T // 128, 4, oF_o),
                           (qH_d, kH_d, vH_d, msb[:, 4:12, :], T // 256, 8, oH_o)])
    if do_rw:
        rwq = rw_chain()
    gq = None
    gates_started = False
    while sbq is not None or rwq is not None or gq is not None:
        for _ in range(1):
            if sbq is not None:
                try:
                    next(sbq)
                except StopIteration:
                    sbq = None
        if sbq is None and do_sb and do_gates and not gates_started:
            gates_started = True
            gq = gates_chain()
        if rwq is not None:
            try:
                next(rwq)
            except StopIteration:
                rwq = None
        if gq is not None:
            try:
                next(gq)
            except StopIteration:
                gq = None
    return C.finish([r_out])


def _consts2():
    c = np.zeros((128, 512), np.float32)
    j = np.arange(128)[:, None]
    s_ = np.arange(128)[None, :]
    c[:, 0:128] = (j >= s_)
    c[:, 128:256] = (j < s_)
    rs = np.arange(128)[:, None] % 64
    ct = np.arange(128)[None, :]
    m = np.zeros((128, 128), np.float32)
    m[:, 0:64] = (rs < ct[:, 0:64])
    m[:, 64:128] = (rs <= (ct[:, 64:128] - 64))
    c[:, 256:384] = m
    tt_ = np.arange(64)[:, None]
    ss_ = np.arange(64)[None, :]
    c[0:64, 384:448] = (ss_ < tt_)
    return c


def _sb_masks(qblocks, kblocks):
    m = np.zeros((128, len(kblocks), len(qblocks) * 128), np.float32)
    key = np.arange(128)[:, None]
    qp = np.arange(128)[None, :]
    for j, kb in enumerate(kblocks):
        for i, qb in enumerate(qblocks):
            m[:, j, i * 128:(i + 1) * 128] = (kb * 128 + key) < (qb * 128 + qp)
    return m.astype(NPBF)


def sb_units(c):
    if c % 2 == 0:
        return 3 * c // 2, 3 * c // 2 + 1, 0
    return (3 * c + 1) // 2, (3 * c - 1) // 2, 1


def prep_stage_b(resA, inp=None):
    wg = _tile_lhsT(np.asarray(inp["w_in"][0][:, 10976:23264])) if inp is not None else None
    qT = np.concatenate([np.asarray(r["qT"]) for r in resA], axis=2)
    kT = np.concatenate([np.asarray(r["kT"]) for r in resA], axis=2)
    v = np.concatenate([np.asarray(r["v"]) for r in resA], axis=0)
    rwf = np.concatenate([np.asarray(r["rwf"]) for r in resA], axis=2)
    ld = np.concatenate([np.asarray(r["ld"]) for r in resA], axis=1)
    rwv = np.concatenate([np.asarray(r["rwv"]) for r in resA], axis=0)
    cst, cst2 = _consts(), _consts2()
    mF = _sb_masks([0, 1, 2, 3], [0, 1, 2, 3])
    maps = []
    for c in range(NCORES):
        hf, hh, par = sb_units(c)
        m = {"cst": cst, "cst2": cst2, "mF": mF}
        m["mH"] = _sb_masks([par, 2 + par, 4 + par, 6 + par], list(range(8)))
        m["qF"] = np.ascontiguousarray(qT[hf])
        m["kF"] = np.ascontiguousarray(kT[hf])
        m["vF"] = np.ascontiguousarray(v[:, hf * 128:(hf + 1) * 128])
        qh = qT[hh].reshape(128, T // 256, 2, 128)[:, :, par, :].reshape(128, T // 2)
        m["qH"] = np.ascontiguousarray(qh)
        m["kH"] = np.ascontiguousarray(kT[hh])
        m["vH"] = np.ascontiguousarray(v[:, hh * 128:(hh + 1) * 128])
        hs = [3 * c + i for i in range(NHB)]
        m["rwf"] = np.ascontiguousarray(np.stack([rwf[:, h * 64:(h + 1) * 64, :].transpose(1, 0, 2) for h in hs]))
        m["ld"] = np.ascontiguousarray(np.stack([ld[h * 64:(h + 1) * 64] for h in hs]))
        m["rwv"] = np.ascontiguousarray(np.stack([rwv[:, h * 64:(h + 1) * 64] for h in hs]))
        if wg is not None:
            m["wg"] = wg
            m["xn"] = np.asarray(resA[c]["xn"])
        maps.append(m)
    return maps


NPC = 32 + 32 + 32 + 2 + 2 + 12 + 12 + 1 + 1
NFT = DFF // 128
HT = 512


def build_stage_c(parts=("mem", "rw", "merge", "out", "ffn")):
    C = Ctx("C")
    P, nc = C.P, C.nc
    xT = C.din("xT", [128, KC, TL], F32)
    memT = C.din("memT", [128, KC, NMEM], F32)
    prm_d = C.din("prm", [128, NPC], F32)
    cst_d = C.din("cst", [128, 384], F32)
    wkvk_d = C.din("wkvk", [8, 128, KC * 128], F32)
    wkvv_d = C.din("wkvv", [2, 128, KC * 512], F32)
    wmq_d = C.din("wmq", [8, 128, KC * 128], F32)
    wu_d = C.din("wu", [32, 128, KC * 128], F32)
    wo_d = C.din("wo", [32, 128, KC * 128], F32)
    wfg_d = C.din("wfg", [NFT, 128, KC * 128], F32)
    wfu_d = C.din("wfu", [NFT, 128, KC * 128], F32)
    wfd_d = C.din("wfd", [32, 2, 128, 43 * 128], F32)
    osb_d = C.din("osb", [128, 12, TL], BF16)
    y_d = C.din("y", [RWW, TL], F32)
    g_d = C.din("g", [RWW, TL], F32)
    bon_d = C.din("bon", [RWW, TL], F32)
    out_o = C.dout("outT", [D, TL], F32)
    sg_s = C.din("sg", [96, 128, TL], BF16)
    xn_d = C.din("xn", [128, KC, TL], BF16)
    h1_s = C.dint("h1_s", [32, 128, TL], F32)
    r_out, r_sg, r_h1 = C.res("outs"), C.res("sg"), C.res("h1")

    load_consts(C, cst_d)
    prm = C.sb("prm", [128, NPC], F32)
    r_prm = C.res("prm")
    P.dma(SP, prm[:], prm_d, w=[r_prm])
    C.eps_ap = prm[:, NPC - 1:NPC]
    gneps_ap = prm[:, NPC - 2:NPC - 1]
    C.r_eps = r_prm
    g_attn, g_mem, g_ffn = prm[:, 0:32], prm[:, 32:64], prm[:, 64:96]
    o_mq, o_mk, o_lng, o_lnb = 96, 98, 100, 112
    one_col = C.c32[:, 0:1]

    NW = 3
    wb = [C.sb(f"wb{i}", [128, KC, 128], BF16) for i in range(NW)]
    rwb = [C.res("wb") for _ in range(NW)]
    wcount = [0]

    def load_w(src):
        i = wcount[0] % NW
        wcount[0] += 1
        P.dma(POOL, wb[i][:], src.rearrange("p (c m) -> p c m", m=128), w=[rwb[i]])
        return wb[i], rwb[i]

    halves = [(0, HT), (HT, HT)]
    omem = C.sb("omem", [128, 8, TL], BF16)
    r_omem = C.res("omem")

    with C.phase():
        xn = C.sb("xn", [128, KC, TL], BF16)
        r_xn = C.res("xn")
        P.dma(SP, xn[:], xn_d, w=[r_xn])

        def gemm_fm(wt, rw, banks, rhs3, r_rhs, segs):
            for b, (s, n) in zip(banks, segs):
                for c in range(KC):
                    P.mm(C.ps[b][:, 0:n], wt[:, c, :], rhs3[:, c, s:s + n], c == 0, c == KC - 1, r=[rw, r_rhs], w=[C.rps[b]])

        if "mem" in parts:
            with C.phase():
                mn = C.sb("mn", [128, KC, NMEM], BF16)
                r_mn = C.res("mn")
                with C.phase():
                    rmsnorm_T(C, memT, NMEM, g_mem, r_prm, mn, r_mn, "nm")
                mk32 = C.sb("mk32", [128, 8, NMEM], F32)
                r_mk32 = C.res("mk32")
                mksq = C.sb("mksq", [128, 8, NMEM], BF16)
                r_mksq = C.res("mksq")
                mkn = C.sb("mkn", [128, 8, NMEM], BF16)
                r_mkn = C.res("mkn")
                mrs = C.sb("mrs", [128, NMEM], F32)
                r_mrs = C.res("mrs")
                for j in range(8):
                    wt, rw = load_w(wkvk_d[j])
                    b = j % 2
                    gemm_fm(wt, rw, [b], mn, r_mn, [(0, NMEM)])
                    P.cp(ACT, mk32[:, j, :], C.ps[b][:, 0:NMEM], r=[C.rps[b]], w=[r_mk32])
                    P.act(mksq[:, j, :], mk32[:, j, :], AF.Square, r=[r_mk32], w=[r_mksq])
                for h in range(4):
                    for dt in range(2):
                        P.mm(C.ps[2][:, 0:NMEM], C.ones, mksq[:, 2 * h + dt, :], dt == 0, dt == 1, r=[r_mksq, C.r_cbf], w=[C.rps[2]])
                    P.act(mrs[:], C.ps[2][:, 0:NMEM], AF.Sqrt, r=[C.rps[2], r_prm], w=[r_mrs], bias=C.eps_ap, scale=1.0 / 256)
                    P.op(DVE, lambda e: e.reciprocal(mrs[:], mrs[:]), r=[r_mrs], w=[r_mrs])
                    for dt in range(2):
                        P.stt(DVE, mkn[:, 2 * h + dt, :], mk32[:, 2 * h + dt, :], prm[:, o_mk + dt:o_mk + dt + 1], mrs[:], ALU.mult, ALU.mult,
                              r=[r_mk32, r_mrs, r_prm], w=[r_mkn])
                mv = C.sb("mv", [128, 2, MEMW], BF16)
                r_mv = C.res("mv")
                with C.phase():
                    wvb = C.sb("wvb", [128, KC, 512], BF16)
                    rwvb = C.res("wvb")
                    for gc in range(2):
                        P.dma(POOL, wvb[:], wkvv_d[gc].rearrange("p (c m) -> p c m", m=512), w=[rwvb])
                        for mt in range(2):
                            b = 3 + mt
                            for c in range(KC):
                                P.mm(C.ps[b][:, :], mn[:, c, mt * 128:(mt + 1) * 128], wvb[:, c, :], c == 0, c == KC - 1, r=[rwvb, r_mn], w=[C.rps[b]])
                            P.cp(ACT, mv[:, mt, gc * 512:(gc + 1) * 512], C.ps[b][:, :], r=[C.rps[b]], w=[r_mv])
                mq32 = C.sb("mq32", [128, 2, TL], F32)
                r_mq32 = C.res("mq32")
                mqsq = C.sb("mqsq", [128, 2, TL], BF16)
                r_mqsq = C.res("mqsq")
                mqn = C.sb("mqn", [128, 2, TL], BF16)
                r_mqn = C.res("mqn")
                qrs = C.sb("qrs", [128, TL], F32)
                r_qrs = C.res("qrs")
                pT = C.sb("pT", [128, 2, HT], BF16)
                r_pT = C.res("pT")
                rden = C.sb("rden", [128, HT], F32)
                r_rden = C.res("rden")
                for h in range(4):
                    for dt in range(2):
                        wt, rw = load_w(wmq_d[2 * h + dt])
                        gemm_fm(wt, rw, [0, 1], xn, r_xn, halves)
                        for hh, (s, n) in enumerate(halves):
                            P.cp(ACT, mq32[:, dt, s:s + n], C.ps[hh][:, :], r=[C.rps[hh]], w=[r_mq32])
                        P.act(mqsq[:, dt, :], mq32[:, dt, :], AF.Square, r=[r_mq32], w=[r_mqsq])
                    for hh, (s, n) in enumerate(halves):
                        for dt in range(2):
                            P.mm(C.ps[2][:, :], C.ones, mqsq[:, dt, s:s + n], dt == 0, dt == 1, r=[r_mqsq, C.r_cbf], w=[C.rps[2]])
                        P.act(qrs[:, s:s + n], C.ps[2][:, :], AF.Sqrt, r=[C.rps[2], r_prm], w=[r_qrs], bias=C.eps_ap, scale=1.0 / 256)
                    P.op(DVE, lambda e: e.reciprocal(qrs[:], qrs[:]), r=[r_qrs], w=[r_qrs])
                    for dt in range(2):
                        P.stt(DVE, mqn[:, dt, :], mq32[:, dt, :], prm[:, o_mq + dt:o_mq + dt + 1], qrs[:], ALU.mult, ALU.mult,
                              r=[r_mq32, r_qrs, r_prm], w=[r_mqn])
                    for hh, (s, n) in enumerate(halves):
                        for mt in range(2):
                            for dt in range(2):
                                P.mm(C.ps[3][:, :], mkn[:, 2 * h + dt, mt * 128:(mt + 1) * 128], mqn[:, dt, s:s + n], dt == 0, dt == 1,
                                     r=[r_mkn, r_mqn], w=[C.rps[3]])
                            P.act(pT[:, mt, :], C.ps[3][:, :], AF.Exp, r=[C.rps[3]], w=[r_pT], scale=1.0 / 16)
                        for mt in range(2):
                            P.mm(C.ps[4][:, :], C.ones, pT[:, mt, :], mt == 0, mt == 1, r=[r_pT, C.r_cbf], w=[C.rps[4]])
                        P.op(DVE, lambda e: e.reciprocal(rden[:], C.ps[4][:, :]), r=[C.rps[4]], w=[r_rden])
                        for dt in range(2):
                            b = 5 + dt
                            for mt in range(2):
                                P.mm(C.ps[b][:, :], mv[:, mt, (2 * h + dt) * 128:(2 * h + dt + 1) * 128], pT[:, mt, :], mt == 0, mt == 1,
                                     r=[r_mv, r_pT], w=[C.rps[b]])
                            P.tt(DVE, omem[:, 2 * h + dt, s:s + n], C.ps[b][:, :], rden[:], ALU.mult, r=[C.rps[b], r_rden], w=[r_omem])


    with C.phase():
        osb = C.sb("osb", [128, 12, TL], BF16)
        r_osb = C.res("osb")
        orw = C.sb("orw", [128, 12, TL], BF16)
        r_orw = C.res("orw")
        mrg = C.sb("mrg", [128, KC, TL], BF16)
        r_mrg = C.res("mrg")
        P.dma(SP, osb[:], osb_d, w=[r_osb])
        if "rw" in parts:
            with C.phase():
                def f32b(n):
                    return C.sb(n, [128, TL], F32), C.res(n)
                yt, r_yt = f32b("yt")
                gt, r_gt = f32b("gt")
                bt, r_bt = f32b("bt")
                dd, r_dd = f32b("dd")
                rs, r_rs = f32b("rs")
                ybf = C.sb("ybf", [128, TL], BF16)
                r_ybf = C.res("ybf")
                for ct in range(12):
                    cs = slice(ct * 128, (ct + 1) * 128)
                    P.dma(SP, yt[:], y_d[cs, :], w=[r_yt])
                    P.dma(SP, gt[:], g_d[cs, :], w=[r_gt])
                    P.dma(SP, bt[:], bon_d[cs, :], w=[r_bt])
                    P.cp(ACT, ybf[:], yt[:], r=[r_yt], w=[r_ybf])
                    for hh, (s, n) in enumerate(halves):
                        P.mm(C.ps[hh][:, :], C.bones, ybf[:, s:s + n], True, True, r=[r_ybf, C.r_cbf], w=[C.rps[hh]])
                        P.stt(DVE, dd[:, s:s + n], C.ps[hh][:, :], -1.0 / 64, yt[:, s:s + n], ALU.mult, ALU.add, r=[C.rps[hh], r_yt], w=[r_dd])
                    P.act(ybf[:], dd[:], AF.Square, r=[r_dd], w=[r_ybf])
                    for hh, (s, n) in enumerate(halves):
                        P.mm(C.ps[2 + hh][:, :], C.bones, ybf[:, s:s + n], True, True, r=[r_ybf, C.r_cbf], w=[C.rps[2 + hh]])
                        P.act(rs[:, s:s + n], C.ps[2 + hh][:, :], AF.Sqrt, r=[C.rps[2 + hh], r_prm], w=[r_rs], bias=gneps_ap, scale=1.0 / 64)
                    P.op(DVE, lambda e: e.reciprocal(rs[:], rs[:]), r=[r_rs], w=[r_rs])
                    P.tt(DVE, dd[:], dd[:], rs[:], ALU.mult, r=[r_dd, r_rs], w=[r_dd])
                    P.ts(DVE, dd[:], dd[:], prm[:, o_lng + ct:o_lng + ct + 1], prm[:, o_lnb + ct:o_lnb + ct + 1], ALU.mult, ALU.add,
                         r=[r_dd, r_prm], w=[r_dd])
                    P.tt(POOL, dd[:], dd[:], bt[:], ALU.add, r=[r_dd, r_bt], w=[r_dd])
                    P.tt(DVE, orw[:, ct, :], dd[:], gt[:], ALU.mult, r=[r_dd, r_gt], w=[r_orw])

        if "merge" in parts:
            with C.phase():
                sgt = [C.sb(f"sgt{i}", [128, 3, TL], BF16) for i in range(2)]
                rsgt = [C.res("sgt") for _ in range(2)]
                m1 = C.sb("m1", [128, TL], F32)
                r_m1 = C.res("m1")
                m2 = C.sb("m2", [128, TL], F32)
                r_m2 = C.res("m2")
                srcs = [(osb, r_osb, 0, 12), (orw, r_orw, 12, 12), (omem, r_omem, 24, 8)]
                for j in range(32):
                    i = j % 2
                    wt, rw = load_w(wu_d[j])
                    for br in range(3):
                        P.dma(SP, sgt[i][:, br, :], sg_s[br * 32 + j], r=[r_sg], w=[rsgt[i]])
                    for br, (src, r_src, c0, ncnk) in enumerate(srcs):
                        for hh, (s, n) in enumerate(halves):
                            b = 2 * br + hh
                            for c in range(ncnk):
                                P.mm(C.ps[b][:, :], wt[:, c0 + c, :], src[:, c, s:s + n], c == 0, c == ncnk - 1, r=[rw, r_src], w=[C.rps[b]])
                    for hh, (s, n) in enumerate(halves):
                        P.tt(DVE, m1[:, s:s + n], C.ps[0 + hh][:, :], sgt[i][:, 0, s:s + n], ALU.mult, r=[C.rps[hh], rsgt[i]], w=[r_m1])
                        P.tt(DVE, m2[:, s:s + n], C.ps[2 + hh][:, :], sgt[i][:, 1, s:s + n], ALU.mult, r=[C.rps[2 + hh], rsgt[i]], w=[r_m2])
                    P.tt(POOL, m1[:], m1[:], m2[:], ALU.add, r=[r_m1, r_m2], w=[r_m1])
                    for hh, (s, n) in enumerate(halves):
                        P.tt(DVE, m2[:, s:s + n], C.ps[4 + hh][:, :], sgt[i][:, 2, s:s + n], ALU.mult, r=[C.rps[4 + hh], rsgt[i], r_m2], w=[r_m2])
                    P.tt(POOL, mrg[:, j, :], m1[:], m2[:], ALU.add, r=[r_m1, r_m2], w=[r_mrg])

        ssq_banks = [6, 7]
        if "out" in parts:
            with C.phase():
                xs = [C.sb(f"xs{i}", [128, TL], F32) for i in range(2)]
                rxs = [C.res("xs") for _ in range(2)]
                hsq = [C.sb(f"hsq{i}", [128, TL], BF16) for i in range(2)]
                rhsq = [C.res("hsq") for _ in range(2)]
                for j in range(32):
                    i = j % 2
                    wt, rw = load_w(wo_d[j])
                    P.dma(SP, xs[i][:], xT[:, j, :], w=[rxs[i]])
                    mb = [0, 1] if i == 0 else [2, 3]
                    for hh, (s, n) in enumerate(halves):
                        for c in range(KC):
                            P.mm(C.ps[mb[hh]][:, :], wt[:, c, :], mrg[:, c, s:s + n], c == 0, c == KC - 1, r=[rw, r_mrg], w=[C.rps[mb[hh]]])
                        P.tt(DVE, xs[i][:, s:s + n], C.ps[mb[hh]][:, :], xs[i][:, s:s + n], ALU.add, r=[C.rps[mb[hh]], rxs[i]], w=[rxs[i]])
                    P.dma(SP, h1_s[j], xs[i][:], r=[rxs[i]], w=[r_h1], own="r")
                    P.act(hsq[i][:], xs[i][:], AF.Square, r=[rxs[i]], w=[rhsq[i]])
                    for hh, (s, n) in enumerate(halves):
                        P.mm(C.ps[ssq_banks[hh]][:, :], C.ones, hsq[i][:, s:s + n], j == 0, j == 31, r=[rhsq[i], C.r_cbf], w=[C.rps[ssq_banks[hh]]])

    if "ffn" in parts:
        with C.phase():
            rstd = C.sb("frstd", [128, TL], F32)
            r_rstd = C.res("frstd")
            for hh, (s, n) in enumerate(halves):
                P.act(rstd[:, s:s + n], C.ps[ssq_banks[hh]][:, :], AF.Sqrt, r=[C.rps[ssq_banks[hh]], r_prm], w=[r_rstd], bias=C.eps_ap, scale=1.0 / D)
            P.op(DVE, lambda e: e.reciprocal(rstd[:], rstd[:]), r=[r_rstd], w=[r_rstd])
            hn = C.sb("hn", [128, KC, HT], BF16)
            r_hn = C.res("hn")
            actT = C.sb("actT", [128, NFT, HT], BF16)
            r_act = C.res("actT")
            hld = [C.sb(f"hld{i}", [128, HT], F32) for i in range(2)]
            rhld = [C.res("hld") for _ in range(2)]
            sgl = [C.sb(f"sgl{i}", [128, HT], F32) for i in range(2)]
            rsgl = [C.res("sgl") for _ in range(2)]
            wdb = [C.sb(f"wdb{i}", [128, 43, 128], BF16) for i in range(3)]
            rwdb = [C.res("wdb") for _ in range(3)]
            wdc = [0]
            for hh, (s, n) in enumerate(halves):
                for j in range(32):
                    i = j % 2
                    P.dma(SP, hld[i][:], h1_s[j][:, s:s + n], r=[r_h1], w=[rhld[i]])
                    P.stt(DVE, hn[:, j, :], hld[i][:], g_ffn[:, j:j + 1], rstd[:, s:s + n], ALU.mult, ALU.mult, r=[rhld[i], r_rstd, r_prm], w=[r_hn])
                for f in range(NFT):
                    i = f % 2
                    gb, ub = (0, 1) if i == 0 else (2, 3)
                    wt, rw = load_w(wfg_d[f])
                    for c in range(KC):
                        P.mm(C.ps[gb][:, :], wt[:, c, :], hn[:, c, :], c == 0, c == KC - 1, r=[rw, r_hn], w=[C.rps[gb]])
                    wt, rw = load_w(wfu_d[f])
                    for c in range(KC):
                        P.mm(C.ps[ub][:, :], wt[:, c, :], hn[:, c, :], c == 0, c == KC - 1, r=[rw, r_hn], w=[C.rps[ub]])
                    P.act(sgl[i][:], C.ps[gb][:, :], AF.Silu, r=[C.rps[gb]], w=[rsgl[i]])
                    P.tt(DVE, actT[:, f, :], sgl[i][:], C.ps[ub][:, :], ALU.mult, r=[rsgl[i], C.rps[ub]], w=[r_act])
                for j in range(32):
                    i = j % 2
                    b = 4 + i
                    for fg in range(2):
                        k = wdc[0] % 3
                        wdc[0] += 1
                        P.dma(POOL, wdb[k][:], wfd_d[j, fg].rearrange("p (c m) -> p c m", m=128), w=[rwdb[k]])
                        for c in range(43):
                            P.mm(C.ps[b][:, :], wdb[k][:, c, :], actT[:, fg * 43 + c, :], fg == 0 and c == 0, fg == 1 and c == 42,
                                 r=[rwdb[k], r_act], w=[C.rps[b]])
                    P.dma(SP, hld[i][:], h1_s[j][:, s:s + n], r=[r_h1], w=[rhld[i]])
                    P.tt(DVE, hld[i][:], C.ps[b][:, :], hld[i][:], ALU.add, r=[C.rps[b], rhld[i]], w=[rhld[i]])
                    P.dma(SP, out_o[j * 128:(j + 1) * 128, s:s + n], hld[i][:], r=[rhld[i]], w=[r_out], own="r")
    return C.finish([r_out])


def prep_stage_c(inp, resA, resB):
    x = inp["x"][0]
    w_in = inp["w_in"][0]
    shared = {"cst": _consts()}
    kv = inp["mem_w_kv"][0]
    shared["wkvk"] = _tile_lhsT(kv[:, 0:1024])
    shared["wkvv"] = _tile_lhsT(kv[:, 1024:2048], 512)
    shared["wmq"] = _tile_lhsT(w_in[:, 9952:10976])
    wu = np.concatenate([inp["w_sb_o"][0], inp["w_rw_o"][0], inp["w_mem_o"][0]], axis=0)
    shared["wu"] = _tile_lhsT(wu)
    shared["wo"] = _tile_lhsT(inp["w_out"][0])
    shared["wfg"] = _tile_lhsT(inp["w_gate"][0])
    shared["wfu"] = _tile_lhsT(inp["w_up"][0])
    wd = inp["w_down"][0].reshape(2, 43, 128, 32, 128).transpose(3, 0, 2, 1, 4)
    shared["wfd"] = np.ascontiguousarray(wd).reshape(32, 2, 128, 43 * 128)
    shared["memT"] = np.ascontiguousarray(inp["mem"][0].T.reshape(KC, 128, NMEM).transpose(1, 0, 2))
    prm = np.zeros((128, NPC), np.float32)
    prm[:, 0:32] = _cols(inp["attn_norm_g"][0], 32)
    prm[:, 32:64] = _cols(inp["mem_norm_g"][0], 32)
    prm[:, 64:96] = _cols(inp["ffn_norm_g"][0], 32)
    prm[:, 96:98] = _cols(inp["mem_q_norm_g"][0], 2)
    prm[:, 98:100] = _cols(inp["mem_k_norm_g"][0], 2)
    prm[:, 100:112] = _cols(inp["rw_ln_g"][0], 12)
    prm[:, 112:124] = _cols(inp["rw_ln_b"][0], 12)
    prm[:, NPC - 2] = GN_EPS
    prm[:, NPC - 1] = EPS
    shared["prm"] = prm
    osb = np.zeros((12, 128, T), NPBF)
    y = np.zeros((RWW, T), np.float32)
    for c in range(NCORES):
        hf, hh, par = sb_units(c)
        osb[hf] = np.asarray(resB[c]["oF"])
        oh = np.asarray(resB[c]["oH"]).reshape(128, T // 256, 128)
        osb[hh].reshape(128, T // 256, 2, 128)[:, :, par, :] = oh
        for i in range(NHB):
            h = 3 * c + i
            y[h * 64:(h + 1) * 64] = np.asarray(resB[c]["y"][i])
    maps = []
    for c in range(NCORES):
        ts_ = slice(c * TL, (c + 1) * TL)
        m = dict(shared)
        m["xT"] = np.ascontiguousarray(x[ts_].T.reshape(KC, 128, TL).transpose(1, 0, 2))
        m["osb"] = np.ascontiguousarray(osb[:, :, ts_].transpose(1, 0, 2))
        m["y"] = np.ascontiguousarray(y[:, ts_])
        m["g"] = np.asarray(resA[c]["g"])
        m["bon"] = np.asarray(resA[c]["bon"])
        m["xn"] = np.asarray(resA[c]["xn"])
        m["sg"] = np.asarray(resB[c]["sg"])
        maps.append(m)
    return maps


def kernel(**inp):
    inp = {k: np.asarray(v) for k, v in inp.items()}
    ids = list(range(NCORES))
    resA = run_bass_kernel_spmd(build_stage_a(), prep_stage_a(inp), core_ids=ids).results
    resB = run_bass_kernel_spmd(build_stage_b(), prep_stage_b(resA, inp), core_ids=ids).results
    resC = run_bass_kernel_spmd(build_stage_c(), prep_stage_c(inp, resA, resB), core_ids=ids).results
    out = np.concatenate([np.asarray(r["outT"]).T for r in resC], axis=0)
    return np.ascontiguousarray(out.reshape(1, T, D).astype(np.float32))
```

```python
import numpy as np
import ml_dtypes
import concourse.bass as bass
import concourse.mybir as mybir
from concourse.bass_utils import run_bass_kernel_spmd

F32 = mybir.dt.float32
BF16 = mybir.dt.bfloat16
AF = mybir.ActivationFunctionType
ALU = mybir.AluOpType
AX = mybir.AxisListType
NPBF = ml_dtypes.bfloat16

PE, DVE, ACT, POOL, SP = "tensor", "vector", "scalar", "gpsimd", "sync"
ENGS = [PE, DVE, ACT, POOL, SP]

NCORES = 8
D = 4096
T = 8192
TL = T // NCORES
KC = D // 128
SBW, RWW, MEMW = 1536, 1536, 1024
RW_SEG = 5344
IN_COLS = 23264
DFF = 11008
NMEM = 256
EPS = 1e-6
GN_EPS = 64e-5


class Res:
    __slots__ = ("name", "last_w", "reads", "dsem", "dcnt", "ws")

    def __init__(self, name):
        self.name = name
        self.ws = []
        self.last_w = None
        self.reads = []
        self.dsem = None
        self.dcnt = 0


class Prog:
    def __init__(self, nc):
        self.nc = nc
        self.q = {e: [] for e in ENGS}
        self.seq = {e: 0 for e in ENGS}
        self.waited = {e: {} for e in ENGS}
        self.esem = {}
        self.nsem = 0
        self.dma_owners = []

    def _newsem(self, name):
        self.nsem += 1
        return self.nc.alloc_semaphore(f"s{self.nsem}_{name}")

    def _esem(self, eng):
        if eng not in self.esem:
            self.esem[eng] = self._newsem("e_" + eng)
        return self.esem[eng]

    def _deps(self, eng, r, w, dma_sem=None):
        deps = []
        for b in r:
            if b.last_w is not None:
                deps.append(b.last_w)
            deps.extend(b.ws)
        for b in w:
            if b.last_w is not None:
                if not (dma_sem is not None and b.last_w[0] is dma_sem):
                    deps.append(b.last_w)
            deps.extend(b.reads)
        wd = self.waited[eng]
        best = {}
        for (sem, val, src) in deps:
            if src == eng and eng == PE:
                continue
            k = id(sem)
            if wd.get(k, 0) >= val:
                continue
            if k not in best or best[k][1] < val:
                best[k] = (sem, val)
        for k, (sem, val) in best.items():
            wd[k] = val
        return list(best.values())

    def op(self, eng, fn, r=(), w=()):
        waits = self._deps(eng, r, w)
        sem = self._esem(eng)
        self.seq[eng] += 1
        ev = (sem, self.seq[eng], eng)
        for b in r:
            b.reads.append(ev)
        for b in w:
            b.last_w = ev
            b.reads = []
        self.q[eng].append((waits, fn, (sem, 1)))
        return ev

    def dma(self, eng, out, in_, r=(), w=(), own="w", **kw):
        owner = w[0] if own == "w" else r[0]
        if owner.dsem is None:
            owner.dsem = self._newsem("d_" + owner.name)
            self.dma_owners.append(owner)
        sem = owner.dsem
        waits = self._deps(eng, r, w, dma_sem=sem)
        owner.dcnt += 16
        ev = (sem, owner.dcnt, "dma")
        for b in r:
            b.reads.append(ev)
        for b in w:
            if own == "r":
                b.ws = [x for x in b.ws if x[0] is not sem] + [ev]
            else:
                b.last_w = ev
                b.reads = []
        self.q[eng].append((waits, lambda e: e.dma_start(out=out, in_=in_, **kw), (sem, 16)))
        return ev

    def barrier(self):
        targets = [(sem, self.seq[e]) for e, sem in self.esem.items() if self.seq[e] > 0]
        targets += [(o.dsem, o.dcnt) for o in self.dma_owners if o.dcnt > 0]
        for eng in ENGS:
            wd = self.waited[eng]
            waits = []
            for sem, val in targets:
                if wd.get(id(sem), 0) >= val:
                    continue
                wd[id(sem)] = val
                waits.append((sem, val))
            if waits:
                self.q[eng].append((waits, None, None))

    def allgather(self, in_ap, out_ap, r, w):
        sem = self._newsem("cc")
        waits = self._deps(POOL, r, w)
        ev = (sem, 1, "cc")
        for b in r:
            b.reads.append(ev)
        for b in w:
            b.last_w = ev
            b.reads = []
        self.q[POOL].append((waits, lambda e: e.collective_compute(
            "AllGather", ALU.bypass, replica_groups=[list(range(NCORES))], ins=[in_ap.opt()], outs=[out_ap.opt()]), (sem, 1)))
        self.q[POOL].append(([(sem, 1)], None, None))
        self.waited[POOL][id(sem)] = 1
        return ev

    def wait_all(self, eng, ress):
        deps = []
        for b in ress:
            if b.last_w is not None:
                deps.append((b.last_w[0], b.last_w[1]))
            deps.extend((x[0], x[1]) for x in b.ws)
        self.q[eng].append((deps, None, None))

    def replay(self, block):
        for eng in ENGS:
            items = self.q[eng]
            if not items:
                continue

            def body(e, items=items):
                for waits, fn, inc in items:
                    for sem, val in waits:
                        e.wait_ge(sem, val)
                    if fn is not None:
                        fn(e).then_inc(inc[0], inc[1])

            getattr(block, eng)(body)

    def mm(self, out, lhsT, rhs, start, stop, r, w):
        return self.op(PE, lambda e: e.matmul(out, lhsT=lhsT, rhs=rhs, start=start, stop=stop), r=r, w=w)

    def act(self, out, in_, func, r, w, bias=None, scale=None, eng=ACT):
        kw = {}
        if bias is not None:
            kw["bias"] = bias
        if scale is not None:
            kw["scale"] = scale
        return self.op(eng, lambda e: e.activation(out, in_, func, **kw), r=r, w=w)

    def tt(self, eng, out, a, b, op, r, w):
        return self.op(eng, lambda e: e.tensor_tensor(out, a, b, op), r=r, w=w)

    def ts(self, eng, out, a, s1, s2, op0, op1, r, w):
        if op1 is None:
            return self.op(eng, lambda e: e.tensor_scalar(out, a, s1, None, op0), r=r, w=w)
        return self.op(eng, lambda e: e.tensor_scalar(out, a, s1, s2, op0, op1), r=r, w=w)

    def stt(self, eng, out, a, s, b, op0, op1, r, w):
        return self.op(eng, lambda e: e.scalar_tensor_tensor(out, a, s, b, op0, op1), r=r, w=w)

    def cp(self, eng, out, in_, r, w):
        if eng == ACT:
            return self.op(eng, lambda e: e.copy(out, in_), r=r, w=w)
        return self.op(eng, lambda e: e.tensor_copy(out, in_), r=r, w=w)


class Ctx:
    def __init__(self, name):
        self.nc = bass.Bass("TRN2", target_bir_lowering=False)
        self.P = Prog(self.nc)
        self.nres = 0
        self.stacks = []
        nc = self.nc
        self.ps = [nc.alloc_psum_tensor(f"psb{i}", [128, 512], F32) for i in range(8)]
        self.rps = [Res(f"ps{i}") for i in range(8)]

    def res(self, name):
        self.nres += 1
        return Res(f"{name}{self.nres}")

    def sb(self, name, shape, dt):
        self.nres += 1
        nm = f"s_{name}_{self.nres}"
        if self.stacks:
            return self.stacks[-1].enter_context(self.nc.sbuf_tensor(nm, list(shape), dt))
        return self.nc.alloc_sbuf_tensor(nm, list(shape), dt)

    def phase(self):
        return _Phase(self)

    def din(self, name, shape, dt):
        return self.nc.dram_tensor(name, list(shape), dt, kind="ExternalInput").ap()

    def dout(self, name, shape, dt):
        return self.nc.dram_tensor(name, list(shape), dt, kind="ExternalOutput").ap()

    def dint(self, name, shape, dt):
        return self.nc.dram_tensor(name, list(shape), dt).ap()

    def finish(self, out_ress):
        self.P.wait_all(SP, out_ress)
        self.P.q[SP].append(([(o.dsem, o.dcnt) for o in self.P.dma_owners if o.dcnt > 0], None, None))
        with self.nc.Block() as block:
            self.P.replay(block)
        return self.nc


class _Phase:
    def __init__(self, C):
        self.C = C

    def __enter__(self):
        import contextlib
        self.st = contextlib.ExitStack()
        self.C.stacks.append(self.st)
        return self

    def __exit__(self, *a):
        self.C.P.barrier()
        self.C.stacks.pop()
        self.st.close()
        return False


def load_consts(C, cst_ap):
    P = C.P
    c32 = C.sb("c32", [128, 384], F32)
    cbf = C.sb("cbf", [128, 384], BF16)
    r32, rbf = C.res("c32"), C.res("cbf")
    P.dma(SP, c32[:], cst_ap, w=[r32])
    P.cp(DVE, cbf[:], c32[:], r=[r32], w=[rbf])
    C.c32, C.cbf, C.r_c32, C.r_cbf = c32, cbf, r32, rbf
    C.ones = cbf[:, 0:128]
    C.bones = cbf[:, 128:256]
    C.ident = cbf[:, 256:384]
    C.ident32 = c32[:, 256:384]


def rmsnorm_T(C, xT_ap, ntok, g_ap, r_g, out_bf, r_out, name):
    P = C.P
    nseg = [(s, min(512, ntok - s)) for s in range(0, ntok, 512)]
    assert len(nseg) <= 3
    xs = [C.sb(f"{name}_x{i}", [128, ntok], F32) for i in range(2)]
    rx = [C.res(f"{name}_x") for _ in range(2)]
    sq = [C.sb(f"{name}_sq{i}", [128, ntok], BF16) for i in range(2)]
    rsq = [C.res(f"{name}_sq") for _ in range(2)]
    rstd = C.sb(f"{name}_rstd", [128, ntok], F32)
    r_rstd = C.res(f"{name}_rstd")
    banks = [5, 6, 7][:len(nseg)]
    for c in range(KC):
        i = c % 2
        P.dma(SP, xs[i][:], xT_ap[:, c, :], w=[rx[i]])
        P.act(sq[i][:], xs[i][:], AF.Square, r=[rx[i]], w=[rsq[i]])
        for b, (s, n) in zip(banks, nseg):
            P.mm(C.ps[b][:, 0:n], C.ones, sq[i][:, s:s + n], c == 0, c == KC - 1, r=[rsq[i], C.r_cbf], w=[C.rps[b]])
    for b, (s, n) in zip(banks, nseg):
        P.act(rstd[:, s:s + n], C.ps[b][:, 0:n], AF.Sqrt, r=[C.rps[b], C.r_eps], w=[r_rstd], bias=C.eps_ap, scale=1.0 / D)
    P.op(DVE, lambda e: e.reciprocal(rstd[:], rstd[:]), r=[r_rstd], w=[r_rstd])
    for c in range(KC):
        i = c % 2
        P.dma(SP, xs[i][:], xT_ap[:, c, :], w=[rx[i]])
        P.stt(DVE, out_bf[:, c, :], xs[i][:], g_ap[:, c:c + 1], rstd[:], ALU.mult, ALU.mult, r=[rx[i], r_rstd, r_g], w=[r_out])


NPA = 32 + 2 + 42 + 5 * 12 + 1


def build_stage_a():
    C = Ctx("A")
    P, nc = C.P, C.nc
    NT = TL + 1
    xT = C.din("xT", [128, KC, NT], F32)
    prm_d = C.din("prm", [128, NPA], F32)
    cst_d = C.din("cst", [128, 384], F32)
    wqk_d = C.din("wqk", [24, 128, KC * 128], F32)
    wv_d = C.din("wv", [3, 128, KC * 512], F32)
    wrw_d = C.din("wrw", [42, 128, KC * 128], F32)
    lora_d = C.din("lora", [128, 6 * 1536], F32)
    qT_o = C.dout("qT", [12, 128, TL], BF16)
    kT_o = C.dout("kT", [12, 128, TL], BF16)
    v_o = C.dout("v", [TL, SBW], BF16)
    rwf_o = C.dout("rwf", [4, RWW, TL], BF16)
    rwv_o = C.dout("rwv", [TL, RWW], BF16)
    ld_o = C.dout("ld", [RWW, TL], F32)
    g_o = C.dout("g", [RWW, TL], F32)
    bon_o = C.dout("bon", [RWW, TL], F32)
    xn_o = C.dout("xn", [128, KC, TL], BF16)
    r_out = C.res("outs")

    load_consts(C, cst_d)
    prm = C.sb("prm", [128, NPA], F32)
    r_prm = C.res("prm")
    P.dma(SP, prm[:], prm_d, w=[r_prm])
    C.eps_ap = prm[:, NPA - 1:NPA]
    C.r_eps = r_prm
    g_attn = prm[:, 0:32]
    o_mix, o_w0, o_a0, o_kk, o_ka, o_rk = 34, 76, 88, 100, 112, 124

    xn = C.sb("xn", [128, KC, NT], BF16)
    r_xn = C.res("xn")
    NW = 3
    wb = [C.sb(f"wb{i}", [128, KC, 128], BF16) for i in range(NW)]
    rwb = [C.res("wb") for _ in range(NW)]
    with C.phase():
        rmsnorm_T(C, xT, NT, g_attn, r_prm, xn, r_xn, "na")
    P.dma(SP, xn_o, xn[:, :, 1:NT], r=[r_xn], w=[r_out], own="r")
    wcount = [0]

    def load_w(src):
        i = wcount[0] % NW
        wcount[0] += 1
        P.dma(POOL, wb[i][:], src.rearrange("p (c m) -> p c m", m=128), w=[rwb[i]])
        return wb[i], rwb[i]

    def gemm_fm(wt, rw, banks, segs):
        for b, (s, n) in zip(banks, segs):
            for c in range(KC):
                P.mm(C.ps[b][:, 0:n], wt[:, c, :], xn[:, c, s:s + n], c == 0, c == KC - 1, r=[rw, r_xn], w=[C.rps[b]])

    seg_main = [(1, 512), (513, 512)]
    seg_halo = [(0, 1)]

    with C.phase():
        qk32 = [C.sb(f"qk32_{i}", [128, TL], F32) for i in range(2)]
        rqk32 = [C.res("qk32") for _ in range(2)]
        qksq = [C.sb(f"qksq_{i}", [128, TL], BF16) for i in range(2)]
        rqksq = [C.res("qksq") for _ in range(2)]
        qkr = [C.sb(f"qkr_{i}", [128, TL], F32) for i in range(2)]
        rqkr = [C.res("qkr") for _ in range(2)]
        qkb = [C.sb(f"qkb_{i}", [128, TL], BF16) for i in range(2)]
        rqkb = [C.res("qkb") for _ in range(2)]
        def qk_gemm(j):
            wt, rw = load_w(wqk_d[j])
            gemm_fm(wt, rw, [0, 1] if j % 2 == 0 else [2, 3], seg_main)

        def qk_post(j):
            i = j % 2
            mb = [0, 1] if i == 0 else [2, 3]
            for h, b in enumerate(mb):
                P.cp(ACT, qk32[i][:, h * 512:(h + 1) * 512], C.ps[b][:, :], r=[C.rps[b]], w=[rqk32[i]])
            P.act(qksq[i][:], qk32[i][:], AF.Square, r=[rqk32[i]], w=[rqksq[i]])
            for h in range(2):
                P.mm(C.ps[4][:, :], C.ones, qksq[i][:, h * 512:(h + 1) * 512], True, True, r=[rqksq[i], C.r_cbf], w=[C.rps[4]])
                P.act(qkr[i][:, h * 512:(h + 1) * 512], C.ps[4][:, :], AF.Sqrt, r=[C.rps[4], r_prm], w=[rqkr[i]], bias=C.eps_ap, scale=1.0 / 128)
            P.op(DVE, lambda e, i=i: e.reciprocal(qkr[i][:], qkr[i][:]), r=[rqkr[i]], w=[rqkr[i]])
            gcol = prm[:, 32:33] if j < 12 else prm[:, 33:34]
            P.stt(DVE, qkb[i][:], qk32[i][:], gcol, qkr[i][:], ALU.mult, ALU.mult, r=[rqk32[i], rqkr[i], r_prm], w=[rqkb[i]])
            dst = qT_o[j] if j < 12 else kT_o[j - 12]
            P.dma(SP, dst, qkb[i][:], r=[rqkb[i]], w=[r_out], own="r")

        qk_gemm(0)
        for j in range(24):
            if j + 1 < 24:
                qk_gemm(j + 1)
            qk_post(j)

    with C.phase():
        wvbs = [C.sb(f"wvb{i}", [128, KC, 512], BF16) for i in range(2)]
        rwvbs = [C.res("wvb") for _ in range(2)]
        vst = [C.sb(f"vst{i}", [128, 512], BF16) for i in range(2)]
        rvst = [C.res("vst") for _ in range(2)]
        cnt = 0
        P.dma(POOL, wvbs[0][:], wv_d[0].rearrange("p (c m) -> p c m", m=512), w=[rwvbs[0]])
        for gcol in range(3):
            wvb, rwvb = wvbs[gcol % 2], rwvbs[gcol % 2]
            if gcol + 1 < 3:
                P.dma(POOL, wvbs[(gcol + 1) % 2][:], wv_d[gcol + 1].rearrange("p (c m) -> p c m", m=512), w=[rwvbs[(gcol + 1) % 2]])
            for tt in range(8):
                b = cnt % 4
                i = cnt % 2
                cnt += 1
                for c in range(KC):
                    P.mm(C.ps[b][:, :], xn[:, c, 1 + tt * 128:1 + (tt + 1) * 128], wvb[:, c, :], c == 0, c == KC - 1, r=[rwvb, r_xn], w=[C.rps[b]])
                P.cp(ACT if cnt % 2 else DVE, vst[i][:], C.ps[b][:, :], r=[C.rps[b]], w=[rvst[i]])
                P.dma(SP, v_o[tt * 128:(tt + 1) * 128, gcol * 512:(gcol + 1) * 512], vst[i][:], r=[rvst[i]], w=[r_out], own="r")

    rwph = C.phase()
    rwph.__enter__()
    lorab = C.sb("lorab", [128, 6, 1536], BF16)
    r_lorab = C.res("lorab")
    with C.phase():
        lora32 = C.sb("lora32", [128, 1536], F32)
        r_l32 = C.res("l32")
        for q in range(6):
            P.dma(SP, lora32[:], lora_d[:, q * 1536:(q + 1) * 1536], w=[r_l32])
            P.cp(DVE, lorab[:, q, :], lora32[:], r=[r_l32], w=[r_lorab])

    seg32 = [C.sb(f"seg32_{i}", [128, NT], F32) for i in range(2)]
    rseg = [C.res("seg") for _ in range(2)]
    tmp = [C.sb(f"tsd_{i}", [128, TL], F32) for i in range(2)]
    rtmp = [C.res("tsd") for _ in range(2)]
    scount = [0]

    def rw_tile(jt, out_ap, r_o, post=None):
        wt, rw = load_w(wrw_d[jt])
        i = scount[0] % 2
        scount[0] += 1
        mb = [0, 1] if i == 0 else [2, 3]
        gemm_fm(wt, rw, mb, seg_main)
        gemm_fm(wt, rw, [4], seg_halo)
        P.cp(ACT, seg32[i][:, 1:513], C.ps[mb[0]][:, :], r=[C.rps[mb[0]]], w=[rseg[i]])
        P.cp(ACT, seg32[i][:, 513:1025], C.ps[mb[1]][:, :], r=[C.rps[mb[1]]], w=[rseg[i]])
        P.cp(DVE, seg32[i][:, 0:1], C.ps[4][:, 0:1], r=[C.rps[4]], w=[rseg[i]])
        P.tt(DVE, tmp[i][:], seg32[i][:, 0:TL], seg32[i][:, 1:NT], ALU.subtract, r=[rseg[i]], w=[rtmp[i]])
        P.stt(DVE, out_ap, tmp[i][:], prm[:, o_mix + jt:o_mix + jt + 1], seg32[i][:, 1:NT], ALU.mult, ALU.add,
              r=[rtmp[i], rseg[i], r_prm], w=[r_o])

    linb = C.sb("linb", [128, 6, TL], BF16)
    r_linb = C.res("linb")
    with C.phase():
        lin = C.sb("lin", [128, 6, TL], F32)
        r_lin = C.res("lin")
        for q in range(6):
            rw_tile(36 + q, lin[:, q, :], r_lin)
        P.act(linb[:, 0, :], lin[:, 0, :], AF.Tanh, r=[r_lin], w=[r_linb])
        P.cp(DVE, linb[:, 1, :], lin[:, 1, :], r=[r_lin], w=[r_linb])
        for q in range(2, 6):
            P.act(linb[:, q, :], lin[:, q, :], AF.Sigmoid, r=[r_lin], w=[r_linb])

    def f32buf(n):
        return C.sb(n, [128, TL], F32), C.res(n)

    def bfbuf(n):
        return C.sb(n, [128, TL], BF16), C.res(n)

    def dbl(fn, n):
        return [fn(f"{n}{i}") for i in range(2)]
    rr2, kk2, vv2, aa2 = dbl(f32buf, "rr"), dbl(f32buf, "kk"), dbl(f32buf, "vv"), dbl(f32buf, "aa")
    ld1, gt1 = f32buf("ldt"), f32buf("gt")
    ld2, gt2 = [ld1, ld1], [gt1, gt1]
    t1, r_t1 = f32buf("t1")
    t2, r_t2 = f32buf("t2")
    bon, r_bon = f32buf("bon")
    ob = [bfbuf(f"ob{i}") for i in range(5)]
    sqb, r_sqb = bfbuf("sqb")
    vT = [C.sb(f"vT{i}", [128, 128], BF16) for i in range(2)]
    rvT = [C.res("vT") for _ in range(2)]

    def bufs_for(ct):
        i = ct % 2
        return rr2[i], kk2[i], vv2[i], aa2[i], ld2[i], gt2[i]

    def rw_part1(ct):
        (rr, r_rr), (kk_, r_k), (vv, r_v), (aa, r_a), (ldt, r_ld), (gt, r_g) = bufs_for(ct)
        cs = slice(ct * 128, (ct + 1) * 128)
        cs = slice(ct * 128, (ct + 1) * 128)
        for h in range(2):
            hs = slice(h * 512, (h + 1) * 512)
            P.mm(C.ps[5][:, :], lorab[:, 0, cs], linb[:, 0, hs], True, True, r=[r_lorab, r_linb], w=[C.rps[5]])
            P.act(ldt[:, hs], C.ps[5][:, :], AF.Sigmoid, r=[C.rps[5], r_prm], w=[r_ld], bias=prm[:, o_w0 + ct:o_w0 + ct + 1], scale=1.0)
            P.mm(C.ps[6][:, :], lorab[:, 1, cs], linb[:, 1, hs], True, True, r=[r_lorab, r_linb], w=[C.rps[6]])
            P.act(aa[:, hs], C.ps[6][:, :], AF.Sigmoid, r=[C.rps[6], r_prm], w=[r_a], bias=prm[:, o_a0 + ct:o_a0 + ct + 1], scale=1.0)
            for q in range(4):
                P.mm(C.ps[7][:, :], lorab[:, 2 + q, cs], linb[:, 2 + q, hs], q == 0, q == 3, r=[r_lorab, r_linb], w=[C.rps[7]])
            P.cp(DVE, gt[:, hs], C.ps[7][:, :], r=[C.rps[7]], w=[r_g])
        P.ts(DVE, ldt[:], ldt[:], -float(np.exp(-0.5)), None, ALU.mult, None, r=[r_ld], w=[r_ld])
        P.dma(SP, ld_o[cs, :], ldt[:], r=[r_ld], w=[r_out], own="r")
        P.dma(SP, g_o[cs, :], gt[:], r=[r_g], w=[r_out], own="r")
        rw_tile(ct, rr[:], r_rr)
        rw_tile(12 + ct, kk_[:], r_k)
        rw_tile(24 + ct, vv[:], r_v)

    def rw_part2(ct):
        (rr, r_rr), (kk_, r_k), (vv, r_v), (aa, r_a), (ldt, r_ld), (gt, r_g) = bufs_for(ct)
        cs = slice(ct * 128, (ct + 1) * 128)
        P.cp(ACT, ob[0][0][:], rr[:], r=[r_rr], w=[ob[0][1]])
        P.dma(SP, rwf_o[0, cs, :], ob[0][0][:], r=[ob[0][1]], w=[r_out], own="r")
        P.ts(DVE, t1[:], kk_[:], prm[:, o_kk + ct:o_kk + ct + 1], None, ALU.mult, None, r=[r_k, r_prm], w=[r_t1])
        P.act(sqb[:], t1[:], AF.Square, r=[r_t1], w=[r_sqb])
        for h in range(2):
            hs = slice(h * 512, (h + 1) * 512)
            P.mm(C.ps[5][:, :], C.bones, sqb[:, hs], True, True, r=[r_sqb, C.r_cbf], w=[C.rps[5]])
            P.act(t2[:, hs], C.ps[5][:, :], AF.Sqrt, r=[C.rps[5]], w=[r_t2])
        P.ts(DVE, t2[:], t2[:], 1e-12, None, ALU.max, None, r=[r_t2], w=[r_t2])
        P.op(DVE, lambda e: e.reciprocal(t2[:], t2[:]), r=[r_t2], w=[r_t2])
        P.tt(DVE, t1[:], t1[:], t2[:], ALU.mult, r=[r_t1, r_t2], w=[r_t1])
        P.ts(DVE, ob[2][0][:], t1[:], -1.0, None, ALU.mult, None, r=[r_t1], w=[ob[2][1]])
        P.tt(DVE, ob[3][0][:], t1[:], aa[:], ALU.mult, r=[r_t1, r_a], w=[ob[3][1]])
        P.dma(SP, rwf_o[2, cs, :], ob[2][0][:], r=[ob[2][1]], w=[r_out], own="r")
        P.dma(SP, rwf_o[3, cs, :], ob[3][0][:], r=[ob[3][1]], w=[r_out], own="r")
        P.ts(DVE, t2[:], aa[:], -1.0, prm[:, o_ka + ct:o_ka + ct + 1], ALU.add, ALU.mult, r=[r_a, r_prm, r_t2], w=[r_t2])
        P.stt(DVE, kk_[:], t2[:], 1.0, kk_[:], ALU.add, ALU.mult, r=[r_t2, r_k], w=[r_k])
        P.cp(ACT, ob[1][0][:], kk_[:], r=[r_k], w=[ob[1][1]])
        P.dma(SP, rwf_o[1, cs, :], ob[1][0][:], r=[ob[1][1]], w=[r_out], own="r")
        P.stt(DVE, t1[:], rr[:], prm[:, o_rk + ct:o_rk + ct + 1], kk_[:], ALU.mult, ALU.mult, r=[r_rr, r_k, r_prm, r_t1], w=[r_t1])
        P.cp(ACT, sqb[:], t1[:], r=[r_t1], w=[r_sqb])
        for h in range(2):
            hs = slice(h * 512, (h + 1) * 512)
            P.mm(C.ps[6][:, :], C.bones, sqb[:, hs], True, True, r=[r_sqb, C.r_cbf], w=[C.rps[6]])
            P.tt(DVE, bon[:, hs], C.ps[6][:, :], vv[:, hs], ALU.mult, r=[C.rps[6], r_v], w=[r_bon])
        P.dma(SP, bon_o[cs, :], bon[:], r=[r_bon], w=[r_out], own="r")
        P.cp(ACT, ob[4][0][:], vv[:], r=[r_v], w=[ob[4][1]])
        for tt in range(8):
            i = tt % 2
            P.op(PE, lambda e, tt=tt: e.transpose(C.ps[7][:, 0:64].bitcast(BF16), ob[4][0][:, tt * 128:(tt + 1) * 128], C.ident),
                 r=[ob[4][1], C.r_cbf], w=[C.rps[7]])
            P.cp(DVE, vT[i][:], C.ps[7][:, 0:64].bitcast(BF16), r=[C.rps[7]], w=[rvT[i]])
            P.dma(SP, rwv_o[tt * 128:(tt + 1) * 128, cs], vT[i][:], r=[rvT[i]], w=[r_out], own="r")


    rw_part1(0)
    for ct in range(12):
        if ct + 1 < 12:
            rw_part1(ct + 1)
        rw_part2(ct)
    rwph.__exit__(None, None, None)
    return C.finish([r_out])


def _tile_lhsT(w, ncols=128):
    K, M = w.shape
    assert K % 128 == 0 and M % ncols == 0
    kc = K // 128
    a = w.reshape(kc, 128, M // ncols, ncols).transpose(2, 1, 0, 3)
    return np.ascontiguousarray(a).reshape(M // ncols, 128, kc * ncols)


def _pad_cols(w, m):
    if w.shape[1] == m:
        return w
    out = np.zeros((w.shape[0], m), w.dtype)
    out[:, :w.shape[1]] = w
    return out


def _cols(v, n):
    return np.ascontiguousarray(v.reshape(n, 128).T)


def _consts():
    c = np.zeros((128, 384), np.float32)
    c[:, 0:128] = 1.0
    c[0:64, 128:192] = 1.0
    c[64:128, 192:256] = 1.0
    c[:, 256:384] = np.eye(128, dtype=np.float32)
    return c


def prep_stage_a(inp):
    x = inp["x"][0]
    w_in = inp["w_in"][0]
    shared = {}
    shared["cst"] = _consts()
    shared["wqk"] = _tile_lhsT(w_in[:, 0:3072])
    shared["wv"] = _tile_lhsT(w_in[:, 3072:4608], 512)
    shared["wrw"] = _tile_lhsT(_pad_cols(w_in[:, 4608:4608 + RW_SEG], 42 * 128))
    lora = np.zeros((128, 6 * 1536), np.float32)
    lora[:, 0:1536] = inp["rw_w_up"][0]
    lora[:, 1536:3072] = inp["rw_a_up"][0]
    gup = np.zeros((512, 1536), np.float32)
    gup[:480] = inp["rw_g_up"][0]
    for q in range(4):
        lora[:, (2 + q) * 1536:(3 + q) * 1536] = gup[q * 128:(q + 1) * 128]
    shared["lora"] = lora
    prm = np.zeros((128, NPA), np.float32)
    prm[:, 0:32] = _cols(inp["attn_norm_g"][0], 32)
    prm[:, 32] = inp["sb_q_norm_g"][0]
    prm[:, 33] = inp["sb_k_norm_g"][0]
    mix = np.zeros(42 * 128, np.float32)
    mix[:RW_SEG] = inp["rw_mix"][0]
    prm[:, 34:76] = _cols(mix, 42)
    prm[:, 76:88] = _cols(inp["rw_w0"][0], 12)
    prm[:, 88:100] = _cols(inp["rw_a0"][0], 12)
    prm[:, 100:112] = _cols(inp["rw_k_k"][0], 12)
    prm[:, 112:124] = _cols(inp["rw_k_a"][0], 12)
    prm[:, 124:136] = _cols(inp["rw_r_k"][0].reshape(-1), 12)
    prm[:, NPA - 1] = EPS
    shared["prm"] = prm
    maps = []
    for c in range(NCORES):
        xc = np.zeros((TL + 1, D), np.float32)
        lo = c * TL - 1
        if c == 0:
            xc[1:] = x[0:TL]
        else:
            xc[:] = x[lo:lo + TL + 1]
        xT = np.ascontiguousarray(xc.T.reshape(KC, 128, TL + 1).transpose(1, 0, 2))
        m = dict(shared)
        m["xT"] = xT
        maps.append(m)
    return maps


CH = 64
SCL = 512
NCH = SCL // CH
NHB = 3


def build_stage_b(do_sb=True, do_rw=True, nsc=T // SCL, ngl=None, do_gates=True):
    C = Ctx("B")
    P, nc = C.P, C.nc
    cst_d = C.din("cst", [128, 384], F32)
    cst2_d = C.din("cst2", [128, 512], F32)
    qF_d = C.din("qF", [128, T], BF16)
    kF_d = C.din("kF", [128, T], BF16)
    vF_d = C.din("vF", [T, 128], BF16)
    qH_d = C.din("qH", [128, T // 2], BF16)
    kH_d = C.din("kH", [128, T], BF16)
    vH_d = C.din("vH", [T, 128], BF16)
    mF_d = C.din("mF", [128, 4, 512], BF16)
    mH_d = C.din("mH", [128, 8, 512], BF16)
    rwf_d = C.din("rwf", [NHB, 64, 4, T], BF16)
    ld_d = C.din("ld", [NHB, 64, T], F32)
    rwv_d = C.din("rwv", [NHB, T, 64], BF16)
    xn_d = C.din("xn", [128, KC, TL], BF16)
    wg_d = C.din("wg", [96, 128, KC * 128], F32)
    sg_o = C.dout("sg", [96, 128, TL], BF16)
    oF_o = C.dout("oF", [128, T], BF16)
    oH_o = C.dout("oH", [128, T // 2], BF16)
    y_o = C.dout("y", [NHB, 64, T], F32)
    r_out = C.res("outs")

    load_consts(C, cst_d)
    c2 = C.sb("c2", [128, 512], F32)
    c2b = C.sb("c2b", [128, 512], BF16)
    r_c2, r_c2b = C.res("c2"), C.res("c2b")
    P.dma(SP, c2[:], cst2_d, w=[r_c2])
    P.cp(DVE, c2b[:], c2[:], r=[r_c2], w=[r_c2b])
    tri, trip = c2b[:, 0:128], c2b[:, 128:256]
    maskM = c2[:, 256:384]
    maskN = c2[0:64, 384:448]
    one_col = C.c32[:, 0:1]

    slots = [(C.ps[b][:, 0:128], C.rps[b]) for b in (5, 6, 7)]
    sl_i = [0]

    def slot():
        s = slots[sl_i[0] % len(slots)]
        sl_i[0] += 1
        return s

    sbshare = {}

    def gates_chain(per_yield=14):
        qk, v = sbshare["qk"], sbshare["v"]
        r_q, r_k, r_v = sbshare["r"]
        ost, r_ost = sbshare["ost"], sbshare["r_ost"]
        xnh = qk[:, :].rearrange("p (c t) -> p c t", c=KC)
        r_xnh = C.res("xnh")
        wbs = [v[:, 0:KC, :], v[:, KC:2 * KC, :]]
        rwbs = [C.res("gwb") for _ in range(2)]
        cnt = 0
        wi = 0
        for hh in range(2):
            P.dma(SP, xnh, xn_d[:, :, hh * 512:(hh + 1) * 512], w=[r_xnh, r_q, r_k])
            for j in range(96):
                i = wi % 2
                P.dma(POOL, wbs[i], wg_d[j].rearrange("p (c m) -> p c m", m=128), w=[rwbs[i], r_v] if wi < 2 else [rwbs[i]])
                wi += 1
                b = j % 4
                for c in range(KC):
                    P.mm(C.ps[b][:, :], wbs[i][:, c, :], xnh[:, c, :], c == 0, c == KC - 1, r=[rwbs[i], r_xnh], w=[C.rps[b]])
                    cnt += 1
                    if cnt % per_yield == 0:
                        yield
                io = j % 2
                P.act(ost[io][:], C.ps[b][:, :], AF.Sigmoid, r=[C.rps[b]], w=[r_ost[io]])
                P.dma(SP, sg_o[j][:, hh * 512:(hh + 1) * 512], ost[io][:], r=[r_ost[io]], w=[r_out], own="r")

    def sb_pipeline(units):
        ZB, RB, OB = [0, 1], [2, 3], 4
        qk = C.sb("qk", [128, 2 * T], BF16)
        q, k = qk[:, 0:T], qk[:, T:2 * T]
        v = C.sb("v", [128, T // 128, 128], BF16)
        r_q, r_k, r_v = C.res("q"), C.res("k"), C.res("v")
        sbshare.update(qk=qk, v=v, r=(r_q, r_k, r_v))
        NE, NS, NX, NWT = 6, 4, 3, 3
        e32 = [C.sb(f"e32_{i}", [128, 512], F32) for i in range(NE)]
        re32 = [C.res("e32") for _ in range(NE)]
        sp = [C.sb(f"sp_{i}", [128, 512], BF16) for i in range(NS)]
        rsp = [C.res("sp") for _ in range(NS)]
        ex = [C.sb(f"ex_{i}", [128, 512], F32) for i in range(NX)]
        rex = [C.res("ex") for _ in range(NX)]
        wt = [C.sb(f"w_{i}", [128, 512], BF16) for i in range(NWT)]
        rwt = [C.res("w") for _ in range(NWT)]
        ost = [C.sb(f"ost{i}", [128, 512], BF16) for i in range(2)]
        r_ost = [C.res("ost") for _ in range(2)]
        sbshare.update(ost=ost, r_ost=r_ost)
        sacc = C.sb("sacc", [128, 512], F32)
        r_sacc = C.res("sacc")
        steps = []
        for ui, (q_d, k_d, v_d, m_sb, nq_blocks, kpg, o_d) in enumerate(units):
            ngroups = nq_blocks // 4 if ngl is None else ngl
            for g in range(ngroups):
                nkb = kpg * (g + 1)
                for idx, kb in enumerate(reversed(range(nkb))):
                    steps.append(dict(u=ui, g=g, kb=kb, first=idx == 0, last=idx == nkb - 1, mj=kb - (nkb - kpg)))
        ns = len(steps)
        gcount = [0]

        def load_unit(ui):
            q_d, k_d, v_d, m_sb, nq_blocks, kpg, o_d = units[ui]
            P.dma(SP, q[:, 0:nq_blocks * 128], q_d, w=[r_q])
            P.dma(SP, k[:], k_d, w=[r_k])
            P.dma(SP, v[:], v_d.rearrange("(b p) d -> p b d", p=128), w=[r_v])

        def st1(i):
            st = steps[i]
            if i == 0 or steps[i - 1]["u"] != st["u"]:
                load_unit(st["u"])
            zb = ZB[i % 2]
            P.mm(C.ps[zb][:, :], k[:, st["kb"] * 128:(st["kb"] + 1) * 128], q[:, st["g"] * 512:(st["g"] + 1) * 512], True, True,
                 r=[r_k, r_q], w=[C.rps[zb]])

        def st2(i):
            st = steps[i]
            zb = ZB[i % 2]
            ie, is_ = i % NE, i % NS
            P.act(e32[ie][:], C.ps[zb][:, :], AF.Exp, r=[C.rps[zb]], w=[re32[ie]], scale=float(128 ** -0.5))
            P.act(sp[is_][:], e32[ie][:], AF.Ln, r=[re32[ie], C.r_c32], w=[rsp[is_]], bias=one_col, scale=1.0)
            if st["mj"] >= 0:
                m_sb = units[st["u"]][3]
                P.tt(POOL, sp[is_][:], sp[is_][:], m_sb[:, st["mj"], :], ALU.mult, r=[rsp[is_], r_m], w=[rsp[is_]])

        def st3(i):
            st = steps[i]
            rb = RB[i % 2]
            is_ = i % NS
            P.mm(C.ps[rb][:, :], tri, sp[is_][:], True, st["first"], r=[rsp[is_], r_c2b], w=[C.rps[rb]])
            if not st["first"]:
                P.mm(C.ps[rb][:, :], C.c32[:, 0:128], sacc[:], False, True, r=[r_sacc, C.r_c32], w=[C.rps[rb]])

        def st4(i):
            st = steps[i]
            rb = RB[i % 2]
            is_, ix = i % NS, i % NX
            P.act(ex[ix][:], C.ps[rb][:, :], AF.Exp, r=[C.rps[rb]], w=[rex[ix]], scale=-1.0)
            if not st["last"]:
                if st["first"]:
                    P.cp(POOL, sacc[:], sp[is_][:], r=[rsp[is_]], w=[r_sacc])
                else:
                    P.tt(POOL, sacc[:], sacc[:], sp[is_][:], ALU.add, r=[rsp[is_], r_sacc], w=[r_sacc])

        def st5(i):
            st = steps[i]
            ie, ix, iw = i % NE, i % NX, i % NWT
            P.tt(DVE, wt[iw][:], e32[ie][:], ex[ix][:], ALU.mult, r=[re32[ie], rex[ix]], w=[rwt[iw]])
            if st["mj"] >= 0:
                m_sb = units[st["u"]][3]
                P.tt(POOL, wt[iw][:], wt[iw][:], m_sb[:, st["mj"], :], ALU.mult, r=[rwt[iw], r_m], w=[rwt[iw]])

        def st6(i):
            st = steps[i]
            iw = i % NWT
            P.mm(C.ps[OB][:, :], v[:, st["kb"], :], wt[iw][:], st["first"], st["last"], r=[r_v, rwt[iw]], w=[C.rps[OB]])
            if st["last"]:
                o_d = units[st["u"]][6]
                io = gcount[0] % 2
                gcount[0] += 1
                P.cp(DVE, ost[io][:], C.ps[OB][:, :], r=[C.rps[OB]], w=[r_ost[io]])
                P.dma(SP, o_d[:, st["g"] * 512:(st["g"] + 1) * 512], ost[io][:], r=[r_ost[io]], w=[r_out], own="r")

        stages = [st1, st2, st3, st4, st5, st6]
        bounds = [0] + [i for i in range(1, ns) if steps[i]["u"] != steps[i - 1]["u"]] + [ns]
        for lo, hi in zip(bounds[:-1], bounds[1:]):
            for t in range(lo, hi + len(stages) - 1):
                for kk in reversed(range(len(stages))):
                    i = t - kk
                    if lo <= i < hi:
                        stages[kk](i)
                yield

    def rw_chain():
        H3 = NHB
        BA, BB, BC = 5, 6, 7
        pA, rA = C.ps[BA], C.rps[BA]
        pB, rB = C.ps[BB], C.rps[BB]
        pC, rC = C.ps[BC], C.rps[BC]
        S32 = C.sb("S32", [64, H3, 64], F32)
        Sbf = C.sb("Sbf", [64, H3, 64], BF16)
        rS = C.res("S")
        P.op(DVE, lambda e: e.memset(S32[:], 0.0), w=[rS])
        P.op(DVE, lambda e: e.memset(Sbf[:], 0.0), w=[rS])
        msk = C.sb("rmsk", [64, H3 * SCL], F32)
        r_msk = C.res("rmsk")
        P.op(DVE, lambda e: e.memset(msk[:], 1.0), w=[r_msk])
        P.op(DVE, lambda e: e.memset(msk[:].rearrange("p (n c) -> p n c", c=CH)[:, :, 0:1], 0.0), w=[r_msk])
        mM3 = C.sb("mM3", [128, H3, 128], F32)
        mN3 = C.sb("mN3", [64, H3, 64], F32)
        id3 = C.sb("id3", [64, H3, 64], F32)
        r_k3 = C.res("k3")
        for h in range(H3):
            P.cp(DVE, mM3[:, h, :], maskM, r=[r_c2], w=[r_k3])
            P.cp(DVE, mN3[:, h, :], maskN, r=[r_c2], w=[r_k3])
            P.cp(DVE, id3[:, h, :], C.ident32[0:64, 0:64], r=[C.r_c32], w=[r_k3])
        fm = C.sb("fm", [64, H3, 4, SCL], BF16)
        ldb = C.sb("ldb", [64, H3 * SCL], F32)
        UVs = [C.sb(f"UV{i}", [128, NCH, H3, 64], BF16) for i in range(2)]
        r_fm, r_ld = C.res("fm"), C.res("ld")
        r_Vs, r_Us = [C.res("V") for _ in range(2)], [C.res("U") for _ in range(2)]
        cur_sc = [0]
        cum = C.sb("cum", [64, H3 * SCL], F32)
        tA = C.sb("tA", [64, H3 * SCL], F32)
        tB = C.sb("tB", [64, H3 * SCL], F32)
        r_cum, r_tA, r_tB = C.res("cum"), C.res("tA"), C.res("tB")
        PC = C.sb("PC", [64, H3, NCH], F32)
        r_PC = C.res("PC")
        Q2 = C.sb("Q2", [64, H3, NCH, 2, CH], BF16)
        KB = C.sb("KB", [64, H3, NCH, 2, CH], BF16)
        KBb = C.sb("KBb", [64, H3, NCH, 2, CH], BF16)
        r_Q2, r_KB, r_KBb = C.res("Q2"), C.res("KB"), C.res("KBb")
        Y = C.sb("Y", [64, H3, SCL], F32)
        r_Y = C.res("Y")

        def wb2(name, shape, dt):
            return [C.sb(f"{name}_{i}", shape, dt) for i in range(2)], [C.res(name) for i in range(2)]
        Mm, r_Mm = wb2("Mm", [128, H3, 128], BF16)
        LTa, r_LTa = wb2("LTa", [64, H3, 128], F32)
        LTb, r_LTb = wb2("LTb", [64, H3, 128], F32)
        Na, r_Na = wb2("Na", [64, H3, 64], F32)
        Nb, r_Nb = wb2("Nb", [64, H3, 64], F32)
        Tt, r_Tt = wb2("Tt", [64, H3, 64], BF16)
        W2, r_W2 = wb2("W2", [64, H3, 64], F32)
        Xs, r_Xs = wb2("Xs", [64, H3, 64], BF16)
        KT, r_KT = wb2("KT", [128, H3, 64], BF16)

        def v4(ap):
            return ap.rearrange("p (h n c) -> p h n c", h=H3, c=CH)

        def f4(j):
            return fm[:, :, j, :].rearrange("p h (n c) -> p h n c", c=CH)

        def load_dma(sc):
            ts_ = slice(sc * SCL, (sc + 1) * SCL)
            for h in range(H3):
                P.dma(SP, fm[:, h, :, :], rwf_d[h, :, :, ts_], w=[r_fm])
                P.dma(SP, ldb[:, h * SCL:(h + 1) * SCL], ld_d[h, :, ts_], w=[r_ld])
                P.dma(SP, UVs[sc % 2][64:128, :, h, :], rwv_d[h, ts_, :].rearrange("(n c) v -> c n v", c=CH), w=[r_Vs[sc % 2]])

        def load(sc):
            P.op(DVE, lambda e: e.tensor_tensor_scan(cum[:], msk[:], ldb[:], 0.0, ALU.mult, ALU.add), r=[r_msk, r_ld], w=[r_cum])
            P.act(tA[:], cum[:], AF.Exp, r=[r_cum], w=[r_tA])
            P.tt(DVE, Q2[:, :, :, 1, :], f4(0), v4(tA[:]), ALU.mult, r=[r_fm, r_tA], w=[r_Q2])
            P.tt(DVE, tB[:], cum[:], ldb[:], ALU.subtract, r=[r_cum, r_ld], w=[r_tB])
            P.act(tA[:], tB[:], AF.Exp, r=[r_tB], w=[r_tA])
            P.tt(DVE, Q2[:, :, :, 0, :], f4(2), v4(tA[:]), ALU.mult, r=[r_fm, r_tA], w=[r_Q2])
            P.act(tA[:], cum[:], AF.Exp, r=[r_cum], w=[r_tA], scale=-1.0)
            P.tt(DVE, KB[:, :, :, 0, :], f4(3), v4(tA[:]), ALU.mult, r=[r_fm, r_tA], w=[r_KB])
            P.tt(DVE, KB[:, :, :, 1, :], f4(1), v4(tA[:]), ALU.mult, r=[r_fm, r_tA], w=[r_KB])
            cumC = v4(cum[:])[:, :, :, CH - 1:CH]
            P.tt(DVE, v4(tB[:]), cumC.to_broadcast([64, H3, NCH, CH]), v4(cum[:]), ALU.subtract, r=[r_cum], w=[r_tB])
            P.act(tA[:], tB[:], AF.Exp, r=[r_tB], w=[r_tA])
            P.tt(DVE, KBb[:, :, :, 0, :], f4(3), v4(tA[:]), ALU.mult, r=[r_fm, r_tA], w=[r_KBb])
            P.tt(DVE, KBb[:, :, :, 1, :], f4(1), v4(tA[:]), ALU.mult, r=[r_fm, r_tA], w=[r_KBb])
            P.act(PC[:].rearrange("p h (n o) -> p h n o", o=1), cumC, AF.Exp, r=[r_cum], w=[r_PC])

        def flat(t4, h, n):
            return t4[:, h, n].rearrange("p a c -> p (a c)")

        def prep(n, i):
            UV, r_V = UVs[cur_sc[0] % 2], r_Vs[cur_sc[0] % 2]
            for h in range(H3):
                P.mm(pA[:, h * 128:(h + 1) * 128], flat(KB, h, n), flat(Q2, h, n), True, True, r=[r_KB, r_Q2], w=[rA])
                P.mm(pB[0:64, h * 64:(h + 1) * 64], Q2[:, h, n, 0, :], KB[:, h, n, 0, :], True, True, r=[r_KB, r_Q2], w=[rB])
            pA3 = pA[:, 0:H3 * 128].rearrange("p (h c) -> p h c", h=H3)
            pB3 = pB[0:64, 0:H3 * 64].rearrange("p (h c) -> p h c", h=H3)
            P.tt(DVE, Mm[i][:], pA3, mM3[:], ALU.mult, r=[rA, r_k3], w=[r_Mm[i]])
            P.tt(DVE, LTa[i][:, :, 0:64], pA3[0:64, :, 0:64], mM3[0:64, :, 0:64], ALU.mult, r=[rA, r_k3], w=[r_LTa[i]])
            P.cp(DVE, LTa[i][:, :, 64:128], id3[:], r=[r_k3], w=[r_LTa[i]])
            P.tt(DVE, Na[i][:], pB3, mN3[:], ALU.mult, r=[rB, r_k3], w=[r_Na[i]])
            yield
            bufs = [(LTa[i], r_LTa[i], Na[i], r_Na[i]), (LTb[i], r_LTb[i], Nb[i], r_Nb[i])]
            for j in range(6):
                (lt, rlt, nn_, rnn) = bufs[j % 2]
                (lt2, rlt2, nn2, rnn2) = bufs[(j + 1) % 2]
                for h in range(H3):
                    P.mm(pA[0:64, h * 128:(h + 1) * 128], nn_[:, h, :], lt[:, h, :], True, True, r=[rnn, rlt], w=[rA])
                    if j < 5:
                        P.mm(pB[0:64, h * 64:(h + 1) * 64], lt[:, h, 0:64], nn_[:, h, :], True, True, r=[rnn, rlt], w=[rB])
                pA3h = pA[0:64, 0:H3 * 128].rearrange("p (h c) -> p h c", h=H3)
                if j < 5:
                    P.cp(DVE, lt2[:, :, 0:64], pA3h[:, :, 0:64], r=[rA], w=[rlt2])
                    P.cp(DVE, nn2[:], pB3, r=[rB], w=[rnn2])
                P.tt(DVE, lt2[:, :, 64:128], pA3h[:, :, 64:128], lt[:, :, 64:128], ALU.add, r=[rA, rlt], w=[rlt2])
                yield
            ltf, rltf = bufs[0][0], bufs[0][1]
            P.cp(DVE, Tt[i][:], ltf[:, :, 64:128], r=[rltf], w=[r_Tt[i]])
            for h in range(H3):
                P.mm(pB[0:64, h * 64:(h + 1) * 64], Mm[i][64:128, h, 0:64], UV[64:128, n, h, :], True, True, r=[r_Mm[i], r_V], w=[rB])
            P.cp(DVE, W2[i][:], pB3, r=[rB], w=[r_W2[i]])
            ptb = pA[:, 0:H3 * 32].bitcast(BF16)
            for h in range(H3):
                P.op(PE, lambda e, h=h: e.transpose(ptb[:, h * 64:(h + 1) * 64], flat(KBb, h, n), C.ident[0:64, 0:64]),
                     r=[r_KBb, C.r_cbf], w=[rA])
            P.cp(DVE, KT[i][:], ptb.rearrange("p (h c) -> p h c", h=H3), r=[rA], w=[r_KT[i]])
            yield

        def crit(n, i):
            UV, r_V, r_U = UVs[cur_sc[0] % 2], r_Vs[cur_sc[0] % 2], r_Us[cur_sc[0] % 2]
            cX = pC[0:64, 0:H3 * 64]
            cU = pC[0:64, H3 * 64:2 * H3 * 64]
            cX3 = cX.rearrange("p (h c) -> p h c", h=H3)
            cU3 = cU.rearrange("p (h c) -> p h c", h=H3)
            for h in range(H3):
                P.mm(cX[:, h * 64:(h + 1) * 64], Q2[:, h, n, 0, :], Sbf[:, h, :], True, True, r=[r_Q2, rS], w=[rC])
            P.tt(DVE, Xs[i][:], cX3, W2[i][:], ALU.add, r=[rC, r_W2[i]], w=[r_Xs[i]])
            for h in range(H3):
                P.mm(cU[:, h * 64:(h + 1) * 64], Tt[i][:, h, :], Xs[i][:, h, :], True, True, r=[r_Tt[i], r_Xs[i]], w=[rC])
            P.cp(DVE, UV[0:64, n, :, :], cU3, r=[rC], w=[r_U])
            yield
            for h in range(H3):
                P.mm(cX[:, h * 64:(h + 1) * 64], Sbf[:, h, :], Q2[:, h, n, 1, :], True, False, r=[rS, r_Q2], w=[rC])
                P.mm(cX[:, h * 64:(h + 1) * 64], UV[:, n, h, :], Mm[i][:, h, 64:128], False, True, r=[r_U, r_V, r_Mm[i]], w=[rC])
                P.mm(cU[:, h * 64:(h + 1) * 64], KT[i][:, h, :], UV[:, n, h, :], True, True, r=[r_KT[i], r_U, r_V], w=[rC])
            P.cp(DVE, Y[:, :, n * CH:(n + 1) * CH], cX3, r=[rC], w=[r_Y])
            P.tt(DVE, S32[:], S32[:], PC[:, :, n:n + 1].to_broadcast([64, H3, 64]), ALU.mult, r=[r_PC, rS], w=[rS])
            P.tt(DVE, S32[:], S32[:], cU3, ALU.add, r=[rC, rS], w=[rS])
            P.cp(DVE, Sbf[:], S32[:], r=[rS], w=[rS])
            yield

        load_dma(0)
        for sc in range(nsc):
            cur_sc[0] = sc
            load(sc)
            if sc + 1 < nsc:
                load_dma(sc + 1)
            yield
            ovl = False
            if not ovl:
                for n in range(NCH):
                    yield from prep(n, n % 2)
                    yield from crit(n, n % 2)
            else:
                yield from prep(0, 0)
                for n in range(NCH):
                    gc = crit(n, n % 2)
                    gp = prep(n + 1, (n + 1) % 2) if n + 1 < NCH else None
                    while gc is not None or gp is not None:
                        if gp is not None:
                            try:
                                next(gp)
                            except StopIteration:
                                gp = None
                        if gc is not None:
                            try:
                                next(gc)
                            except StopIteration:
                                gc = None
                        yield
            for h in range(H3):
                P.dma(SP, y_o[h, :, sc * SCL:(sc + 1) * SCL], Y[:, h, :], r=[r_Y], w=[r_out], own="r")

    sbq, rwq = None, None
    if do_sb:
        msb = C.sb("msb", [128, 12, 512], BF16)
        r_m = C.res("msb")
        P.dma(SP, msb[:, 0:4, :], mF_d, w=[r_m])
        P.dma(SP, msb[:, 4:12, :], mH_d, w=[r_m])
        sbq = sb_pipeline([(qF_d, kF_d, vF_d, msb[:, 0:4, :], T // 128, 4, oF_o),
                           (qH_d, kH_d, vH_d, msb[:, 4:12, :], T // 256, 8, oH_o)])
    if do_rw:
        rwq = rw_chain()
    gq = None
    gates_started = False
    while sbq is not None or rwq is not None or gq is not None:
        for _ in range(1):
            if sbq is not None:
                try:
                    next(sbq)
                except StopIteration:
                    sbq = None
        if sbq is None and do_sb and do_gates and not gates_started:
            gates_started = True
            gq = gates_chain()
        if rwq is not None:
            try:
                next(rwq)
            except StopIteration:
                rwq = None
        if gq is not None:
            try:
                next(gq)
            except StopIteration:
                gq = None
    return C.finish([r_out])


def _consts2():
    c = np.zeros((128, 512), np.float32)
    j = np.arange(128)[:, None]
    s_ = np.arange(128)[None, :]
    c[:, 0:128] = (j >= s_)
    c[:, 128:256] = (j < s_)
    rs = np.arange(128)[:, None] % 64
    ct = np.arange(128)[None, :]
    m = np.zeros((128, 128), np.float32)
    m[:, 0:64] = (rs < ct[:, 0:64])
    m[:, 64:128] = (rs <= (ct[:, 64:128] - 64))
    c[:, 256:384] = m
    tt_ = np.arange(64)[:, None]
    ss_ = np.arange(64)[None, :]
    c[0:64, 384:448] = (ss_ < tt_)
    return c


def _sb_masks(qblocks, kblocks):
    m = np.zeros((128, len(kblocks), len(qblocks) * 128), np.float32)
    key = np.arange(128)[:, None]
    qp = np.arange(128)[None, :]
    for j, kb in enumerate(kblocks):
        for i, qb in enumerate(qblocks):
            m[:, j, i * 128:(i + 1) * 128] = (kb * 128 + key) < (qb * 128 + qp)
    return m.astype(NPBF)


def sb_units(c):
    if c % 2 == 0:
        return 3 * c // 2, 3 * c // 2 + 1, 0
    return (3 * c + 1) // 2, (3 * c - 1) // 2, 1


def prep_stage_b(resA, inp=None):
    wg = _tile_lhsT(np.asarray(inp["w_in"][0][:, 10976:23264])) if inp is not None else None
    qT = np.concatenate([np.asarray(r["qT"]) for r in resA], axis=2)
    kT = np.concatenate([np.asarray(r["kT"]) for r in resA], axis=2)
    v = np.concatenate([np.asarray(r["v"]) for r in resA], axis=0)
    rwf = np.concatenate([np.asarray(r["rwf"]) for r in resA], axis=2)
    ld = np.concatenate([np.asarray(r["ld"]) for r in resA], axis=1)
    rwv = np.concatenate([np.asarray(r["rwv"]) for r in resA], axis=0)
    cst, cst2 = _consts(), _consts2()
    mF = _sb_masks([0, 1, 2, 3], [0, 1, 2, 3])
    maps = []
    for c in range(NCORES):
        hf, hh, par = sb_units(c)
        m = {"cst": cst, "cst2": cst2, "mF": mF}
        m["mH"] = _sb_masks([par, 2 + par, 4 + par, 6 + par], list(range(8)))
        m["qF"] = np.ascontiguousarray(qT[hf])
        m["kF"] = np.ascontiguousarray(kT[hf])
        m["vF"] = np.ascontiguousarray(v[:, hf * 128:(hf + 1) * 128])
        qh = qT[hh].reshape(128, T // 256, 2, 128)[:, :, par, :].reshape(128, T // 2)
        m["qH"] = np.ascontiguousarray(qh)
        m["kH"] = np.ascontiguousarray(kT[hh])
        m["vH"] = np.ascontiguousarray(v[:, hh * 128:(hh + 1) * 128])
        hs = [3 * c + i for i in range(NHB)]
        m["rwf"] = np.ascontiguousarray(np.stack([rwf[:, h * 64:(h + 1) * 64, :].transpose(1, 0, 2) for h in hs]))
        m["ld"] = np.ascontiguousarray(np.stack([ld[h * 64:(h + 1) * 64] for h in hs]))
        m["rwv"] = np.ascontiguousarray(np.stack([rwv[:, h * 64:(h + 1) * 64] for h in hs]))
        if wg is not None:
            m["wg"] = wg
            m["xn"] = np.asarray(resA[c]["xn"])
        maps.append(m)
    return maps


NPC = 32 + 32 + 32 + 2 + 2 + 12 + 12 + 1 + 1
NFT = DFF // 128
HT = 512


def build_stage_c(parts=("mem", "rw", "merge", "out", "ffn")):
    C = Ctx("C")
    P, nc = C.P, C.nc
    xT = C.din("xT", [128, KC, TL], F32)
    memT = C.din("memT", [128, KC, NMEM], F32)
    prm_d = C.din("prm", [128, NPC], F32)
    cst_d = C.din("cst", [128, 384], F32)
    wkvk_d = C.din("wkvk", [8, 128, KC * 128], F32)
    wkvv_d = C.din("wkvv", [2, 128, KC * 512], F32)
    wmq_d = C.din("wmq", [8, 128, KC * 128], F32)
    wu_d = C.din("wu", [32, 128, KC * 128], F32)
    wo_d = C.din("wo", [32, 128, KC * 128], F32)
    wfg_d = C.din("wfg", [NFT, 128, KC * 128], F32)
    wfu_d = C.din("wfu", [NFT, 128, KC * 128], F32)
    wfd_d = C.din("wfd", [32, 2, 128, 43 * 128], F32)
    osb_d = C.din("osb", [128, 12, TL], BF16)
    y_d = C.din("y", [RWW, TL], F32)
    g_d = C.din("g", [RWW, TL], F32)
    bon_d = C.din("bon", [RWW, TL], F32)
    out_o = C.dout("outT", [D, TL], F32)
    sg_s = C.din("sg", [96, 128, TL], BF16)
    xn_d = C.din("xn", [128, KC, TL], BF16)
    h1_s = C.dint("h1_s", [32, 128, TL], F32)
    r_out, r_sg, r_h1 = C.res("outs"), C.res("sg"), C.res("h1")

    load_consts(C, cst_d)
    prm = C.sb("prm", [128, NPC], F32)
    r_prm = C.res("prm")
    P.dma(SP, prm[:], prm_d, w=[r_prm])
    C.eps_ap = prm[:, NPC - 1:NPC]
    gneps_ap = prm[:, NPC - 2:NPC - 1]
    C.r_eps = r_prm
    g_attn, g_mem, g_ffn = prm[:, 0:32], prm[:, 32:64], prm[:, 64:96]
    o_mq, o_mk, o_lng, o_lnb = 96, 98, 100, 112
    one_col = C.c32[:, 0:1]

    NW = 3
    wb = [C.sb(f"wb{i}", [128, KC, 128], BF16) for i in range(NW)]
    rwb = [C.res("wb") for _ in range(NW)]
    wcount = [0]

    def load_w(src):
        i = wcount[0] % NW
        wcount[0] += 1
        P.dma(POOL, wb[i][:], src.rearrange("p (c m) -> p c m", m=128), w=[rwb[i]])
        return wb[i], rwb[i]

    halves = [(0, HT), (HT, HT)]
    omem = C.sb("omem", [128, 8, TL], BF16)
    r_omem = C.res("omem")

    with C.phase():
        xn = C.sb("xn", [128, KC, TL], BF16)
        r_xn = C.res("xn")
        P.dma(SP, xn[:], xn_d, w=[r_xn])

        def gemm_fm(wt, rw, banks, rhs3, r_rhs, segs):
            for b, (s, n) in zip(banks, segs):
                for c in range(KC):
                    P.mm(C.ps[b][:, 0:n], wt[:, c, :], rhs3[:, c, s:s + n], c == 0, c == KC - 1, r=[rw, r_rhs], w=[C.rps[b]])

        if "mem" in parts:
            with C.phase():
                mn = C.sb("mn", [128, KC, NMEM], BF16)
                r_mn = C.res("mn")
                with C.phase():
                    rmsnorm_T(C, memT, NMEM, g_mem, r_prm, mn, r_mn, "nm")
                mk32 = C.sb("mk32", [128, 8, NMEM], F32)
                r_mk32 = C.res("mk32")
                mksq = C.sb("mksq", [128, 8, NMEM], BF16)
                r_mksq = C.res("mksq")
                mkn = C.sb("mkn", [128, 8, NMEM], BF16)
                r_mkn = C.res("mkn")
                mrs = C.sb("mrs", [128, NMEM], F32)
                r_mrs = C.res("mrs")
                for j in range(8):
                    wt, rw = load_w(wkvk_d[j])
                    b = j % 2
                    gemm_fm(wt, rw, [b], mn, r_mn, [(0, NMEM)])
                    P.cp(ACT, mk32[:, j, :], C.ps[b][:, 0:NMEM], r=[C.rps[b]], w=[r_mk32])
                    P.act(mksq[:, j, :], mk32[:, j, :], AF.Square, r=[r_mk32], w=[r_mksq])
                for h in range(4):
                    for dt in range(2):
                        P.mm(C.ps[2][:, 0:NMEM], C.ones, mksq[:, 2 * h + dt, :], dt == 0, dt == 1, r=[r_mksq, C.r_cbf], w=[C.rps[2]])
                    P.act(mrs[:], C.ps[2][:, 0:NMEM], AF.Sqrt, r=[C.rps[2], r_prm], w=[r_mrs], bias=C.eps_ap, scale=1.0 / 256)
                    P.op(DVE, lambda e: e.reciprocal(mrs[:], mrs[:]), r=[r_mrs], w=[r_mrs])
                    for dt in range(2):
                        P.stt(DVE, mkn[:, 2 * h + dt, :], mk32[:, 2 * h + dt, :], prm[:, o_mk + dt:o_mk + dt + 1], mrs[:], ALU.mult, ALU.mult,
                              r=[r_mk32, r_mrs, r_prm], w=[r_mkn])
                mv = C.sb("mv", [128, 2, MEMW], BF16)
                r_mv = C.res("mv")
                with C.phase():
                    wvb = C.sb("wvb", [128, KC, 512], BF16)
                    rwvb = C.res("wvb")
                    for gc in range(2):
                        P.dma(POOL, wvb[:], wkvv_d[gc].rearrange("p (c m) -> p c m", m=512), w=[rwvb])
                        for mt in range(2):
                            b = 3 + mt
                            for c in range(KC):
                                P.mm(C.ps[b][:, :], mn[:, c, mt * 128:(mt + 1) * 128], wvb[:, c, :], c == 0, c == KC - 1, r=[rwvb, r_mn], w=[C.rps[b]])
                            P.cp(ACT, mv[:, mt, gc * 512:(gc + 1) * 512], C.ps[b][:, :], r=[C.rps[b]], w=[r_mv])
                mq32 = C.sb("mq32", [128, 2, TL], F32)
                r_mq32 = C.res("mq32")
                mqsq = C.sb("mqsq", [128, 2, TL], BF16)
                r_mqsq = C.res("mqsq")
                mqn = C.sb("mqn", [128, 2, TL], BF16)
                r_mqn = C.res("mqn")
                qrs = C.sb("qrs", [128, TL], F32)
                r_qrs = C.res("qrs")
                pT = C.sb("pT", [128, 2, HT], BF16)
                r_pT = C.res("pT")
                rden = C.sb("rden", [128, HT], F32)
                r_rden = C.res("rden")
                for h in range(4):
                    for dt in range(2):
                        wt, rw = load_w(wmq_d[2 * h + dt])
                        gemm_fm(wt, rw, [0, 1], xn, r_xn, halves)
                        for hh, (s, n) in enumerate(halves):
                            P.cp(ACT, mq32[:, dt, s:s + n], C.ps[hh][:, :], r=[C.rps[hh]], w=[r_mq32])
                        P.act(mqsq[:, dt, :], mq32[:, dt, :], AF.Square, r=[r_mq32], w=[r_mqsq])
                    for hh, (s, n) in enumerate(halves):
                        for dt in range(2):
                            P.mm(C.ps[2][:, :], C.ones, mqsq[:, dt, s:s + n], dt == 0, dt == 1, r=[r_mqsq, C.r_cbf], w=[C.rps[2]])
                        P.act(qrs[:, s:s + n], C.ps[2][:, :], AF.Sqrt, r=[C.rps[2], r_prm], w=[r_qrs], bias=C.eps_ap, scale=1.0 / 256)
                    P.op(DVE, lambda e: e.reciprocal(qrs[:], qrs[:]), r=[r_qrs], w=[r_qrs])
                    for dt in range(2):
                        P.stt(DVE, mqn[:, dt, :], mq32[:, dt, :], prm[:, o_mq + dt:o_mq + dt + 1], qrs[:], ALU.mult, ALU.mult,
                              r=[r_mq32, r_qrs, r_prm], w=[r_mqn])
                    for hh, (s, n) in enumerate(halves):
                        for mt in range(2):
                            for dt in range(2):
                                P.mm(C.ps[3][:, :], mkn[:, 2 * h + dt, mt * 128:(mt + 1) * 128], mqn[:, dt, s:s + n], dt == 0, dt == 1,
                                     r=[r_mkn, r_mqn], w=[C.rps[3]])
                            P.act(pT[:, mt, :], C.ps[3][:, :], AF.Exp, r=[C.rps[3]], w=[r_pT], scale=1.0 / 16)
                        for mt in range(2):
                            P.mm(C.ps[4][:, :], C.ones, pT[:, mt, :], mt == 0, mt == 1, r=[r_pT, C.r_cbf], w=[C.rps[4]])
                        P.op(DVE, lambda e: e.reciprocal(rden[:], C.ps[4][:, :]), r=[C.rps[4]], w=[r_rden])
                        for dt in range(2):
                            b = 5 + dt
                            for mt in range(2):
                                P.mm(C.ps[b][:, :], mv[:, mt, (2 * h + dt) * 128:(2 * h + dt + 1) * 128], pT[:, mt, :], mt == 0, mt == 1,
                                     r=[r_mv, r_pT], w=[C.rps[b]])
                            P.tt(DVE, omem[:, 2 * h + dt, s:s + n], C.ps[b][:, :], rden[:], ALU.mult, r=[C.rps[b], r_rden], w=[r_omem])


    with C.phase():
        osb = C.sb("osb", [128, 12, TL], BF16)
        r_osb = C.res("osb")
        orw = C.sb("orw", [128, 12, TL], BF16)
        r_orw = C.res("orw")
        mrg = C.sb("mrg", [128, KC, TL], BF16)
        r_mrg = C.res("mrg")
        P.dma(SP, osb[:], osb_d, w=[r_osb])
        if "rw" in parts:
            with C.phase():
                def f32b(n):
                    return C.sb(n, [128, TL], F32), C.res(n)
                yt, r_yt = f32b("yt")
                gt, r_gt = f32b("gt")
                bt, r_bt = f32b("bt")
                dd, r_dd = f32b("dd")
                rs, r_rs = f32b("rs")
                ybf = C.sb("ybf", [128, TL], BF16)
                r_ybf = C.res("ybf")
                for ct in range(12):
                    cs = slice(ct * 128, (ct + 1) * 128)
                    P.dma(SP, yt[:], y_d[cs, :], w=[r_yt])
                    P.dma(SP, gt[:], g_d[cs, :], w=[r_gt])
                    P.dma(SP, bt[:], bon_d[cs, :], w=[r_bt])
                    P.cp(ACT, ybf[:], yt[:], r=[r_yt], w=[r_ybf])
                    for hh, (s, n) in enumerate(halves):
                        P.mm(C.ps[hh][:, :], C.bones, ybf[:, s:s + n], True, True, r=[r_ybf, C.r_cbf], w=[C.rps[hh]])
                        P.stt(DVE, dd[:, s:s + n], C.ps[hh][:, :], -1.0 / 64, yt[:, s:s + n], ALU.mult, ALU.add, r=[C.rps[hh], r_yt], w=[r_dd])
                    P.act(ybf[:], dd[:], AF.Square, r=[r_dd], w=[r_ybf])
                    for hh, (s, n) in enumerate(halves):
                        P.mm(C.ps[2 + hh][:, :], C.bones, ybf[:, s:s + n], True, True, r=[r_ybf, C.r_cbf], w=[C.rps[2 + hh]])
                        P.act(rs[:, s:s + n], C.ps[2 + hh][:, :], AF.Sqrt, r=[C.rps[2 + hh], r_prm], w=[r_rs], bias=gneps_ap, scale=1.0 / 64)
                    P.op(DVE, lambda e: e.reciprocal(rs[:], rs[:]), r=[r_rs], w=[r_rs])
                    P.tt(DVE, dd[:], dd[:], rs[:], ALU.mult, r=[r_dd, r_rs], w=[r_dd])
                    P.ts(DVE, dd[:], dd[:], prm[:, o_lng + ct:o_lng + ct + 1], prm[:, o_lnb + ct:o_lnb + ct + 1], ALU.mult, ALU.add,
                         r=[r_dd, r_prm], w=[r_dd])
                    P.tt(POOL, dd[:], dd[:], bt[:], ALU.add, r=[r_dd, r_bt], w=[r_dd])
                    P.tt(DVE, orw[:, ct, :], dd[:], gt[:], ALU.mult, r=[r_dd, r_gt], w=[r_orw])

        if "merge" in parts:
            with C.phase():
                sgt = [C.sb(f"sgt{i}", [128, 3, TL], BF16) for i in range(2)]
                rsgt = [C.res("sgt") for _ in range(2)]
                m1 = C.sb("m1", [128, TL], F32)
                r_m1 = C.res("m1")
                m2 = C.sb("m2", [128, TL], F32)
                r_m2 = C.res("m2")
                srcs = [(osb, r_osb, 0, 12), (orw, r_orw, 12, 12), (omem, r_omem, 24, 8)]
                for j in range(32):
                    i = j % 2
                    wt, rw = load_w(wu_d[j])
                    for br in range(3):
                        P.dma(SP, sgt[i][:, br, :], sg_s[br * 32 + j], r=[r_sg], w=[rsgt[i]])
                    for br, (src, r_src, c0, ncnk) in enumerate(srcs):
                        for hh, (s, n) in enumerate(halves):
                            b = 2 * br + hh
                            for c in range(ncnk):
                                P.mm(C.ps[b][:, :], wt[:, c0 + c, :], src[:, c, s:s + n], c == 0, c == ncnk - 1, r=[rw, r_src], w=[C.rps[b]])
                    for hh, (s, n) in enumerate(halves):
                        P.tt(DVE, m1[:, s:s + n], C.ps[0 + hh][:, :], sgt[i][:, 0, s:s + n], ALU.mult, r=[C.rps[hh], rsgt[i]], w=[r_m1])
                        P.tt(DVE, m2[:, s:s + n], C.ps[2 + hh][:, :], sgt[i][:, 1, s:s + n], ALU.mult, r=[C.rps[2 + hh], rsgt[i]], w=[r_m2])
                    P.tt(POOL, m1[:], m1[:], m2[:], ALU.add, r=[r_m1, r_m2], w=[r_m1])
                    for hh, (s, n) in enumerate(halves):
                        P.tt(DVE, m2[:, s:s + n], C.ps[4 + hh][:, :], sgt[i][:, 2, s:s + n], ALU.mult, r=[C.rps[4 + hh], rsgt[i], r_m2], w=[r_m2])
                    P.tt(POOL, mrg[:, j, :], m1[:], m2[:], ALU.add, r=[r_m1, r_m2], w=[r_mrg])

        ssq_banks = [6, 7]
        if "out" in parts:
            with C.phase():
                xs = [C.sb(f"xs{i}", [128, TL], F32) for i in range(2)]
                rxs = [C.res("xs") for _ in range(2)]
                hsq = [C.sb(f"hsq{i}", [128, TL], BF16) for i in range(2)]
                rhsq = [C.res("hsq") for _ in range(2)]
                for j in range(32):
                    i = j % 2
                    wt, rw = load_w(wo_d[j])
                    P.dma(SP, xs[i][:], xT[:, j, :], w=[rxs[i]])
                    mb = [0, 1] if i == 0 else [2, 3]
                    for hh, (s, n) in enumerate(halves):
                        for c in range(KC):
                            P.mm(C.ps[mb[hh]][:, :], wt[:, c, :], mrg[:, c, s:s + n], c == 0, c == KC - 1, r=[rw, r_mrg], w=[C.rps[mb[hh]]])
                        P.tt(DVE, xs[i][:, s:s + n], C.ps[mb[hh]][:, :], xs[i][:, s:s + n], ALU.add, r=[C.rps[mb[hh]], rxs[i]], w=[rxs[i]])
                    P.dma(SP, h1_s[j], xs[i][:], r=[rxs[i]], w=[r_h1], own="r")
                    P.act(hsq[i][:], xs[i][:], AF.Square, r=[rxs[i]], w=[rhsq[i]])
                    for hh, (s, n) in enumerate(halves):
                        P.mm(C.ps[ssq_banks[hh]][:, :], C.ones, hsq[i][:, s:s + n], j == 0, j == 31, r=[rhsq[i], C.r_cbf], w=[C.rps[ssq_banks[hh]]])

    if "ffn" in parts:
        with C.phase():
            rstd = C.sb("frstd", [128, TL], F32)
            r_rstd = C.res("frstd")
            for hh, (s, n) in enumerate(halves):
                P.act(rstd[:, s:s + n], C.ps[ssq_banks[hh]][:, :], AF.Sqrt, r=[C.rps[ssq_banks[hh]], r_prm], w=[r_rstd], bias=C.eps_ap, scale=1.0 / D)
            P.op(DVE, lambda e: e.reciprocal(rstd[:], rstd[:]), r=[r_rstd], w=[r_rstd])
            hn = C.sb("hn", [128, KC, HT], BF16)
            r_hn = C.res("hn")
            actT = C.sb("actT", [128, NFT, HT], BF16)
            r_act = C.res("actT")
            hld = [C.sb(f"hld{i}", [128, HT], F32) for i in range(2)]
            rhld = [C.res("hld") for _ in range(2)]
            sgl = [C.sb(f"sgl{i}", [128, HT], F32) for i in range(2)]
            rsgl = [C.res("sgl") for _ in range(2)]
            wdb = [C.sb(f"wdb{i}", [128, 43, 128], BF16) for i in range(3)]
            rwdb = [C.res("wdb") for _ in range(3)]
            wdc = [0]
            for hh, (s, n) in enumerate(halves):
                for j in range(32):
                    i = j % 2
                    P.dma(SP, hld[i][:], h1_s[j][:, s:s + n], r=[r_h1], w=[rhld[i]])
                    P.stt(DVE, hn[:, j, :], hld[i][:], g_ffn[:, j:j + 1], rstd[:, s:s + n], ALU.mult, ALU.mult, r=[rhld[i], r_rstd, r_prm], w=[r_hn])
                for f in range(NFT):
                    i = f % 2
                    gb, ub = (0, 1) if i == 0 else (2, 3)
                    wt, rw = load_w(wfg_d[f])
                    for c in range(KC):
                        P.mm(C.ps[gb][:, :], wt[:, c, :], hn[:, c, :], c == 0, c == KC - 1, r=[rw, r_hn], w=[C.rps[gb]])
                    wt, rw = load_w(wfu_d[f])
                    for c in range(KC):
                        P.mm(C.ps[ub][:, :], wt[:, c, :], hn[:, c, :], c == 0, c == KC - 1, r=[rw, r_hn], w=[C.rps[ub]])
                    P.act(sgl[i][:], C.ps[gb][:, :], AF.Silu, r=[C.rps[gb]], w=[rsgl[i]])
                    P.tt(DVE, actT[:, f, :], sgl[i][:], C.ps[ub][:, :], ALU.mult, r=[rsgl[i], C.rps[ub]], w=[r_act])
                for j in range(32):
                    i = j % 2
                    b = 4 + i
                    for fg in range(2):
                        k = wdc[0] % 3
                        wdc[0] += 1
                        P.dma(POOL, wdb[k][:], wfd_d[j, fg].rearrange("p (c m) -> p c m", m=128), w=[rwdb[k]])
                        for c in range(43):
                            P.mm(C.ps[b][:, :], wdb[k][:, c, :], actT[:, fg * 43 + c, :], fg == 0 and c == 0, fg == 1 and c == 42,
                                 r=[rwdb[k], r_act], w=[C.rps[b]])
                    P.dma(SP, hld[i][:], h1_s[j][:, s:s + n], r=[r_h1], w=[rhld[i]])
                    P.tt(DVE, hld[i][:], C.ps[b][:, :], hld[i][:], ALU.add, r=[C.rps[b], rhld[i]], w=[rhld[i]])
                    P.dma(SP, out_o[j * 128:(j + 1) * 128, s:s + n], hld[i][:], r=[rhld[i]], w=[r_out], own="r")
    return C.finish([r_out])


def prep_stage_c(inp, resA, resB):
    x = inp["x"][0]
    w_in = inp["w_in"][0]
    shared = {"cst": _consts()}
    kv = inp["mem_w_kv"][0]
    shared["wkvk"] = _tile_lhsT(kv[:, 0:1024])
    shared["wkvv"] = _tile_lhsT(kv[:, 1024:2048], 512)
    shared["wmq"] = _tile_lhsT(w_in[:, 9952:10976])
    wu = np.concatenate([inp["w_sb_o"][0], inp["w_rw_o"][0], inp["w_mem_o"][0]], axis=0)
    shared["wu"] = _tile_lhsT(wu)
    shared["wo"] = _tile_lhsT(inp["w_out"][0])
    shared["wfg"] = _tile_lhsT(inp["w_gate"][0])
    shared["wfu"] = _tile_lhsT(inp["w_up"][0])
    wd = inp["w_down"][0].reshape(2, 43, 128, 32, 128).transpose(3, 0, 2, 1, 4)
    shared["wfd"] = np.ascontiguousarray(wd).reshape(32, 2, 128, 43 * 128)
    shared["memT"] = np.ascontiguousarray(inp["mem"][0].T.reshape(KC, 128, NMEM).transpose(1, 0, 2))
    prm = np.zeros((128, NPC), np.float32)
    prm[:, 0:32] = _cols(inp["attn_norm_g"][0], 32)
    prm[:, 32:64] = _cols(inp["mem_norm_g"][0], 32)
    prm[:, 64:96] = _cols(inp["ffn_norm_g"][0], 32)
    prm[:, 96:98] = _cols(inp["mem_q_norm_g"][0], 2)
    prm[:, 98:100] = _cols(inp["mem_k_norm_g"][0], 2)
    prm[:, 100:112] = _cols(inp["rw_ln_g"][0], 12)
    prm[:, 112:124] = _cols(inp["rw_ln_b"][0], 12)
    prm[:, NPC - 2] = GN_EPS
    prm[:, NPC - 1] = EPS
    shared["prm"] = prm
    osb = np.zeros((12, 128, T), NPBF)
    y = np.zeros((RWW, T), np.float32)
    for c in range(NCORES):
        hf, hh, par = sb_units(c)
        osb[hf] = np.asarray(resB[c]["oF"])
        oh = np.asarray(resB[c]["oH"]).reshape(128, T // 256, 128)
        osb[hh].reshape(128, T // 256, 2, 128)[:, :, par, :] = oh
        for i in range(NHB):
            h = 3 * c + i
            y[h * 64:(h + 1) * 64] = np.asarray(resB[c]["y"][i])
    maps = []
    for c in range(NCORES):
        ts_ = slice(c * TL, (c + 1) * TL)
        m = dict(shared)
        m["xT"] = np.ascontiguousarray(x[ts_].T.reshape(KC, 128, TL).transpose(1, 0, 2))
        m["osb"] = np.ascontiguousarray(osb[:, :, ts_].transpose(1, 0, 2))
        m["y"] = np.ascontiguousarray(y[:, ts_])
        m["g"] = np.asarray(resA[c]["g"])
        m["bon"] = np.asarray(resA[c]["bon"])
        m["xn"] = np.asarray(resA[c]["xn"])
        m["sg"] = np.asarray(resB[c]["sg"])
        maps.append(m)
    return maps


def kernel(**inp):
    inp = {k: np.asarray(v) for k, v in inp.items()}
    ids = list(range(NCORES))
    resA = run_bass_kernel_spmd(build_stage_a(), prep_stage_a(inp), core_ids=ids).results
    resB = run_bass_kernel_spmd(build_stage_b(), prep_stage_b(resA, inp), core_ids=ids).results
    resC = run_bass_kernel_spmd(build_stage_c(), prep_stage_c(inp, resA, resB), core_ids=ids).results
    out = np.concatenate([np.asarray(r["outT"]).T for r in resC], axis=0)
    return np.ascontiguousarray(out.reshape(1, T, D).astype(np.float32))
```
